# Optimizing a Trainium2 kernel written in Bass

```python
import math
import jax, jax.numpy as jnp
from jax import lax
import numpy as np

D_MODEL = 1024
BATCH = 2
SEQ = 8192
DEPTH = 1

MEM_LEN = 256
SSM_WIDTH = D_MODEL // 2
SSM_GROUP = 16
SSM_GROUPS = SSM_WIDTH // SSM_GROUP
SSM_STATE = 64
SSM_LOG_STEP_MIN = math.log(1e-3)
SSM_LOG_STEP_MAX = math.log(1e-1)
GLA_WIDTH = D_MODEL - SSM_WIDTH
GLA_HEADS = 4
GLA_DV = GLA_WIDTH // GLA_HEADS
GLA_DK = GLA_DV // 2
GLA_KEY_WIDTH = GLA_HEADS * GLA_DK
GLA_GATE_RANK = 16
GLA_TAU = 16.0
GLA_CHUNK = 16
MIX_WIDTH = SSM_WIDTH + GLA_WIDTH
IN_COLS = SSM_WIDTH + 2 * GLA_KEY_WIDTH + 2 * GLA_WIDTH + GLA_GATE_RANK
MEM_HEADS = 4
MEM_HEAD_DIM = D_MODEL // MEM_HEADS
D_FF = 4 * D_MODEL
DN_ALPHA = (2.0 * DEPTH) ** 0.25
DN_BETA = (8.0 * DEPTH) ** -0.25
LN_EPS = 1e-5

kernel_name = "hybrid_s5_gla_memxattn_deepnorm"


def layer_norm(x, g, b):
    xf = x.astype(jnp.float32)
    mu = jnp.mean(xf, axis=-1, keepdims=True)
    xc = xf - mu
    var = jnp.mean(xc * xc, axis=-1, keepdims=True)
    return (xc * lax.rsqrt(var + LN_EPS) * g.astype(jnp.float32) + b.astype(jnp.float32)).astype(x.dtype)


def _complex_affine_combine(e1, e2):
    a1r, a1i, b1r, b1i = e1
    a2r, a2i, b2r, b2i = e2
    ar = a1r * a2r - a1i * a2i
    ai = a1r * a2i + a1i * a2r
    br = a2r * b1r - a2i * b1i + b2r
    bi = a2r * b1i + a2i * b1r + b2i
    return (ar, ai, br, bi)


def s5_mixer(u, lam_re, lam_im, log_step, b_re, b_im, c_re, c_im, d_skip, w_glu, b_glu):
    f32 = jnp.float32
    bsz, seq, _ = u.shape
    ug = u.astype(f32).reshape(bsz, seq, SSM_GROUPS, SSM_GROUP)
    lam_re = lam_re.astype(f32)
    lam_im = lam_im.astype(f32)
    step = jnp.exp(log_step.astype(f32))[:, None]
    z_re = lam_re * step
    z_im = lam_im * step
    mag = jnp.exp(z_re)
    a_re = mag * jnp.cos(z_im)
    a_im = mag * jnp.sin(z_im)
    den = lam_re * lam_re + lam_im * lam_im
    num_re = a_re - 1.0
    f_re = (num_re * lam_re + a_im * lam_im) / den
    f_im = (a_im * lam_re - num_re * lam_im) / den
    br = b_re.astype(f32)
    bi = b_im.astype(f32)
    bb_re = f_re[..., None] * br - f_im[..., None] * bi
    bb_im = f_re[..., None] * bi + f_im[..., None] * br
    bu_re = jnp.einsum('blgh,gph->blgp', ug, bb_re)
    bu_im = jnp.einsum('blgh,gph->blgp', ug, bb_im)
    a_re_t = jnp.broadcast_to(a_re, bu_re.shape)
    a_im_t = jnp.broadcast_to(a_im, bu_im.shape)
    _, _, s_re, s_im = lax.associative_scan(
        _complex_affine_combine, (a_re_t, a_im_t, bu_re, bu_im), axis=1)
    y = (jnp.einsum('blgp,ghp->blgh', s_re, c_re.astype(f32))
         - jnp.einsum('blgp,ghp->blgh', s_im, c_im.astype(f32))
         + d_skip.astype(f32) * ug)
    y = y.reshape(bsz, seq, SSM_WIDTH).astype(u.dtype)
    g = jax.nn.gelu(y)
    return g * jax.nn.sigmoid(g @ w_glu + b_glu)


def gla_mixer(q, k, v, r, g_lr, w_gate_up, b_gate, norm_g):
    f32 = jnp.float32
    bsz, seq, _ = q.shape
    n = seq // GLA_CHUNK
    gk = jax.nn.log_sigmoid((g_lr @ w_gate_up + b_gate).astype(f32)) / GLA_TAU

    def to_chunks(t, dh):
        return t.astype(f32).reshape(bsz, n, GLA_CHUNK, GLA_HEADS, dh).transpose(0, 3, 1, 2, 4)

    qc = to_chunks(q, GLA_DK) * (GLA_DK ** -0.5)
    kc = to_chunks(k, GLA_DK)
    vc = to_chunks(v, GLA_DV)
    bcum = jnp.cumsum(to_chunks(gk, GLA_DK), axis=3)
    causal = jnp.tril(jnp.ones((GLA_CHUNK, GLA_CHUNK), dtype=bool))
    diff = bcum[..., :, None, :] - bcum[..., None, :, :]
    decay = jnp.exp(jnp.where(causal[:, :, None], diff, -jnp.inf))
    scores = jnp.einsum('bhnid,bhnjd,bhnijd->bhnij', qc, kc, decay)
    o_intra = jnp.einsum('bhnij,bhnjv->bhniv', scores, vc)
    b_last = bcum[..., -1:, :]
    upd = jnp.einsum('bhncd,bhncv->bhndv', kc * jnp.exp(b_last - bcum), vc)
    chunk_decay = jnp.exp(b_last[..., 0, :])

    def carry_state(state, inp):
        dec, u = inp
        return dec[..., None] * state + u, state

    s0 = jnp.zeros((bsz, GLA_HEADS, GLA_DK, GLA_DV), f32)
    _, s_prev = lax.scan(carry_state, s0,
                         (jnp.moveaxis(chunk_decay, 2, 0), jnp.moveaxis(upd, 2, 0)))
    s_prev = jnp.moveaxis(s_prev, 0, 2)
    o_inter = jnp.einsum('bhncd,bhndv->bhncv', qc * jnp.exp(bcum), s_prev)
    o = (o_intra + o_inter).transpose(0, 2, 3, 1, 4).reshape(bsz, seq, GLA_HEADS, GLA_DV)
    o = o * lax.rsqrt(jnp.mean(o * o, axis=-1, keepdims=True) + LN_EPS)
    o = o.reshape(bsz, seq, GLA_WIDTH) * norm_g.astype(f32)
    return (o * jax.nn.silu(r.astype(f32))).astype(q.dtype)


def memory_cross_attention(x, mem_n, w_q, w_kv, w_o):
    bsz, seq, _ = x.shape
    q = (x @ w_q).reshape(bsz, seq, MEM_HEADS, MEM_HEAD_DIM)
    kv = mem_n @ w_kv
    k = kv[..., :D_MODEL].reshape(bsz, MEM_LEN, MEM_HEADS, MEM_HEAD_DIM)
    v = kv[..., D_MODEL:].reshape(bsz, MEM_LEN, MEM_HEADS, MEM_HEAD_DIM)
    s = jnp.einsum('blhd,bmhd->bhlm', q, k).astype(jnp.float32) * (MEM_HEAD_DIM ** -0.5)
    p = jax.nn.softmax(s, axis=-1).astype(x.dtype)
    o = jnp.einsum('bhlm,bmhd->blhd', p, v).reshape(bsz, seq, D_MODEL)
    return o @ w_o


def setup_inputs(seed: int = 0) -> dict:
    key = jax.random.key(seed)
    ks = jax.random.split(key, 40)
    f32 = jnp.float32
    L = DEPTH

    def nrm(i, shape, scale):
        return jax.random.normal(ks[i], shape, f32) * scale

    return {
        "x": nrm(0, (BATCH, SEQ, D_MODEL), 1.0),
        "mem": nrm(1, (BATCH, MEM_LEN, D_MODEL), 1.0),
        "emb_ln_g": 1.0 + nrm(2, (D_MODEL,), 0.02),
        "emb_ln_b": nrm(3, (D_MODEL,), 0.02),
        "mem_ln_g": 1.0 + nrm(4, (D_MODEL,), 0.02),
        "mem_ln_b": nrm(5, (D_MODEL,), 0.02),
        "w_in": nrm(6, (L, D_MODEL, IN_COLS), D_MODEL ** -0.5),
        "ssm_lam_re": -0.5 + nrm(7, (L, SSM_GROUPS, SSM_STATE), 0.01),
        "ssm_lam_im": math.pi * jnp.broadcast_to(jnp.arange(SSM_STATE, dtype=f32), (L, SSM_GROUPS, SSM_STATE))
                      + nrm(8, (L, SSM_GROUPS, SSM_STATE), 0.01),
        "ssm_log_step": jax.random.uniform(ks[9], (L, SSM_GROUPS), f32, SSM_LOG_STEP_MIN, SSM_LOG_STEP_MAX),
        "ssm_b_re": nrm(10, (L, SSM_GROUPS, SSM_STATE, SSM_GROUP), (2 * SSM_GROUP) ** -0.5),
        "ssm_b_im": nrm(11, (L, SSM_GROUPS, SSM_STATE, SSM_GROUP), (2 * SSM_GROUP) ** -0.5),
        "ssm_c_re": nrm(12, (L, SSM_GROUPS, SSM_GROUP, SSM_STATE), SSM_STATE ** -0.5),
        "ssm_c_im": nrm(13, (L, SSM_GROUPS, SSM_GROUP, SSM_STATE), SSM_STATE ** -0.5),
        "ssm_d": nrm(14, (L, SSM_GROUPS, SSM_GROUP), 1.0),
        "w_glu": nrm(15, (L, SSM_WIDTH, SSM_WIDTH), SSM_WIDTH ** -0.5),
        "b_glu": nrm(16, (L, SSM_WIDTH), 0.01),
        "w_gate_up": nrm(17, (L, GLA_GATE_RANK, GLA_KEY_WIDTH), GLA_GATE_RANK ** -0.5),
        "b_gate": nrm(18, (L, GLA_KEY_WIDTH), 0.01),
        "gla_norm_g": 1.0 + nrm(19, (L, GLA_WIDTH), 0.02),
        "w_out": nrm(20, (L, MIX_WIDTH, D_MODEL), MIX_WIDTH ** -0.5 * DN_BETA),
        "ln1_g": 1.0 + nrm(21, (L, D_MODEL), 0.02),
        "ln1_b": nrm(22, (L, D_MODEL), 0.02),
        "w_mq": nrm(23, (L, D_MODEL, D_MODEL), D_MODEL ** -0.5),
        "w_mkv": nrm(24, (L, D_MODEL, 2 * D_MODEL), D_MODEL ** -0.5),
        "w_mo": nrm(25, (L, D_MODEL, D_MODEL), D_MODEL ** -0.5 * DN_BETA),
        "ln2_g": 1.0 + nrm(26, (L, D_MODEL), 0.02),
        "ln2_b": nrm(27, (L, D_MODEL), 0.02),
        "w_ff1": nrm(28, (L, D_MODEL, D_FF), D_MODEL ** -0.5),
        "w_ff2": nrm(29, (L, D_FF, D_MODEL), D_FF ** -0.5 * DN_BETA),
        "ln3_g": 1.0 + nrm(30, (L, D_MODEL), 0.02),
        "ln3_b": nrm(31, (L, D_MODEL), 0.02),
    }


def reference(x, mem, emb_ln_g, emb_ln_b, mem_ln_g, mem_ln_b, w_in,
              ssm_lam_re, ssm_lam_im, ssm_log_step, ssm_b_re, ssm_b_im, ssm_c_re, ssm_c_im, ssm_d,
              w_glu, b_glu, w_gate_up, b_gate, gla_norm_g, w_out, ln1_g, ln1_b,
              w_mq, w_mkv, w_mo, ln2_g, ln2_b, w_ff1, w_ff2, ln3_g, ln3_b):
    h = layer_norm(x, emb_ln_g, emb_ln_b)
    mem_n = layer_norm(mem, mem_ln_g, mem_ln_b)
    o1 = SSM_WIDTH
    o2 = o1 + GLA_KEY_WIDTH
    o3 = o2 + GLA_KEY_WIDTH
    o4 = o3 + GLA_WIDTH
    o5 = o4 + GLA_WIDTH
    for l in range(DEPTH):
        z = h @ w_in[l]
        y_ssm = s5_mixer(z[..., :o1], ssm_lam_re[l], ssm_lam_im[l], ssm_log_step[l],
                         ssm_b_re[l], ssm_b_im[l], ssm_c_re[l], ssm_c_im[l], ssm_d[l],
                         w_glu[l], b_glu[l])
        y_gla = gla_mixer(z[..., o1:o2], z[..., o2:o3], z[..., o3:o4], z[..., o4:o5],
                          z[..., o5:], w_gate_up[l], b_gate[l], gla_norm_g[l])
        y = jnp.concatenate([y_ssm, y_gla], axis=-1) @ w_out[l]
        h = layer_norm(DN_ALPHA * h + y, ln1_g[l], ln1_b[l])
        h = layer_norm(DN_ALPHA * h + memory_cross_attention(h, mem_n, w_mq[l], w_mkv[l], w_mo[l]),
                       ln2_g[l], ln2_b[l])
        ff = jnp.square(jax.nn.relu(h @ w_ff1[l])) @ w_ff2[l]
        h = layer_norm(DN_ALPHA * h + ff, ln3_g[l], ln3_b[l])
    return h
```

```python
import os
import math
import numpy as np
from contextlib import ExitStack
import concourse.bass as bass
import concourse.mybir as mybir
from concourse.bass_utils import run_bass_kernel_spmd

F32 = mybir.dt.float32
BF16 = mybir.dt.bfloat16
ALU = mybir.AluOpType
AF = mybir.ActivationFunctionType
AX = mybir.AxisListType

NCORES = 8
TOK = 2048
D = 1024
NT = TOK // 128
LN_EPS = 1e-5
DN_ALPHA = 2.0 ** 0.25
PI = math.pi


class Sched:
    EPOCH = 8000
    ENG = ('pe', 'act', 'dve', 'pool', 'sp')

    def __init__(self):
        self.q = {e: [] for e in self.ENG}
        self.cnt = {e: 0 for e in self.ENG}
        self.waited = {}
        self.lastw = {}
        self.readers = {}
        self.dmacnt = {}
        self.semkeys = []
        self._semset = set()
        self.targets = {e: set() for e in self.ENG}

    def _sem(self, k):
        if k not in self._semset:
            self._semset.add(k)
            self.semkeys.append(k)

    def _filter(self, eng, need):
        waits = []
        for s, v in need.items():
            if eng == 'pe' and s == ('eng', 'pe'):
                continue
            if self.waited.get((eng, s), -1) >= v:
                continue
            self.waited[(eng, s)] = v
            waits.append((s, v))
            if s[0] == 'eng':
                self.targets[s[1]].add(v)
        return waits

    def _deps(self, eng, reads, writes):
        need = {}

        def add(ev):
            s, v = ev
            if need.get(s, -1) < v:
                need[s] = v
        for k in reads:
            if k in self.lastw:
                add(self.lastw[k])
        for k in writes:
            if k in self.lastw:
                add(self.lastw[k])
            for ev in self.readers.get(k, ()):
                add(ev)
        return self._filter(eng, need)

    def _register(self, ev, reads, writes):
        for k in reads:
            self.readers.setdefault(k, []).append(ev)
        for k in writes:
            self.lastw[k] = ev
            self.readers[k] = []

    def op(self, eng, fn, r=(), w=()):
        waits = self._deps(eng, r, w)
        idx = self.cnt[eng]
        self.cnt[eng] += 1
        ev = (('eng', eng), idx)
        self._register(ev, r, w)
        self.q[eng].append([waits, fn, 'op', idx])

    def dma(self, qeng, fn, r=(), w=(), sem=None, inc=16, kind='dma'):
        waits = self._deps(qeng, r, w)
        s = (kind, sem)
        self._sem(s)
        self.dmacnt[s] = self.dmacnt.get(s, 0) + inc
        ev = (s, self.dmacnt[s])
        self._register(ev, r, w)
        self.q[qeng].append([waits, fn, 'dma', (s, inc)])

    def fence(self):
        latest = {}
        for e in self.ENG:
            if self.cnt[e] > 0:
                latest[('eng', e)] = self.cnt[e] - 1
        for s_, v in self.dmacnt.items():
            latest[s_] = v
        for e in self.ENG:
            waits = self._filter(e, dict(latest))
            if waits:
                self.q[e].append([waits, None, 'nop', None])
        self.lastw = {}
        self.readers = {}

    def final_waits(self, qeng, keys):
        waits = self._deps(qeng, keys, keys)
        self.q[qeng].append([waits, None, 'nop', None])

    def finalize(self):
        self.rank = {}
        for e in self.ENG:
            self.rank[e] = {idx: r for r, idx in enumerate(sorted(self.targets[e]))}
            n = len(self.rank[e])
            for ep in range((n + self.EPOCH - 1) // self.EPOCH):
                self._sem(('eng', e, ep))

    def _semval(self, e, idx):
        r = self.rank[e][idx]
        return ('eng', e, r // self.EPOCH), r % self.EPOCH + 1

    def replay(self, nc, block, sems):
        def run(engname):
            def f(eng):
                for waits, fn, kind, info in self.q[engname]:
                    for (ws, wv) in waits:
                        if ws[0] == 'eng':
                            k, v = self._semval(ws[1], wv)
                            eng.wait_ge(sems[k], v)
                        else:
                            eng.wait_ge(sems[ws], wv)
                    if fn is None:
                        continue
                    ins = fn(eng)
                    if kind == 'op':
                        if info in self.rank[engname]:
                            k, _ = self._semval(engname, info)
                            ins.then_inc(sems[k], 1)
                    else:
                        ins.then_inc(sems[info[0]], info[1])
            return f
        block.tensor(run('pe'))
        block.scalar(run('act'))
        block.vector(run('dve'))
        block.gpsimd(run('pool'))
        block.sync(run('sp'))


class B:
    def __init__(self, nc, sched):
        self.nc = nc
        self.s = sched

    def dma(self, q, out, in_, r=(), w=(), sem=None):
        self.s.dma(q, lambda e: e.dma_start(out=out, in_=in_), r, w, sem)

    def mm(self, out, lhsT, rhs, start, stop, r=(), w=(), **kw):
        self.s.op('pe', lambda e: e.matmul(out, lhsT, rhs, start=start, stop=stop, **kw), r, w)

    def tr(self, out, in_, ident, r=(), w=()):
        self.s.op('pe', lambda e: e.transpose(out, in_, ident), r, w)

    def act(self, out, in_, func, bias=0.0, scale=1.0, r=(), w=(), accum_out=None):
        if accum_out is None:
            self.s.op('act', lambda e: e.activation(out=out, in_=in_, func=func, bias=bias, scale=scale), r, w)
        else:
            self.s.op('act', lambda e: e.activation(out=out, in_=in_, func=func, bias=bias, scale=scale,
                                                    accum_out=accum_out), r, w)

    def tt(self, eng, out, in0, in1, op, r=(), w=()):
        self.s.op(eng, lambda e: e.tensor_tensor(out=out, in0=in0, in1=in1, op=op), r, w)

    def ts(self, eng, out, in0, s1, s2, op0, op1=None, r=(), w=()):
        if op1 is None:
            self.s.op(eng, lambda e: e.tensor_scalar(out=out, in0=in0, scalar1=s1, scalar2=None, op0=op0), r, w)
        else:
            self.s.op(eng, lambda e: e.tensor_scalar(out=out, in0=in0, scalar1=s1, scalar2=s2, op0=op0, op1=op1), r, w)

    def stt(self, out, in0, scalar, in1, op0, op1, r=(), w=()):
        self.s.op('dve', lambda e: e.scalar_tensor_tensor(out=out, in0=in0, scalar=scalar, in1=in1, op0=op0, op1=op1), r, w)

    def copy(self, eng, out, in_, r=(), w=()):
        if eng == 'act':
            self.s.op('act', lambda e: e.copy(out=out, in_=in_), r, w)
        else:
            self.s.op(eng, lambda e: e.tensor_copy(out=out, in_=in_), r, w)

    def memset(self, eng, ap, val, w=()):
        self.s.op(eng, lambda e: e.memset(ap, val), (), w)

    def scan(self, out, d0, d1, init, op0, op1, r=(), w=()):
        self.s.op('dve', lambda e: e.tensor_tensor_scan(out=out, data0=d0, data1=d1, initial=init, op0=op0, op1=op1), r, w)

    def bn_stats(self, out, in_, r=(), w=()):
        self.s.op('dve', lambda e: e.bn_stats(out=out, in_=in_), r, w)

    def bn_aggr(self, out, in_, r=(), w=()):
        self.s.op('dve', lambda e: e.bn_aggr(out=out, in_=in_), r, w)

    def reduce(self, out, in_, op, r=(), w=()):
        self.s.op('dve', lambda e: e.tensor_reduce(out=out, in_=in_, axis=AX.X, op=op), r, w)

    def cc_allreduce(self, out, in_, r=(), w=()):
        groups = [list(range(NCORES))]
        n = sum(1 for k in self.s.semkeys if k[0] == 'cc')
        self.s.dma('pool', lambda e: e.collective_compute("AllReduce", ALU.add, replica_groups=groups,
                                                          ins=[in_], outs=[out]), r, w, sem=n, inc=1, kind='cc')

    def recip(self, out, in_, r=(), w=()):
        self.s.op('dve', lambda e: e.reciprocal(out=out, in_=in_), r, w)


I32 = mybir.dt.int32
TWO_PI = 2.0 * math.pi


def _prod(xs):
    r = 1
    for v in xs:
        r *= v
    return r


class Arena:
    def __init__(self, nc, es, nbytes):
        self.t = es.enter_context(nc.sbuf_tensor("arena", [128, nbytes // 4], F32))
        self.h = {F32: self.t, BF16: self.t.bitcast(BF16), I32: self.t.bitcast(I32)}
        self.off = 0
        self.cap = nbytes
        self.peak = 0

    def mark(self):
        return self.off

    def release(self, m):
        self.off = m

    def alloc(self, shape, dt=F32):
        sz = 2 if dt == BF16 else 4
        n = _prod(shape[1:])
        nbytes = (n * sz + 31) // 32 * 32
        assert self.off + nbytes <= self.cap, f"arena overflow: {self.off}+{nbytes}>{self.cap}"
        lo = self.off // sz
        ap = self.h[dt][:, lo:lo + n]
        self.off += nbytes
        self.peak = max(self.peak, self.off)
        if len(shape) > 2:
            names = 'abcdef'[:len(shape) - 1]
            pat = "p (" + " ".join(names) + ") -> p " + " ".join(names)
            ap = ap.rearrange(pat, **{k: v for k, v in zip(names, shape[1:])})
        if shape[0] != 128:
            ap = ap[0:shape[0]]
        return ap


def bc(ap, shape, axis):
    return ap.unsqueeze(axis).to_broadcast(list(shape))


class SubArena:
    def __init__(self, A, start, nbytes):
        self.A = A
        self.start = start
        self.cap = nbytes
        self.off = 0

    def reset(self):
        self.off = 0

    def alloc(self, shape, dt=F32):
        save_off, save_cap, save_peak = self.A.off, self.A.cap, self.A.peak
        self.A.off = self.start + self.off
        self.A.cap = self.start + self.cap
        ap = self.A.alloc(shape, dt)
        self.off = self.A.off - self.start
        self.A.off, self.A.cap, self.A.peak = save_off, save_cap, save_peak
        return ap
class T:
    def __init__(self, ap, keys):
        self.ap = ap
        self.k = [keys] if isinstance(keys, str) else list(keys)

    def __getitem__(self, idx):
        return T(self.ap[idx], self.k)

    def v(self, fn):
        return T(fn(self.ap), self.k)

    def key(self, keys):
        return T(self.ap, keys)


def _k(x):
    return x.k if isinstance(x, T) else []


def _a(x):
    return x.ap if isinstance(x, T) else x


class Ops:
    def __init__(self, b):
        self.b = b

    def dma(self, q, out, in_, sem):
        if sem in ('c', 'dbg'):
            self._uniq = getattr(self, '_uniq', 0) + 1
            sem = f'{sem}{self._uniq}'
        self.b.dma(q, _a(out), _a(in_), r=_k(in_), w=_k(out), sem=sem)

    def mm(self, out, lhsT, rhs, start, stop, **kw):
        self.b.mm(_a(out), _a(lhsT), _a(rhs), start, stop, r=_k(lhsT) + _k(rhs), w=_k(out), **kw)

    def tr(self, out, in_, ident):
        self.b.tr(_a(out), _a(in_), _a(ident), r=_k(in_) + _k(ident), w=_k(out))

    def act(self, out, in_, func, bias=0.0, scale=1.0, accum_out=None):
        self.b.act(_a(out), _a(in_), func, bias=_a(bias), scale=_a(scale), r=_k(in_) + _k(bias) + _k(scale),
                   w=_k(out) + _k(accum_out), accum_out=_a(accum_out) if accum_out is not None else None)

    def tt(self, eng, out, a, c, op):
        self.b.tt(eng, _a(out), _a(a), _a(c), op, r=_k(a) + _k(c), w=_k(out))

    def ts(self, eng, out, a, s1, s2, op0, op1=None):
        self.b.ts(eng, _a(out), _a(a), _a(s1), _a(s2), op0, op1, r=_k(a) + _k(s1) + _k(s2), w=_k(out))

    def stt(self, out, a, scalar, c, op0, op1):
        self.b.stt(_a(out), _a(a), _a(scalar), _a(c), op0, op1, r=_k(a) + _k(scalar) + _k(c), w=_k(out))

    def cp(self, eng, out, a):
        self.b.copy(eng, _a(out), _a(a), r=_k(a), w=_k(out))

    def memset(self, eng, out, val):
        self.b.memset(eng, _a(out), val, w=_k(out))

    def scan(self, out, d0, d1, init, op0=None, op1=None):
        self.b.scan(_a(out), _a(d0), _a(d1), _a(init), op0 or ALU.mult, op1 or ALU.add,
                    r=_k(d0) + _k(d1) + _k(init), w=_k(out))

    def reduce(self, out, in_, op):
        self.b.reduce(_a(out), _a(in_), op, r=_k(in_), w=_k(out))

    def recip(self, out, in_):
        self.b.recip(_a(out), _a(in_), r=_k(in_), w=_k(out))

    def bn_stats(self, out, in_):
        self.b.bn_stats(_a(out), _a(in_), r=_k(in_), w=_k(out))

    def bn_aggr(self, out, in_):
        self.b.bn_aggr(_a(out), _a(in_), r=_k(in_), w=_k(out))

    def allreduce(self, out, in_):
        self.b.cc_allreduce(_a(out), _a(in_), r=_k(in_), w=_k(out))
def build_program(debug=()):
    nc = bass.Bass("TRN2", target_bir_lowering=False)
    es = ExitStack()
    S = Sched()
    b = B(nc, S)
    o = Ops(b)
    A = Arena(nc, es, 206 * 1024)

    def alloc(name, shape, dt=F32):
        return T(A.alloc(shape, dt), name)

    def din(name, shape, dt=F32):
        return T(nc.dram_tensor(name, list(shape), dt, kind="ExternalInput").ap(), 'd_' + name)

    pb = [T(es.enter_context(nc.psum_tensor(f"pb{i}", [128, 512], F32))[:], f'pb{i}') for i in range(8)]
    dbg_out = []
    dumpables = {}

    def dump(name, t, shape, dt):
        od = T(nc.dram_tensor("dbg_" + name, list(shape), dt, kind="ExternalOutput").ap(), 'dbgo_' + name)
        dbg_out.append(name)
        o.dma('sp', od, t, sem='dbg')

    mult, add, sub = ALU.mult, ALU.add, ALU.subtract

    x_d = din("x", [TOK, D])
    xprev_d = [din(f"xprev{k}", [TOK, D]) for k in range(3)]
    valid_d = din("valid", [128, 3])
    out_d = T(nc.dram_tensor("out", [TOK, D], F32, kind="ExternalOutput").ap(), 'd_out')
    ident_d = din("ident", [128, 128])
    lng0_d = din("lng0", [128, 8])
    lnb0_d = din("lnb0", [128, 8])
    w_in_d = din("w_in", [128, 8, 2064])
    s5d = {n: din(n, sh) for n, sh in [
        ("s5_lam_re", [128, 4, 64]), ("s5_lam_im", [128, 4, 64]), ("s5_lst", [128, 4]),
        ("s5_bre", [128, 4, 64]), ("s5_bim", [128, 4, 64]), ("s5_cre", [128, 4, 64]), ("s5_cim", [128, 4, 64]),
        ("s5_d", [128, 4]), ("s5_maske", [128, 2]), ("s5p_lam_re", [128, 16]), ("s5p_lam_im", [128, 16]),
        ("s5p_lst", [128, 16]), ("kvec", [128, 129]),
        ("w_glu", [128, 4, 512]), ("b_glu", [128, 4])]}

    ident = alloc('ident', [128, 128])
    lng0 = alloc('lng0', [128, 8])
    lnb0 = alloc('lnb0', [128, 8])
    epsc = alloc('epsc', [128, 1])
    zrow = alloc('zrow', [128, 128], BF16)
    hT = alloc('hT', [128, 8, TOK], BF16)
    ymix_off = A.off
    ymix = alloc('ymix', [128, 8, TOK], BF16)
    OV = SubArena(A, ymix_off, 32 * 1024)

    def oalloc(name, shape, dt=F32):
        return T(OV.alloc(shape, dt), name)
    hTk = lambda kt, g: f'hT{kt}_{g}'

    o.dma('sp', ident, ident_d, sem='c')
    o.dma('sp', lng0, lng0_d, sem='c')
    o.dma('sp', lnb0, lnb0_d, sem='c')
    o.memset('dve', epsc, LN_EPS)
    o.memset('dve', zrow, 0.0)

    def finish():
        for nm_, (t_, shp_, dt_) in dumpables.items():
            if nm_ in debug:
                dump(nm_, t_, shp_, dt_)
        S.fence()
        S.final_waits('sp', ['dbgo_' + n for n in dbg_out])
        print("arena peak bytes", A.peak, "instr counts", S.cnt)
        S.finalize()
        sems = {}
        for k in S.semkeys:
            sems[k] = es.enter_context(nc.semaphore("s_" + "_".join(str(t_) for t_ in k)))
        with nc.Block() as block:
            S.replay(nc, block, sems)
        es.close()
        return nc, dbg_out

    mMix = A.mark()
    uT = alloc('uT', [128, 4, 16, 128], BF16)
    W = alloc('W', [128, 4, 16, 2, 128], BF16)
    dumpables['uT'] = (uT.key([f'uT{m}' for m in range(4)]), [128, 4, 16, 128], BF16)
    dumpables['W'] = (W.key([f'W{k}_{r}' for k in range(16) for r in range(2)]), [128, 4, 16, 2, 128], BF16)
    dumpables['ymix'] = (T(ymix.ap[:, 0:4, :], [f'ymix{a}_{c}' for a in range(4) for c in range(4)]), [128, 4, TOK], BF16)
    dumpables['hT'] = (hT.key([hTk(kt, g) for kt in range(8) for g in range(4)]), [128, 8, TOK], BF16)
    prm = {}
    prm["s5_d"] = alloc("s5_d", [128, 4])
    prm["s5_maske"] = alloc("s5_maske", [128, 2])
    valid = alloc("valid", [128, 3])
    o.dma('sp', valid, valid_d, sem='c')
    sh = [128, 4, 64]
    sh2 = [128, 4, 2, 64]
    shp = [128, 16]
    are, aim = alloc('are', sh), alloc('aim', sh)
    Bm_re, Bm_im = alloc('Bm_re', sh2), alloc('Bm_im', sh2)
    Cm_re, Cm_im = alloc('Cm_re', sh2), alloc('Cm_im', sh2)
    t1, t2, t3, t4 = (alloc(f't{i}', sh2) for i in range(4))
    s1, s2 = alloc('s1', sh), alloc('s2', sh)
    cur = [(alloc('cur_re0', sh), alloc('cur_im0', sh)), (alloc('cur_re1', sh), alloc('cur_im1', sh))]
    Us, Uc = alloc('Us', [128, 16, 129]), alloc('Uc', [128, 16, 129])
    rho, rho128 = alloc('rho', shp), alloc('rho128', shp)
    a128re, a128im = alloc('a128re', shp), alloc('a128im', shp)
    wglu = alloc('wglu', [128, 4, 512], BF16)
    bglu = alloc('bglu', [128, 4])
    o.dma('pool', wglu, s5d['w_glu'], sem='wglu')
    o.dma('sp', bglu, s5d['b_glu'], sem='c')
    win_u = alloc('win_u', [128, 8, 512], BF16)
    o.dma('pool', win_u, w_in_d[:, :, 0:512], sem='win_u')
    mP1 = A.mark()
    for n in ("s5_lam_re", "s5_lam_im", "s5_bre", "s5_bim", "s5_cre", "s5_cim"):
        prm[n] = alloc(n, [128, 4, 64])
    prm["s5_lst"] = alloc("s5_lst", [128, 4])
    for n in ("s5p_lam_re", "s5p_lam_im", "s5p_lst"):
        prm[n] = alloc(n, [128, 16])
    prm["kvec"] = alloc("kvec", [128, 129])
    for n in prm:
        o.dma('sp', prm[n], s5d[n], sem='c')

    def sincos(x, shape, nm, out_s, out_c, kI=None, kf=None):
        if kI is None:
            kI = alloc(nm + '_ki', shape, I32)
            kf = alloc(nm + '_kf', shape)
        for off, r in ((0.0, out_s), (math.pi / 2, out_c)):
            o.ts('dve', kI, x, 1.0 / TWO_PI, off / TWO_PI, mult, add)
            o.cp('dve', kf, kI)
            o.stt(r, kf, -TWO_PI, x, mult, add)
            o.ts('dve', r, r, off, math.pi, add, ALU.min)
            o.ts('dve', r, r, -math.pi, None, ALU.max)
            o.act(r, r, AF.Sin)

    lam_re, lam_im = prm["s5_lam_re"], prm["s5_lam_im"]
    step = alloc('step', [128, 4])
    o.act(step, prm["s5_lst"], AF.Exp)
    stepb = step.v(lambda a: bc(a, sh, 2))
    zre = alloc('zre', sh)
    zim = alloc('zim', sh)
    o.tt('dve', zre, lam_re, stepb, mult)
    o.tt('dve', zim, lam_im, stepb, mult)
    mag = alloc('mag', sh)
    o.act(mag, zre, AF.Exp)
    sinz, cosz = alloc('sinz', sh), alloc('cosz', sh)
    sincos(zim, sh, 'z', sinz, cosz)
    o.tt('dve', are, mag, cosz, mult)
    o.tt('dve', aim, mag, sinz, mult)
    den = alloc('den', sh)
    tA = alloc('tA', sh)
    tB = alloc('tB', sh)
    o.tt('dve', den, lam_re, lam_re, mult)
    o.tt('dve', tA, lam_im, lam_im, mult)
    o.tt('dve', den, den, tA, add)
    o.recip(den, den)
    nre = alloc('nre', sh)
    o.ts('dve', nre, are, -1.0, None, add)
    fre = alloc('fre', sh)
    fim = alloc('fim', sh)
    o.tt('dve', tA, nre, lam_re, mult)
    o.tt('dve', tB, aim, lam_im, mult)
    o.tt('dve', tA, tA, tB, add)
    o.tt('dve', fre, tA, den, mult)
    o.tt('dve', tA, aim, lam_re, mult)
    o.tt('dve', tB, nre, lam_im, mult)
    o.tt('dve', tA, tA, tB, sub)
    o.tt('dve', fim, tA, den, mult)
    bbre = alloc('bbre', sh)
    bbim = alloc('bbim', sh)
    Bre, Bim, Cre, Cim = prm["s5_bre"], prm["s5_bim"], prm["s5_cre"], prm["s5_cim"]
    o.tt('dve', tA, fre, Bre, mult)
    o.tt('dve', tB, fim, Bim, mult)
    o.tt('dve', bbre, tA, tB, sub)
    o.tt('dve', tA, fre, Bim, mult)
    o.tt('dve', tB, fim, Bre, mult)
    o.tt('dve', bbim, tA, tB, add)
    maske = prm["s5_maske"]
    for e in range(2):
        me = maske[:, e:e + 1]
        o.ts('dve', Bm_re[:, :, e, :], bbre, me, None, mult)
        o.ts('dve', Bm_im[:, :, e, :], bbim, me, None, mult)
        o.ts('dve', Cm_re[:, :, e, :], Cre, me, None, mult)
        o.ts('dve', Cm_im[:, :, e, :], Cim, me, None, mult)

    def bce(t):
        return t.v(lambda a: bc(a, sh2, 2))

    def cur_step(pe_, j):
        cr, ci = cur[j % 2]
        nr, ni = cur[(j + 1) % 2]
        o.tt(pe_, s1, cr, are, mult)
        o.tt(pe_, s2, ci, aim, mult)
        o.tt(pe_, nr, s1, s2, sub)
        o.tt(pe_, s1, cr, aim, mult)
        o.tt(pe_, s2, ci, are, mult)
        o.tt(pe_, ni, s1, s2, add)

    W6 = W.v(lambda a: a.rearrange("p m k r (e q) -> p m k r e q", e=2))

    def Wk(kap, ri):
        return T(W6.ap[:, :, kap, ri, :, :], f'W{kap}_{ri}')

    if 'stopP0' in debug:
        return finish()
    o.memset('pool', cur[0][0], 1.0)
    o.memset('pool', cur[0][1], 0.0)
    for j in range(16):
        kap = 15 - j
        cr, ci = cur[j % 2]
        o.tt('pool', t1, bce(cr), Bm_re, mult)
        o.tt('pool', t2, bce(ci), Bm_im, mult)
        o.tt('pool', Wk(kap, 0), t1, t2, sub)
        o.tt('pool', t3, bce(cr), Bm_im, mult)
        o.tt('pool', t4, bce(ci), Bm_re, mult)
        o.tt('pool', Wk(kap, 1), t3, t4, add)
        if j < 15:
            cur_step('pool', j)

    if 'stopP1' in debug:
        return finish()
    stepP = alloc('stepP', shp)
    o.act(stepP, prm["s5p_lst"], AF.Exp)
    zreP = alloc('zreP', shp)
    o.tt('dve', zreP, prm["s5p_lam_re"], stepP, mult)
    o.act(rho, zreP, AF.Exp, scale=16.0)
    o.act(rho128, zreP, AF.Exp, scale=2048.0)
    phi = alloc('phi', shp)
    o.tt('dve', phi, prm["s5p_lam_im"], stepP, mult)
    phk = alloc('phk', shp, I32)
    phf = alloc('phf', shp)
    o.ts('dve', phk, phi, 16.0 / TWO_PI, None, mult)
    o.cp('dve', phf, phk)
    o.ts('dve', phi, phi, 16.0, None, mult)
    o.stt(phi, phf, -TWO_PI, phi, mult, add)
    shU = [128, 8, 129]
    argk = alloc('argk', shU)
    ukI, ukf = alloc('ukI', shU, I32), alloc('ukf', shU)
    for hh in range(2):
        qs = slice(8 * hh, 8 * hh + 8)
        o.tt('dve', argk, phi[:, qs].v(lambda a: bc(a, shU, 2)), prm["kvec"].v(lambda a: bc(a, shU, 1)), mult)
        sincos(argk, shU, f'U{hh}', Us[:, qs, :], Uc[:, qs, :], ukI, ukf)
    o.tt('dve', a128re, rho128, Uc[:, :, 128], mult)
    o.tt('dve', a128im, rho128, Us[:, :, 128], mult)
    if 'stopP2' in debug:
        return finish()
    S.fence()
    A.release(mP1)
    Eend = alloc('Eend', [128, 16, 2])
    prev = [alloc(f'prev{k}', [128, 16, 2]) for k in range(3)]
    Sin_ = alloc('Sin', [128, 16, 2])
    Sn = alloc('Sn', [128, 16, 2])
    c1, c2 = alloc('c1', [128, 16]), alloc('c2', [128, 16])
    Kblk = alloc('Kblk', [128, 4, 16, 128], BF16)
    mSeg = A.mark()
    st = alloc('st', [128, NT, 12])
    mv = alloc('mv', [128, NT, 2])
    sd = alloc('sd', [128, NT])
    rstd = alloc('rstd', [128, NT])
    nmr = alloc('nmr', [128, NT])
    hTg = alloc('hTg', [128, 8, 512], BF16)
    for k_ in range(3):
        dumpables[f'prev{k_}'] = (prev[k_], [128, 16, 2], F32)
    dumpables['Sin'] = (Sin_, [128, 16, 2], F32)
    dumpables['Eend'] = (Eend, [128, 16, 2], F32)
    m1, m2, m3, m4 = (T(t_.ap.rearrange("p m e q -> p (m e q)").rearrange("p (a c) -> p a c", a=4), t_.k)
                      for t_ in (t1, t2, t3, t4))

    def ln_stats_tile(i, src):
        for hh in range(2):
            o.bn_stats(st[:, i, hh * 6:(hh + 1) * 6].key(f'st{i}'), src[:, hh * 512:(hh + 1) * 512])
        o.bn_aggr(mv[:, i, :].key(f'mv{i}'), st[:, i, :].key(f'st{i}'))

    def ln_stats_group(g):
        gs = slice(4 * g, 4 * g + 4)
        mvg = mv[:, gs, :].key([f'mv{4 * g + j}' for j in range(4)])
        o.act(sd[:, gs].key(f'sd{g}'), mvg[:, :, 1], AF.Sqrt, bias=epsc[:, 0:1], scale=1.0)
        o.recip(rstd[:, gs].key(f'rstd{g}'), sd[:, gs].key(f'sd{g}'))
        o.stt(nmr[:, gs].key(f'nmr{g}'), mvg[:, :, 0], -1.0, rstd[:, gs].key(f'rstd{g}'), mult, mult)

    def transposes_to(dst_fn, src_tiles, gT_, bT_):
        for kt in range(8):
            bank = pb[kt % 2]
            for j in range(4):
                o.tr(bank[:, j * 128:(j + 1) * 128], src_tiles[j][:, kt * 128:(kt + 1) * 128], ident)
            dst = dst_fn(kt)
            if kt % 2 == 0:
                o.act(dst, bank, AF.Identity, bias=bT_[:, kt:kt + 1], scale=gT_[:, kt:kt + 1])
            else:
                o.ts('dve', dst, bank, gT_[:, kt:kt + 1], bT_[:, kt:kt + 1], mult, add)

    def s5_segment(xsrc, own, E_dst, uT_dst, ukey):
        S.fence()
        OV.reset()
        xin = [oalloc(f'xin{i}', [128, D]) for i in range(4)]
        xng = oalloc('xng', [128, 4, D])
        for g in range(4):
            for j in range(4):
                i = 4 * g + j
                o.dma('sp', xin[j], xsrc[i * 128:(i + 1) * 128, :], sem=f'xin{j}')
                ln_stats_tile(i, xin[j])
            ln_stats_group(g)
            tiles = []
            for j in range(4):
                i = 4 * g + j
                dst = xng[:, j, :].key(f'xng{j}')
                o.act(dst, xin[j], AF.Identity, bias=nmr[:, i:i + 1].key(f'nmr{g}'), scale=rstd[:, i:i + 1].key(f'rstd{g}'))
                tiles.append(dst)
            if own:
                transposes_to(lambda kt: hT[:, kt, g * 512:(g + 1) * 512].key(hTk(kt, g)), tiles, lng0, lnb0)
                src_fn = lambda kt: hT[:, kt, g * 512:(g + 1) * 512].key(hTk(kt, g))
            else:
                transposes_to(lambda kt: hTg[:, kt, :].key(f'hTg{kt}'), tiles, lng0, lnb0)
                src_fn = lambda kt: hTg[:, kt, :].key(f'hTg{kt}')
            for m in range(4):
                bank = pb[2 + m % 2]
                for kt in range(8):
                    o.mm(bank, win_u[:, kt, m * 128:(m + 1) * 128], src_fn(kt), start=(kt == 0), stop=(kt == 7))
                o.cp('act', uT_dst[:, m, :, 32 * g:32 * g + 32].key(f'{ukey}{m}'),
                     bank.v(lambda a: a.rearrange("p (c t) -> p t c", t=16)))
        S.fence()
        OV.reset()
        Xp_ = oalloc('Xp', [128, 16, 2, 128])
        Vb_ = oalloc('Vb', [128, 16, 2, 128])
        if 'drain' in debug:
            S.op('pe', lambda e: e.drain(), (), ())
        for q in range(16):
            m, qq = divmod(q, 4)
            bank = pb[4 + qq]
            col0 = (m % 2) * 256
            rows = slice(32 * qq, 32 * qq + 32)
            for ri in range(2):
                for kap in range(16):
                    wk = [f'W{kap}_{ri}'] + (['xc_serial'] if ('serial' in debug and ri == 0 and kap == 0) else [])
                    o.mm(bank[:, col0 + ri * 128: col0 + (ri + 1) * 128],
                         T(W.ap[rows, m, kap, ri, :], wk),
                         uT_dst[rows, m, kap, :].key(f'{ukey}{m}'),
                         start=(kap == 0), stop=(kap == 15), tile_position=(32 * qq, 0))
            Xre, Xim = bank[:, col0:col0 + 128], bank[:, col0 + 128:col0 + 256]
            Uc1, Us1 = Uc[:, q, 1:129], Us[:, q, 1:129]
            a1, a2, a3, a4 = m1[:, qq, :], m2[:, qq, :], m3[:, qq, :], m4[:, qq, :]
            o.tt('dve', a1, Xre, Uc1, mult)
            o.tt('dve', a2, Xim, Us1, mult)
            o.tt('dve', Xp_[:, q, 0, :].key(f'Xp{q}'), a1, a2, add)
            o.tt('dve', a3, Xim, Uc1, mult)
            o.tt('dve', a4, Xre, Us1, mult)
            o.tt('dve', Xp_[:, q, 1, :].key([f'Xp{q}'] + (['xc_serial'] if 'serial' in debug else [])), a3, a4, sub)
            for ri in range(2):
                o.scan(Vb_[:, q, ri, :].key(f'Vb{q}'), rho[:, q:q + 1].v(lambda a: a.to_broadcast([128, 128])),
                       Xp_[:, q, ri, :].key(f'Xp{q}'), 0.0)
        if 'drain' in debug:
            S.op('pe', lambda e: e.drain(), (), ())
        Vall = Vb_.key([f'Vb{q}' for q in range(16)])
        Vre, Vim = Vall[:, :, 0, 127], Vall[:, :, 1, 127]
        Uc128, Us128 = Uc[:, :, 128], Us[:, :, 128]
        o.tt('dve', c1, Uc128, Vre, mult)
        o.tt('dve', c2, Us128, Vim, mult)
        o.tt('dve', E_dst[:, :, 0], c1, c2, sub)
        o.tt('dve', c1, Us128, Vre, mult)
        o.tt('dve', c2, Uc128, Vim, mult)
        o.tt('dve', E_dst[:, :, 1], c1, c2, add)
        return Xp_, Vb_

    uTp = T(Kblk.ap.rearrange("p m j c -> p (m j c)").rearrange("p (m t c) -> p m t c", m=4, t=16), 'uTp')
    for k in range(3):
        Xp_k, Vb_k = s5_segment(xprev_d[k], False, prev[k], uTp, 'uTp')
        pk = prev[k].v(lambda a: a.rearrange("p q r -> p (q r)"))
        o.ts('dve', pk, pk, valid[:, k:k + 1], None, mult)
        if k == 0 and 'stopK0' in debug:
            dumpables['uTp'] = (uTp.key([f'uTp{m}' for m in range(4)]), [128, 4, 16, 128], BF16)
            dumpables['XpK'] = (Xp_k.key([f'Xp{q}' for q in range(16)]), [128, 16, 2, 128], F32)
            dumpables['VbK'] = (Vb_k.key([f'Vb{q}' for q in range(16)]), [128, 16, 2, 128], F32)
            return finish()
    Xp, Vb = s5_segment(x_d, True, Eend, uT, 'uT')
    if 'stopB' in debug:
        return finish()
    S.fence()
    A.release(mSeg)
    E0 = alloc('E0', [128, 4, 2, 128], BF16)
    BTb = alloc('BTb', [128, 4, 2, 128], BF16)
    Ef = [alloc(f'Ef{i}', [128, 2, 4, 128]) for i in range(2)]
    Sprev = alloc('Sprev', [128, 16, 2, 128], BF16)
    sgt = [alloc(f'sgt{i}', [128, 512], BF16) for i in range(2)]
    dumpables['Sprev'] = (Sprev.key(['Sprev'] + [f'Sprev{m}' for m in range(4)]), [128, 16, 2, 128], BF16)
    dumpables['Kblk'] = (Kblk.key([f'Kblk{m}' for m in range(4)]), [128, 4, 16, 128], BF16)
    cursrc = prev[2]
    for k in (1, 0):
        o.tt('dve', c1, a128re, cursrc[:, :, 0], mult)
        o.tt('dve', c2, a128im, cursrc[:, :, 1], mult)
        o.tt('dve', c1, c1, c2, sub)
        dst = Sin_ if k == 0 else Sn
        o.tt('dve', dst[:, :, 0], c1, prev[k][:, :, 0], add)
        o.tt('dve', c1, a128re, cursrc[:, :, 1], mult)
        o.tt('dve', c2, a128im, cursrc[:, :, 0], mult)
        o.tt('dve', c1, c1, c2, add)
        o.tt('dve', dst[:, :, 1], c1, prev[k][:, :, 1], add)
        cursrc = dst
    S.fence()
    for q in range(16):
        for ri in range(2):
            o.scan(Vb[:, q, ri, :].key(f'Vb{q}'), rho[:, q:q + 1].v(lambda a: a.to_broadcast([128, 128])),
                   Xp[:, q, ri, :].key(f'Xp{q}'), Sin_[:, q, ri:ri + 1])
    S.fence()
    o.cp('dve', Sprev[:, :, 0, 0], Sin_[:, :, 0])
    o.ts('dve', Sprev[:, :, 1, 0], Sin_[:, :, 1], -1.0, None, mult)
    for m in range(4):
        qs = slice(4 * m, 4 * m + 4)
        Vm = Vb[:, qs, :, :].key([f'Vb{q}' for q in range(4 * m, 4 * m + 4)])
        Vr, Vi = Vm[:, :, 0, 0:127], Vm[:, :, 1, 0:127]
        Uc0, Us0 = Uc[:, qs, 1:128], Us[:, qs, 1:128]
        a1, a2, a3, a4 = m1[:, :, 0:127], m2[:, :, 0:127], m3[:, :, 0:127], m4[:, :, 0:127]
        o.tt('dve', a1, Uc0, Vr, mult)
        o.tt('dve', a2, Us0, Vi, mult)
        o.tt('dve', Sprev[:, qs, 0, 1:128].key(f'Sprev{m}'), a1, a2, sub)
        o.tt('dve', a3, Us0, Vr, mult)
        o.tt('dve', a4, Uc0, Vi, mult)
        o.stt(Sprev[:, qs, 1, 1:128].key(f'Sprev{m}'), a3, -1.0, a4, mult, sub)
    if 'stopC' in debug:
        return finish()
    SprevK = lambda m: ['Sprev', f'Sprev{m}']
    flat4 = lambda t_: t_.v(lambda a: a.rearrange("p m e q -> p m (e q)"))
    for ri, src, sc in (((0, Bm_re, 1.0), (1, Bm_im, -1.0)) if 'skipBT' not in debug else ()):
        bank = pb[6 + ri]
        for m in range(4):
            o.tr(bank[:, m * 128:(m + 1) * 128], flat4(src)[:, m, :], ident)
        o.act(BTb[:, :, ri, :], bank.v(lambda a: a.rearrange("p (m c) -> p m c", m=4)), AF.Identity, scale=sc)

    PE2 = 'dve'
    o.memset(PE2, cur[0][0], 1.0)
    o.memset(PE2, cur[0][1], 0.0)
    for j in (range(17) if 'skipEloop' not in debug else ()):
        cr, ci = cur[j % 2]
        Efj = Ef[j % 2]
        Ef6 = Efj.v(lambda a: a.rearrange("p r m (e q) -> p r m e q", e=2))
        o.tt(PE2, t1, bce(cr), Cm_re, mult)
        o.tt(PE2, t2, bce(ci), Cm_im, mult)
        o.tt(PE2, Ef6[:, 0], t1, t2, sub)
        o.tt(PE2, t3, bce(ci), Cm_re, mult)
        o.tt(PE2, t4, bce(cr), Cm_im, mult)
        o.tt(PE2, Ef6[:, 1], t3, t4, add)
        for ri in (range(2) if 'skipEtr' not in debug else ()):
            bank = pb[6 + ri]
            for m in range(4):
                o.tr(bank[:, m * 128:(m + 1) * 128], Efj[:, ri, m, :], ident)
            if j == 0:
                dst = E0[:, :, ri, :]
            else:
                dst = T(W.ap[:, :, j - 1, ri, :], f'W{j - 1}_{ri}')
            src = bank.v(lambda a: a.rearrange("p (m c) -> p m c", m=4))
            if ri == 0:
                o.cp('act', dst, src)
            else:
                o.cp('dve', dst, src)
        if j < 16:
            cur_step(PE2, j)

    if 'stopD0' in debug:
        return finish()
    for m in range(4):
        for j in range(16):
            t = j % 4
            bank = pb[4 + (m * 4 + j // 4) % 2]
            o.mm(bank[:, t * 128:(t + 1) * 128], zrow[0:1, 0:128], zrow[0:1, 0:128], start=True, stop=False)
            for qq in range(4):
                cols = slice(32 * qq, 32 * qq + 32)
                for ri in range(2):
                    if j == 0:
                        Es = E0[:, m, ri, cols]
                    else:
                        Es = T(W.ap[:, m, j - 1, ri, cols], f'W{j - 1}_{ri}')
                    o.mm(bank[32 * qq:32 * qq + 32, t * 128 + 32 * qq: t * 128 + 32 * qq + 32],
                         BTb[:, m, ri, cols], Es, start=False, stop=(ri == 1), tile_position=(0, 32 * qq))
            if t == 3:
                o.cp('act', Kblk[:, m, j - 3:j + 1, :].key(f'Kblk{m}'),
                     bank.v(lambda a: a.rearrange("p (t c) -> p t c", t=4)))
                if j == 3:
                    o.stt(Kblk[:, m, 0, :].key(f'Kblk{m}'), ident, prm["s5_d"][:, m:m + 1], bank[:, 0:128], mult, add)

    if 'stopD' in debug:
        return finish()

    def gT(m):
        return T(uT.ap[:, m].rearrange("p t c -> p (t c)"), f'uT{m}')

    for m in range(4):
        for bk in range(4):
            bank = pb[bk]
            first = True
            for j in range(0, 4 * bk + 4):
                tlo, thi = max(j, 4 * bk), 4 * bk + 3
                rhs = T(uT.ap[:, m, tlo - j:thi - j + 1, :].rearrange("p t c -> p (t c)"), f'uT{m}')
                o.mm(bank[:, (tlo - 4 * bk) * 128:512], Kblk[:, m, j, :].key(f'Kblk{m}'), rhs, start=first, stop=False)
                first = False
            for tau in range(4 * bk, 4 * bk + 4):
                for qq in range(4):
                    cols = slice(32 * qq, 32 * qq + 32)
                    for ri in range(2):
                        last = (tau == 4 * bk + 3 and ri == 1)
                        o.mm(bank[32 * qq:32 * qq + 32, (tau - 4 * bk) * 128:(tau - 4 * bk + 1) * 128],
                             T(W.ap[:, m, tau, ri, cols], f'W{tau}_{ri}'),
                             Sprev[:, 4 * m + qq, ri, :].key(SprevK(m)), start=False, stop=last,
                             tile_position=(0, 32 * qq))
        for bk in range(4):
            dst = gT(m).v(lambda a: a.rearrange("p (c t) -> p t c", t=16))[:, 4 * bk:4 * bk + 4, :]
            o.act(dst, pb[bk].v(lambda a: a.rearrange("p (t c) -> p t c", t=4)), AF.Gelu_apprx_tanh)

    if 'stopE' in debug:
        return finish()
    S.fence()
    for mo in range(4):
        for nb in range(4):
            idx = mo * 4 + nb
            bank = pb[4 + idx % 2]
            ns = slice(nb * 512, (nb + 1) * 512)
            for m in range(4):
                o.mm(bank, wglu[:, m, mo * 128:(mo + 1) * 128], gT(m)[:, ns], start=(m == 0), stop=(m == 3))
            sg = sgt[idx % 2]
            o.act(sg, bank, AF.Sigmoid, bias=bglu[:, mo:mo + 1])
            o.tt('dve', ymix[:, mo, ns].key(f'ymix{mo}_{nb}'), gT(mo)[:, ns], sg, mult)

    if 'nogla' in debug:
        return finish()
    S.fence()
    A.release(mMix)
    gd = {n: din(n, sh) for n, sh in [("gla_wgu", [16, 256]), ("gla_bg", [128, 2]), ("gla_ng", [128, 512]),
                                      ("tri", [128, 128]), ("rmask", [128, 512])]}
    valid = alloc('valid2', [128, 3])
    o.dma('sp', valid, valid_d, sem='c')
    st = alloc('st2', [128, NT, 12])
    mv = alloc('mv2', [128, NT, 2])
    sd = alloc('sd2', [128, NT])
    rstd = alloc('rstd2', [128, NT])
    nmr = alloc('nmr2', [128, NT])
    hTg = alloc('hTg2', [128, 8, 512], BF16)
    wk_ = alloc('wk', [128, 8, 256], BF16)
    wq_ = alloc('wq', [128, 8, 256], BF16)
    wv_ = alloc('wv', [128, 8, 512], BF16)
    wr_ = alloc('wr', [128, 8, 512], BF16)
    wg_ = alloc('wg', [128, 8, 16], BF16)
    o.dma('pool', wq_, w_in_d[:, :, 512:768], sem='wq')
    o.dma('pool', wk_, w_in_d[:, :, 768:1024], sem='wk')
    o.dma('pool', wv_, w_in_d[:, :, 1024:1536], sem='wv')
    o.dma('pool', wr_, w_in_d[:, :, 1536:2048], sem='wr')
    o.dma('pool', wg_, w_in_d[:, :, 2048:2064], sem='wg')
    wgu = alloc('wgu', [16, 256], BF16)
    o.dma('pool', wgu, gd['gla_wgu'], sem='wgu')
    bg = alloc('bg', [128, 2])
    nbg = alloc('nbg', [128, 2])
    gng = alloc('gng', [128, 512])
    tri = alloc('tri', [128, 128])
    rmask = alloc('rmask', [128, 512])
    onec = alloc('onec', [128, 1])
    o.dma('sp', bg, gd['gla_bg'], sem='c')
    o.dma('sp', gng, gd['gla_ng'], sem='c')
    o.dma('sp', tri, gd['tri'], sem='c')
    o.dma('sp', rmask, gd['rmask'], sem='c')
    o.ts('dve', nbg, bg, -1.0, None, mult)
    o.memset('dve', onec, 1.0)
    xin2 = [alloc(f'gxin{i}', [128, D]) for i in range(4)]
    xng2 = alloc('gxng', [128, 4, D])
    glr = alloc('glr', [16, 512], BF16)
    spl = alloc('spl', [128, 2, 512])
    cum = alloc('cum', [128, 2, 512])
    ekl = alloc('ekl', [128, 2, 512])
    eb = alloc('eb', [128, 2, 512])
    enb = alloc('enb', [128, 2, 512])
    klT = alloc('klT', [128, 2, 512])
    klt = alloc('klt', [128, 4, 256], BF16)
    vt = alloc('vt', [128, 4, 512], BF16)
    qeT = alloc('qeT', [128, 2, 512], BF16)
    keT = alloc('keT', [128, 2, 512], BF16)
    ncl = alloc('ncl', [128, 2, 4])
    dec = alloc('dec', [128, 2, 4])
    Sg_ = alloc('Sgla', [128, 2, 128])
    Sbf = alloc('Sbf', [128, 2, 128], BF16)
    scT = alloc('scT', [128, 4, 128], BF16)
    rsil = alloc('rsil', [128, 512])
    gnrs = alloc('gnrs', [128, 512])
    ysb = alloc('ysb', [128, 512])
    ssq = alloc('ssq', [128, 4])
    rs4 = alloc('rs4', [128, 4])
    junk = alloc('junk', [128, 128])
    dumpables['ygla'] = (T(ymix.ap[:, 4:8, :], [f'ymixg{c}' for c in range(NT)]), [128, 4, TOK], BF16)
    dumpables['Sgla'] = (Sg_, [128, 2, 128], F32)

    def gla_segment(xsrc, own):
        for g in range(4):
            if own:
                src_fn = lambda kt: hT[:, kt, g * 512:(g + 1) * 512].key(hTk(kt, g))
            else:
                for j in range(4):
                    i = 4 * g + j
                    o.dma('sp', xin2[j], xsrc[i * 128:(i + 1) * 128, :], sem=f'gxin{j}')
                    ln_stats_tile(i, xin2[j])
                ln_stats_group(g)
                tiles = []
                for j in range(4):
                    i = 4 * g + j
                    dst = xng2[:, j, :].key(f'gxng{j}')
                    o.act(dst, xin2[j], AF.Identity, bias=nmr[:, i:i + 1].key(f'nmr{g}'), scale=rstd[:, i:i + 1].key(f'rstd{g}'))
                    tiles.append(dst)
                transposes_to(lambda kt: hTg[:, kt, :].key(f'hTg{kt}'), tiles, lng0, lnb0)
                src_fn = lambda kt: hTg[:, kt, :].key(f'hTg{kt}')
            bank = pb[2]
            for kt in range(8):
                o.mm(bank[0:16, :], wg_[:, kt, :], src_fn(kt), start=(kt == 0), stop=(kt == 7))
            o.cp('act', glr, bank[0:16, :])
            for t in range(2):
                bk = pb[3]
                o.mm(bk, wgu[:, t * 128:(t + 1) * 128], glr, start=True, stop=True)
                o.act(spl[:, t, :], bk, AF.Exp, bias=nbg[:, t:t + 1], scale=-1.0)
                o.act(spl[:, t, :], spl[:, t, :], AF.Ln, bias=onec[:, 0:1], scale=1.0)
                o.scan(cum[:, t, :], rmask, spl[:, t, :], 0.0)
                cl = cum[:, t, :].v(lambda a: a.rearrange("p (c k) -> p c k", k=128))[:, :, 127]
                o.ts('dve', ncl[:, t, :], cl, -1.0 / 16.0, None, mult)
                o.act(dec[:, t, :], ncl[:, t, :], AF.Exp)
                for cc in range(4):
                    cs = slice(cc * 128, (cc + 1) * 128)
                    o.act(ekl[:, t, cs], cum[:, t, cs], AF.Exp, bias=ncl[:, t, cc:cc + 1], scale=1.0 / 16.0)
                if own:
                    o.act(eb[:, t, :], cum[:, t, :], AF.Exp, scale=-1.0 / 16.0)
                    o.act(enb[:, t, :], cum[:, t, :], AF.Exp, scale=1.0 / 16.0)
            for t in range(2):
                bk = pb[4 + t]
                for kt in range(8):
                    o.mm(bk, wk_[:, kt, t * 128:(t + 1) * 128], src_fn(kt), start=(kt == 0), stop=(kt == 7))
                o.tt('dve', klT[:, t, :], bk, ekl[:, t, :], mult)
                if own:
                    o.tt('dve', keT[:, t, :], bk, enb[:, t, :], mult)
                    bq = pb[6 + t]
                    for kt in range(8):
                        o.mm(bq, wq_[:, kt, t * 128:(t + 1) * 128], src_fn(kt), start=(kt == 0), stop=(kt == 7))
                    o.stt(qeT[:, t, :], bq, 0.125, eb[:, t, :], mult, mult)
            for j in range(4):
                c = 4 * g + j
                tsl = slice(j * 128, (j + 1) * 128)
                bv = pb[0]
                for kt in range(8):
                    o.mm(bv, src_fn(kt)[:, tsl], wv_[:, kt, :], start=(kt == 0), stop=(kt == 7))
                o.cp('act', vt[:, j, :], bv)
                bt = pb[1]
                for t in range(2):
                    o.tr(bt[:, t * 128:(t + 1) * 128], klT[:, t, tsl], ident)
                o.cp('dve', klt[:, j, :], bt[:, 0:256])
                if own:
                    bsb = (pb[2], pb[7])
                    for h in range(4):
                        t, r0 = divmod(h, 2)
                        rows = slice(64 * r0, 64 * r0 + 64)
                        o.mm(bsb[r0][:, t * 128:(t + 1) * 128], keT[rows, t, tsl], qeT[rows, t, tsl], start=True, stop=True)
                    for r0 in range(2):
                        o.tt('dve', T(scT.ap.rearrange("p (t r) i -> p r t i", r=2)[:, r0], scT.k),
                             bsb[r0][:, 0:256].v(lambda a: a.rearrange("p (t i) -> p t i", t=2)),
                             tri.v(lambda a: bc(a, [128, 2, 128], 1)), mult)
                    bob = (pb[3], pb[5])
                    for h in range(4):
                        t, r0 = divmod(h, 2)
                        rows = slice(64 * r0, 64 * r0 + 64)
                        hs = slice(h * 128, (h + 1) * 128)
                        ob = bob[r0][:, t * 128:(t + 1) * 128]
                        o.mm(ob, scT[:, h, :], vt[:, j, hs], start=True, stop=False)
                        o.mm(ob, qeT[rows, t, tsl], Sbf[rows, t, :], start=False, stop=True)
                    br = pb[4]
                    for kt in range(8):
                        o.mm(br, src_fn(kt)[:, tsl], wr_[:, kt, :], start=(kt == 0), stop=(kt == 7))
                    o.act(rsil, br, AF.Silu)
                    o.tt('dve', gnrs, rsil, gng, mult)
                    for h in range(4):
                        t, r0 = divmod(h, 2)
                        o.act(junk, bob[r0][:, t * 128:(t + 1) * 128], AF.Square, accum_out=ssq[:, h:h + 1])
                    o.ts('dve', rs4, ssq, 1.0 / 128.0, LN_EPS, mult, add)
                    o.act(rs4, rs4, AF.Sqrt)
                    o.recip(rs4, rs4)
                    for h in range(4):
                        t, r0 = divmod(h, 2)
                        hs = slice(h * 128, (h + 1) * 128)
                        o.stt(ysb[:, hs], bob[r0][:, t * 128:(t + 1) * 128], rs4[:, h:h + 1], gnrs[:, hs], mult, mult)
                    bt2 = pb[1]
                    for h in range(4):
                        o.tr(bt2[:, h * 128:(h + 1) * 128], ysb[:, h * 128:(h + 1) * 128], ident)
                    o.cp('act', T(ymix.ap[:, 4:8, c * 128:(c + 1) * 128], f'ymixg{c}'),
                         bt2.v(lambda a: a.rearrange("p (h k) -> p h k", h=4)))
                bu_ = pb[6]
                for h in range(4):
                    t, r0 = divmod(h, 2)
                    o.mm(bu_[64 * r0:64 * r0 + 64, t * 128:(t + 1) * 128], klt[:, j, h * 64:(h + 1) * 64],
                         vt[:, j, h * 128:(h + 1) * 128], start=True, stop=True)
                for t in range(2):
                    o.stt(Sg_[:, t, :], Sg_[:, t, :], dec[:, t, j:j + 1], bu_[:, t * 128:(t + 1) * 128], mult, add)
                o.cp('act', Sbf, Sg_)

    o.memset('dve', Sg_, 0.0)
    o.memset('dve', Sbf, 0.0)
    Sflat = Sg_.v(lambda a: a.rearrange("p t v -> p (t v)"))
    for k in (2, 1, 0):
        gla_segment(xprev_d[k], False)
        o.ts('dve', Sflat, Sflat, valid[:, k:k + 1], None, mult)
        o.cp('act', Sbf, Sg_)
    if 'stopG0' in debug:
        return finish()
    gla_segment(x_d, True)
    if 'stopMix' in debug:
        return finish()
    S.fence()
    A.release(mMix)
    bd = {n: din(n, sh) for n, sh in [
        ("g0rep", [128, D]), ("b0rep", [128, D]), ("g1rep", [128, D]), ("b1rep", [128, D]),
        ("g2rep", [128, D]), ("b2rep", [128, D]), ("g3rep", [128, D]), ("b3rep", [128, D]),
        ("g1col", [128, 8]), ("b1col", [128, 8]), ("g2col", [128, 8]), ("b2col", [128, 8]),
        ("gmcol", [128, 8]), ("bmcol", [128, 8]), ("ones", [128, 128]), ("mem", [256, D]),
        ("w_out", [128, 8, 1024]), ("w_mq", [128, 8, 1024]), ("w_mo", [128, 8, 1024]), ("w_mkv", [128, 8, 2048]),
        ("w_ff1", [128, 8, 4096]), ("w_ff2", [128, 32, 1024])]}
    xn = alloc('xn', [128, NT, D])
    st = alloc('st3', [128, NT, 12])
    mv = alloc('mv3', [128, NT, 2])
    sd = alloc('sd3', [128, NT])
    rstd = alloc('rstd3', [128, NT])
    nmr = alloc('nmr3', [128, NT])
    grep_, brep_ = alloc('grep', [128, D]), alloc('brep', [128, D])
    gcol, bcol = alloc('gcol', [128, 8]), alloc('bcol', [128, 8])
    tb = [alloc(f'tb{i}', [128, D]) for i in range(4)]
    wA = alloc('wA', [128, 8, 1024], BF16)
    mv0, sc0 = alloc('mv0', [128, 2]), alloc('sc0', [128, 4])
    st0 = alloc('st0', [128, 12])
    dumpables['xn'] = (xn.key([f'xn{i}' for i in range(NT)]), [128, NT, D], F32)

    def ln_tile_inplace(buf):
        for hh in range(2):
            o.bn_stats(st0[:, hh * 6:(hh + 1) * 6], buf[:, hh * 512:(hh + 1) * 512])
        o.bn_aggr(mv0, st0)
        o.act(sc0[:, 0:1], mv0[:, 1:2], AF.Sqrt, bias=epsc[:, 0:1], scale=1.0)
        o.recip(sc0[:, 1:2], sc0[:, 0:1])
        o.stt(sc0[:, 2:3], mv0[:, 0:1], -1.0, sc0[:, 1:2], mult, mult)
        o.act(buf, buf, AF.Identity, bias=sc0[:, 2:3], scale=sc0[:, 1:2])

    def res_from_x(i, dst):
        o.dma('sp', dst, x_d[i * 128:(i + 1) * 128, :], sem=f'oxin{i % 4}')
        ln_tile_inplace(dst)
        o.tt('dve', dst, dst, grep_, mult)
        o.tt('dve', dst, dst, brep_, add)

    def res_from_xn(i, dst):
        o.tt('dve', dst, xn[:, i, :].key(f'xn{i}'), grep_, mult)
        o.tt('dve', dst, dst, brep_, add)

    def post_ln_group(g, write_hT):
        ln_stats_group(g)
        tiles = []
        for j in range(4):
            i = 4 * g + j
            dst = xn[:, i, :].key(f'xn{i}')
            o.act(dst, tb[j], AF.Identity, bias=nmr[:, i:i + 1].key(f'nmr{g}'), scale=rstd[:, i:i + 1].key(f'rstd{g}'))
            tiles.append(dst)
        if write_hT:
            transposes_to(lambda kt: hT[:, kt, g * 512:(g + 1) * 512].key(hTk(kt, g)), tiles, gcol, bcol)

    def dense_res_ln(srcT_fn, nk, w_fn, res_fn, write_hT=True):
        for g in range(4):
            for j in range(4):
                i = 4 * g + j
                banks = (pb[2], pb[3])
                for half in range(2):
                    for kt in range(nk):
                        o.mm(banks[half], srcT_fn(kt, i), w_fn(kt, half), start=(kt == 0), stop=(kt == nk - 1))
                res_fn(i, tb[j])
                for half in range(2):
                    hs = slice(half * 512, (half + 1) * 512)
                    o.stt(tb[j][:, hs], tb[j][:, hs], DN_ALPHA, banks[half], mult, add)
                ln_stats_tile(i, tb[j])
            post_ln_group(g, write_hT)

    o.dma('pool', wA, bd['w_out'], sem='wA')
    o.dma('sp', grep_, bd['g0rep'], sem='c')
    o.dma('sp', brep_, bd['b0rep'], sem='c')
    o.dma('sp', gcol, bd['g1col'], sem='c')
    o.dma('sp', bcol, bd['b1col'], sem='c')

    def ymixT(kt, i):
        key = f'ymix{kt}_{i // 4}' if kt < 4 else f'ymixg{i}'
        return T(ymix.ap[:, kt, i * 128:(i + 1) * 128], key)

    dense_res_ln(ymixT, 8, lambda kt, half: wA[:, kt, half * 512:(half + 1) * 512], res_from_x)
    if 'stopO' in debug:
        return finish()
    S.fence()
    mX = A.mark()
    kT = alloc('kT', [128, 8, 256], BF16)
    vtk = alloc('vtk', [128, 2, 1024], BF16)
    wkvq = alloc('wkvq', [128, 8, 512], BF16)
    qTh = alloc('qTh', [128, 2, TOK], BF16)
    pT1 = alloc('pT1', [128, 2, 512], BF16)
    rrec = alloc('rrec', [128, 512])
    onesb = alloc('onesb', [128, 128], BF16)
    gmc, bmc = alloc('gmc', [128, 8]), alloc('bmc', [128, 8])
    o.dma('pool', onesb, bd['ones'], sem='onesb')
    o.dma('sp', gmc, bd['gmcol'], sem='c')
    o.dma('sp', bmc, bd['bmcol'], sem='c')
    oT = T(ymix.ap, 'oT')
    mS_ = A.mark()
    memT = alloc('memT', [128, 8, 256], BF16)
    for mt in range(2):
        o.dma('sp', tb[mt], bd['mem'][mt * 128:(mt + 1) * 128, :], sem=f'memin{mt}')
        ln_tile_inplace(tb[mt])
    for kt in range(8):
        bank = pb[kt % 2]
        for mt in range(2):
            o.tr(bank[:, mt * 128:(mt + 1) * 128], tb[mt][:, kt * 128:(kt + 1) * 128], ident)
        o.act(memT[:, kt, :], bank[:, 0:256], AF.Identity, bias=bmc[:, kt:kt + 1], scale=gmc[:, kt:kt + 1])
    for qd in range(4):
        o.dma('pool', wkvq, bd['w_mkv'][:, :, qd * 512:(qd + 1) * 512], sem='wkvq')
        if qd < 2:
            for c4 in range(4):
                bank = pb[4 + c4 % 2]
                for kt in range(8):
                    o.mm(bank[:, 0:256], wkvq[:, kt, c4 * 128:(c4 + 1) * 128], memT[:, kt, :], start=(kt == 0), stop=(kt == 7))
                o.cp('act', kT[:, qd * 4 + c4, :], bank[:, 0:256])
        else:
            for mt in range(2):
                bank = pb[6 + mt]
                for kt in range(8):
                    o.mm(bank, memT[:, kt, mt * 128:(mt + 1) * 128], wkvq[:, kt, :], start=(kt == 0), stop=(kt == 7))
                o.cp('act', vtk[:, mt, (qd - 2) * 512:(qd - 1) * 512], bank)
    S.fence()
    A.release(mS_)
    o.dma('pool', wA, bd['w_mq'], sem='wA')
    for h in range(4):
        for c2 in range(2):
            c = 2 * h + c2
            for g in range(4):
                bank = pb[4 + g % 2]
                for kt in range(8):
                    o.mm(bank, wA[:, kt, c * 128:(c + 1) * 128], hT[:, kt, g * 512:(g + 1) * 512].key(hTk(kt, g)),
                         start=(kt == 0), stop=(kt == 7))
                o.act(qTh[:, c2, g * 512:(g + 1) * 512], bank, AF.Identity, scale=1.0 / 16.0)
        for g in range(4):
            gs = slice(g * 512, (g + 1) * 512)
            for mt in range(2):
                bank = pb[6 + mt]
                for c2 in range(2):
                    o.mm(bank, kT[:, 2 * h + c2, mt * 128:(mt + 1) * 128], qTh[:, c2, gs], start=(c2 == 0), stop=(c2 == 1))
                o.act(pT1[:, mt, :], bank, AF.Exp)
            bank = pb[0]
            for mt in range(2):
                o.mm(bank, onesb, pT1[:, mt, :], start=(mt == 0), stop=(mt == 1))
            o.recip(rrec, bank)
            for c2 in range(2):
                c = 2 * h + c2
                bank = pb[1 + c2]
                for mt in range(2):
                    o.mm(bank, vtk[:, mt, c * 128:(c + 1) * 128], pT1[:, mt, :], start=(mt == 0), stop=(mt == 1))
                o.tt('dve', T(oT.ap[:, c, gs], f'oT{c}_{g}'), bank, rrec, mult)
    o.dma('pool', wA, bd['w_mo'], sem='wA')
    o.dma('sp', grep_, bd['g1rep'], sem='c')
    o.dma('sp', brep_, bd['b1rep'], sem='c')
    o.dma('sp', gcol, bd['g2col'], sem='c')
    o.dma('sp', bcol, bd['b2col'], sem='c')
    dense_res_ln(lambda kt, i: T(oT.ap[:, kt, i * 128:(i + 1) * 128], f'oT{kt}_{i // 4}'), 8,
                 lambda kt, half: wA[:, kt, half * 512:(half + 1) * 512], res_from_xn)
    if 'stopX' in debug:
        return finish()

    S.fence()
    A.release(mX)
    hid = T(ymix.ap, 'hid')
    w1q = alloc('w1q', [128, 8, 1024], BF16)
    w2q = alloc('w2q', [128, 8, 1024], BF16)
    rl = [alloc('rl0', [128, 512])]
    o.dma('sp', grep_, bd['g2rep'], sem='c')
    o.dma('sp', brep_, bd['b2rep'], sem='c')
    for qt in range(4):
        o.dma('pool', w1q, bd['w_ff1'][:, :, qt * 1024:(qt + 1) * 1024], sem='w1q')
        o.dma('pool', w2q, bd['w_ff2'][:, qt * 8:(qt + 1) * 8, :], sem='w2q')
        for ft in range(8):
            for g in range(4):
                idx = ft * 4 + g
                bank = pb[4 + idx % 2]
                for kt in range(8):
                    o.mm(bank, w1q[:, kt, ft * 128:(ft + 1) * 128], hT[:, kt, g * 512:(g + 1) * 512].key(hTk(kt, g)),
                         start=(kt == 0), stop=(kt == 7))
                r_ = rl[0]
                o.act(r_, bank, AF.Relu)
                o.tt('dve', T(hid.ap[:, ft, g * 512:(g + 1) * 512], f'hid{ft}_{g}'), r_, r_, mult)
        if qt == 1:
            o.dma('sp', grep_, bd['g3rep'], sem='c')
            o.dma('sp', brep_, bd['b3rep'], sem='c')
        for g in range(4):
            for j in range(4):
                i = 4 * g + j
                banks = (pb[2], pb[3])
                for half in range(2):
                    for ft in range(8):
                        o.mm(banks[half], T(hid.ap[:, ft, i * 128:(i + 1) * 128], f'hid{ft}_{g}'),
                             w2q[:, ft, half * 512:(half + 1) * 512], start=(ft == 0), stop=(ft == 7))
                xi = xn[:, i, :].key(f'xn{i}')
                if qt == 0:
                    res_from_xn(i, xi)
                for half in range(2):
                    hs = slice(half * 512, (half + 1) * 512)
                    if qt == 0:
                        o.stt(xi[:, hs], xi[:, hs], DN_ALPHA, banks[half], mult, add)
                    elif qt < 3:
                        o.tt('dve', xi[:, hs], xi[:, hs], banks[half], add)
                    else:
                        o.tt('dve', tb[j][:, hs], xi[:, hs], banks[half], add)
                if qt == 3:
                    ln_stats_tile(i, tb[j])
            if qt == 3:
                ln_stats_group(g)
                for j in range(4):
                    i = 4 * g + j
                    o.act(tb[j], tb[j], AF.Identity, bias=nmr[:, i:i + 1].key(f'nmr{g}'), scale=rstd[:, i:i + 1].key(f'rstd{g}'))
                    o.tt('dve', tb[j], tb[j], grep_, mult)
                    o.tt('dve', tb[j], tb[j], brep_, add)
                    o.dma('sp', T(out_d.ap[i * 128:(i + 1) * 128, :], f'out{i}'), tb[j], sem=f'outs{j}')
    return finish()


def host_inputs(inp):
    f32 = np.float32
    x = np.ascontiguousarray(inp['x'], dtype=f32)

    def cols(v):
        v = np.asarray(v, f32).reshape(-1)
        return np.ascontiguousarray(v.reshape(-1, 128).T)

    def ktile(wm):
        K, N = wm.shape
        return np.ascontiguousarray(np.asarray(wm, f32).reshape(K // 128, 128, N).transpose(1, 0, 2))

    def rep(v):
        return np.ascontiguousarray(np.broadcast_to(np.asarray(v, f32).reshape(1, -1), (128, D)))

    def gh_layout(a_gp):
        a = np.asarray(a_gp, f32).reshape(4, 8, 64)
        a = np.broadcast_to(a.transpose(1, 0, 2)[:, None, :, :], (8, 16, 4, 64))
        return np.ascontiguousarray(a.reshape(128, 4, 64))

    def pair_layout(a_gp):
        a = np.asarray(a_gp, f32).reshape(16, 2, 64)
        return np.ascontiguousarray(a.transpose(1, 2, 0).reshape(128, 16))

    lst = np.asarray(inp['ssm_log_step'][0], f32)
    b_re = np.asarray(inp['ssm_b_re'][0], f32).reshape(4, 8, 64, 16)
    b_im = np.asarray(inp['ssm_b_im'][0], f32).reshape(4, 8, 64, 16)
    c_re = np.asarray(inp['ssm_c_re'][0], f32).reshape(4, 8, 16, 64)
    c_im = np.asarray(inp['ssm_c_im'][0], f32).reshape(4, 8, 16, 64)
    maske = np.zeros((8, 16, 2), f32)
    for gp in range(8):
        maske[gp, :, gp % 2] = 1.0
    common = {
        'ident': np.eye(128, dtype=f32),
        'lng0': cols(inp['emb_ln_g']),
        'lnb0': cols(inp['emb_ln_b']),
        'w_in': ktile(inp['w_in'][0]),
        's5_lam_re': gh_layout(inp['ssm_lam_re'][0]),
        's5_lam_im': gh_layout(inp['ssm_lam_im'][0]),
        's5_lst': np.ascontiguousarray(np.broadcast_to(lst.reshape(4, 8).T[:, None, :], (8, 16, 4)).reshape(128, 4)),
        's5_bre': np.ascontiguousarray(b_re.transpose(1, 3, 0, 2).reshape(128, 4, 64)),
        's5_bim': np.ascontiguousarray(b_im.transpose(1, 3, 0, 2).reshape(128, 4, 64)),
        's5_cre': np.ascontiguousarray(c_re.transpose(1, 2, 0, 3).reshape(128, 4, 64)),
        's5_cim': np.ascontiguousarray(c_im.transpose(1, 2, 0, 3).reshape(128, 4, 64)),
        's5_d': np.ascontiguousarray(np.asarray(inp['ssm_d'][0], f32).reshape(4, 8, 16).transpose(1, 2, 0).reshape(128, 4)),
        's5_maske': np.ascontiguousarray(maske.reshape(128, 2)),
        's5p_lam_re': pair_layout(inp['ssm_lam_re'][0]),
        's5p_lam_im': pair_layout(inp['ssm_lam_im'][0]),
        's5p_lst': np.ascontiguousarray(np.broadcast_to(lst.reshape(16, 2)[:, :, None], (16, 2, 64)).transpose(1, 2, 0).reshape(128, 16)),
        'kvec': np.ascontiguousarray(np.broadcast_to(np.arange(129, dtype=f32)[None, :], (128, 129))),
        'w_glu': ktile(inp['w_glu'][0]),
        'gla_wgu': np.ascontiguousarray(np.asarray(inp['w_gate_up'][0], f32)),
        'gla_bg': cols(inp['b_gate'][0]),
        'gla_ng': np.ascontiguousarray(np.broadcast_to(np.asarray(inp['gla_norm_g'][0], f32)[None, :], (128, 512))),
        'tri': np.triu(np.ones((128, 128), f32)),
        'ones': np.ones((128, 128), f32),
        'g0rep': rep(inp['emb_ln_g']), 'b0rep': rep(inp['emb_ln_b']),
        'g1rep': rep(inp['ln1_g'][0]), 'b1rep': rep(inp['ln1_b'][0]),
        'g2rep': rep(inp['ln2_g'][0]), 'b2rep': rep(inp['ln2_b'][0]),
        'g3rep': rep(inp['ln3_g'][0]), 'b3rep': rep(inp['ln3_b'][0]),
        'g1col': cols(inp['ln1_g'][0]), 'b1col': cols(inp['ln1_b'][0]),
        'g2col': cols(inp['ln2_g'][0]), 'b2col': cols(inp['ln2_b'][0]),
        'gmcol': cols(inp['mem_ln_g']), 'bmcol': cols(inp['mem_ln_b']),
        'w_out': ktile(inp['w_out'][0]), 'w_mq': ktile(inp['w_mq'][0]), 'w_mo': ktile(inp['w_mo'][0]),
        'w_mkv': ktile(inp['w_mkv'][0]), 'w_ff1': ktile(inp['w_ff1'][0]), 'w_ff2': ktile(inp['w_ff2'][0]),
        'rmask': np.ascontiguousarray(np.broadcast_to((np.arange(512) % 128 != 0).astype(f32)[None, :], (128, 512))),
        'b_glu': cols(inp['b_glu'][0]),
    }
    maps = []
    for c in range(NCORES):
        bb, j = divmod(c, 4)
        m = dict(common)
        m['x'] = np.ascontiguousarray(x[bb, j * TOK:(j + 1) * TOK, :])
        m['mem'] = np.ascontiguousarray(np.asarray(inp['mem'], f32)[bb])
        val = np.zeros((128, 3), f32)
        for k in range(3):
            src = j - 1 - k
            if src >= 0:
                m[f'xprev{k}'] = np.array(x[bb, src * TOK:(src + 1) * TOK, :], dtype=f32, order='C', copy=True)
                val[:, k] = 1.0
            else:
                m[f'xprev{k}'] = np.array(x[bb, j * TOK:(j + 1) * TOK, :], dtype=f32, order='C', copy=True)
        m['valid'] = val
        maps.append(m)
    return maps


_CACHE = {}


def kernel(**inputs):
    if 'nc' not in _CACHE:
        _CACHE['nc'] = build_program()[0]
    nc = _CACHE['nc']
    maps = host_inputs(inputs)
    res = run_bass_kernel_spmd(nc, maps, core_ids=list(range(NCORES)))
    out = np.empty((2, 8192, D), np.float32)
    for c in range(NCORES):
        bb, j = divmod(c, 4)
        out[bb, j * TOK:(j + 1) * TOK, :] = res.results[c]['out']
    return out
```

```python
import os
import math
import numpy as np
from contextlib import ExitStack
import concourse.bass as bass
import concourse.mybir as mybir
from concourse.bass_utils import run_bass_kernel_spmd

F32 = mybir.dt.float32
BF16 = mybir.dt.bfloat16
ALU = mybir.AluOpType
AF = mybir.ActivationFunctionType
AX = mybir.AxisListType

NCORES = 8
TOK = 2048
D = 1024
NT = TOK // 128
LN_EPS = 1e-5
DN_ALPHA = 2.0 ** 0.25
PI = math.pi


class Sched:
    EPOCH = 8000
    ENG = ('pe', 'act', 'dve', 'pool', 'sp')

    def __init__(self):
        self.q = {e: [] for e in self.ENG}
        self.cnt = {e: 0 for e in self.ENG}
        self.waited = {}
        self.lastw = {}
        self.readers = {}
        self.dmacnt = {}
        self.semkeys = []
        self._semset = set()
        self.targets = {e: set() for e in self.ENG}

    def _sem(self, k):
        if k not in self._semset:
            self._semset.add(k)
            self.semkeys.append(k)

    def _filter(self, eng, need):
        waits = []
        for s, v in need.items():
            if eng == 'pe' and s == ('eng', 'pe'):
                continue
            if self.waited.get((eng, s), -1) >= v:
                continue
            self.waited[(eng, s)] = v
            waits.append((s, v))
            if s[0] == 'eng':
                self.targets[s[1]].add(v)
        return waits

    def _deps(self, eng, reads, writes):
        need = {}

        def add(ev):
            s, v = ev
            if need.get(s, -1) < v:
                need[s] = v
        for k in reads:
            if k in self.lastw:
                add(self.lastw[k])
        for k in writes:
            if k in self.lastw:
                add(self.lastw[k])
            for ev in self.readers.get(k, ()):
                add(ev)
        return self._filter(eng, need)

    def _register(self, ev, reads, writes):
        for k in reads:
            self.readers.setdefault(k, []).append(ev)
        for k in writes:
            self.lastw[k] = ev
            self.readers[k] = []

    def op(self, eng, fn, r=(), w=()):
        waits = self._deps(eng, r, w)
        idx = self.cnt[eng]
        self.cnt[eng] += 1
        ev = (('eng', eng), idx)
        self._register(ev, r, w)
        self.q[eng].append([waits, fn, 'op', idx])

    def dma(self, qeng, fn, r=(), w=(), sem=None, inc=16, kind='dma'):
        waits = self._deps(qeng, r, w)
        s = (kind, sem)
        self._sem(s)
        self.dmacnt[s] = self.dmacnt.get(s, 0) + inc
        ev = (s, self.dmacnt[s])
        self._register(ev, r, w)
        self.q[qeng].append([waits, fn, 'dma', (s, inc)])

    def fence(self):
        latest = {}
        for e in self.ENG:
            if self.cnt[e] > 0:
                latest[('eng', e)] = self.cnt[e] - 1
        for s_, v in self.dmacnt.items():
            latest[s_] = v
        for e in self.ENG:
            waits = self._filter(e, dict(latest))
            if waits:
                self.q[e].append([waits, None, 'nop', None])
        self.lastw = {}
        self.readers = {}

    def final_waits(self, qeng, keys):
        waits = self._deps(qeng, keys, keys)
        self.q[qeng].append([waits, None, 'nop', None])

    def finalize(self):
        self.rank = {}
        for e in self.ENG:
            self.rank[e] = {idx: r for r, idx in enumerate(sorted(self.targets[e]))}
            n = len(self.rank[e])
            for ep in range((n + self.EPOCH - 1) // self.EPOCH):
                self._sem(('eng', e, ep))

    def _semval(self, e, idx):
        r = self.rank[e][idx]
        return ('eng', e, r // self.EPOCH), r % self.EPOCH + 1

    def replay(self, nc, block, sems):
        def run(engname):
            def f(eng):
                for waits, fn, kind, info in self.q[engname]:
                    for (ws, wv) in waits:
                        if ws[0] == 'eng':
                            k, v = self._semval(ws[1], wv)
                            eng.wait_ge(sems[k], v)
                        else:
                            eng.wait_ge(sems[ws], wv)
                    if fn is None:
                        continue
                    ins = fn(eng)
                    if kind == 'op':
                        if info in self.rank[engname]:
                            k, _ = self._semval(engname, info)
                            ins.then_inc(sems[k], 1)
                    else:
                        ins.then_inc(sems[info[0]], info[1])
            return f
        block.tensor(run('pe'))
        block.scalar(run('act'))
        block.vector(run('dve'))
        block.gpsimd(run('pool'))
        block.sync(run('sp'))


class B:
    def __init__(self, nc, sched):
        self.nc = nc
        self.s = sched

    def dma(self, q, out, in_, r=(), w=(), sem=None):
        self.s.dma(q, lambda e: e.dma_start(out=out, in_=in_), r, w, sem)

    def mm(self, out, lhsT, rhs, start, stop, r=(), w=(), **kw):
        self.s.op('pe', lambda e: e.matmul(out, lhsT, rhs, start=start, stop=stop, **kw), r, w)

    def tr(self, out, in_, ident, r=(), w=()):
        self.s.op('pe', lambda e: e.transpose(out, in_, ident), r, w)

    def act(self, out, in_, func, bias=0.0, scale=1.0, r=(), w=(), accum_out=None):
        if accum_out is None:
            self.s.op('act', lambda e: e.activation(out=out, in_=in_, func=func, bias=bias, scale=scale), r, w)
        else:
            self.s.op('act', lambda e: e.activation(out=out, in_=in_, func=func, bias=bias, scale=scale,
                                                    accum_out=accum_out), r, w)

    def tt(self, eng, out, in0, in1, op, r=(), w=()):
        self.s.op(eng, lambda e: e.tensor_tensor(out=out, in0=in0, in1=in1, op=op), r, w)

    def ts(self, eng, out, in0, s1, s2, op0, op1=None, r=(), w=()):
        if op1 is None:
            self.s.op(eng, lambda e: e.tensor_scalar(out=out, in0=in0, scalar1=s1, scalar2=None, op0=op0), r, w)
        else:
            self.s.op(eng, lambda e: e.tensor_scalar(out=out, in0=in0, scalar1=s1, scalar2=s2, op0=op0, op1=op1), r, w)

    def stt(self, out, in0, scalar, in1, op0, op1, r=(), w=()):
        self.s.op('dve', lambda e: e.scalar_tensor_tensor(out=out, in0=in0, scalar=scalar, in1=in1, op0=op0, op1=op1), r, w)

    def copy(self, eng, out, in_, r=(), w=()):
        if eng == 'act':
            self.s.op('act', lambda e: e.copy(out=out, in_=in_), r, w)
        else:
            self.s.op(eng, lambda e: e.tensor_copy(out=out, in_=in_), r, w)

    def memset(self, eng, ap, val, w=()):
        self.s.op(eng, lambda e: e.memset(ap, val), (), w)

    def scan(self, out, d0, d1, init, op0, op1, r=(), w=()):
        self.s.op('dve', lambda e: e.tensor_tensor_scan(out=out, data0=d0, data1=d1, initial=init, op0=op0, op1=op1), r, w)

    def bn_stats(self, out, in_, r=(), w=()):
        self.s.op('dve', lambda e: e.bn_stats(out=out, in_=in_), r, w)

    def bn_aggr(self, out, in_, r=(), w=()):
        self.s.op('dve', lambda e: e.bn_aggr(out=out, in_=in_), r, w)

    def reduce(self, out, in_, op, r=(), w=()):
        self.s.op('dve', lambda e: e.tensor_reduce(out=out, in_=in_, axis=AX.X, op=op), r, w)

    def cc_allreduce(self, out, in_, r=(), w=()):
        groups = [list(range(NCORES))]
        n = sum(1 for k in self.s.semkeys if k[0] == 'cc')
        self.s.dma('pool', lambda e: e.collective_compute("AllReduce", ALU.add, replica_groups=groups,
                                                          ins=[in_], outs=[out]), r, w, sem=n, inc=1, kind='cc')

    def recip(self, out, in_, r=(), w=()):
        self.s.op('dve', lambda e: e.reciprocal(out=out, in_=in_), r, w)


I32 = mybir.dt.int32
TWO_PI = 2.0 * math.pi


def _prod(xs):
    r = 1
    for v in xs:
        r *= v
    return r


class Arena:
    def __init__(self, nc, es, nbytes):
        self.t = es.enter_context(nc.sbuf_tensor("arena", [128, nbytes // 4], F32))
        self.h = {F32: self.t, BF16: self.t.bitcast(BF16), I32: self.t.bitcast(I32)}
        self.off = 0
        self.cap = nbytes
        self.peak = 0

    def mark(self):
        return self.off

    def release(self, m):
        self.off = m

    def alloc(self, shape, dt=F32):
        sz = 2 if dt == BF16 else 4
        n = _prod(shape[1:])
        nbytes = (n * sz + 31) // 32 * 32
        assert self.off + nbytes <= self.cap, f"arena overflow: {self.off}+{nbytes}>{self.cap}"
        lo = self.off // sz
        ap = self.h[dt][:, lo:lo + n]
        self.off += nbytes
        self.peak = max(self.peak, self.off)
        if len(shape) > 2:
            names = 'abcdef'[:len(shape) - 1]
            pat = "p (" + " ".join(names) + ") -> p " + " ".join(names)
            ap = ap.rearrange(pat, **{k: v for k, v in zip(names, shape[1:])})
        if shape[0] != 128:
            ap = ap[0:shape[0]]
        return ap


def bc(ap, shape, axis):
    return ap.unsqueeze(axis).to_broadcast(list(shape))


class SubArena:
    def __init__(self, A, start, nbytes):
        self.A = A
        self.start = start
        self.cap = nbytes
        self.off = 0

    def reset(self):
        self.off = 0

    def alloc(self, shape, dt=F32):
        save_off, save_cap, save_peak = self.A.off, self.A.cap, self.A.peak
        self.A.off = self.start + self.off
        self.A.cap = self.start + self.cap
        ap = self.A.alloc(shape, dt)
        self.off = self.A.off - self.start
        self.A.off, self.A.cap, self.A.peak = save_off, save_cap, save_peak
        return ap
class T:
    def __init__(self, ap, keys):
        self.ap = ap
        self.k = [keys] if isinstance(keys, str) else list(keys)

    def __getitem__(self, idx):
        return T(self.ap[idx], self.k)

    def v(self, fn):
        return T(fn(self.ap), self.k)

    def key(self, keys):
        return T(self.ap, keys)


def _k(x):
    return x.k if isinstance(x, T) else []


def _a(x):
    return x.ap if isinstance(x, T) else x


class Ops:
    def __init__(self, b):
        self.b = b

    def dma(self, q, out, in_, sem):
        if sem in ('c', 'dbg'):
            self._uniq = getattr(self, '_uniq', 0) + 1
            sem = f'{sem}{self._uniq}'
        self.b.dma(q, _a(out), _a(in_), r=_k(in_), w=_k(out), sem=sem)

    def mm(self, out, lhsT, rhs, start, stop, **kw):
        self.b.mm(_a(out), _a(lhsT), _a(rhs), start, stop, r=_k(lhsT) + _k(rhs), w=_k(out), **kw)

    def tr(self, out, in_, ident):
        self.b.tr(_a(out), _a(in_), _a(ident), r=_k(in_) + _k(ident), w=_k(out))

    def act(self, out, in_, func, bias=0.0, scale=1.0, accum_out=None):
        self.b.act(_a(out), _a(in_), func, bias=_a(bias), scale=_a(scale), r=_k(in_) + _k(bias) + _k(scale),
                   w=_k(out) + _k(accum_out), accum_out=_a(accum_out) if accum_out is not None else None)

    def tt(self, eng, out, a, c, op):
        self.b.tt(eng, _a(out), _a(a), _a(c), op, r=_k(a) + _k(c), w=_k(out))

    def ts(self, eng, out, a, s1, s2, op0, op1=None):
        self.b.ts(eng, _a(out), _a(a), _a(s1), _a(s2), op0, op1, r=_k(a) + _k(s1) + _k(s2), w=_k(out))

    def stt(self, out, a, scalar, c, op0, op1):
        self.b.stt(_a(out), _a(a), _a(scalar), _a(c), op0, op1, r=_k(a) + _k(scalar) + _k(c), w=_k(out))

    def cp(self, eng, out, a):
        self.b.copy(eng, _a(out), _a(a), r=_k(a), w=_k(out))

    def memset(self, eng, out, val):
        self.b.memset(eng, _a(out), val, w=_k(out))

    def scan(self, out, d0, d1, init, op0=None, op1=None):
        self.b.scan(_a(out), _a(d0), _a(d1), _a(init), op0 or ALU.mult, op1 or ALU.add,
                    r=_k(d0) + _k(d1) + _k(init), w=_k(out))

    def reduce(self, out, in_, op):
        self.b.reduce(_a(out), _a(in_), op, r=_k(in_), w=_k(out))

    def recip(self, out, in_):
        self.b.recip(_a(out), _a(in_), r=_k(in_), w=_k(out))

    def bn_stats(self, out, in_):
        self.b.bn_stats(_a(out), _a(in_), r=_k(in_), w=_k(out))

    def bn_aggr(self, out, in_):
        self.b.bn_aggr(_a(out), _a(in_), r=_k(in_), w=_k(out))

    def allreduce(self, out, in_):
        self.b.cc_allreduce(_a(out), _a(in_), r=_k(in_), w=_k(out))
def build_program(debug=()):
    nc = bass.Bass("TRN2", target_bir_lowering=False)
    es = ExitStack()
    S = Sched()
    b = B(nc, S)
    o = Ops(b)
    A = Arena(nc, es, 206 * 1024)

    def alloc(name, shape, dt=F32):
        return T(A.alloc(shape, dt), name)

    def din(name, shape, dt=F32):
        return T(nc.dram_tensor(name, list(shape), dt, kind="ExternalInput").ap(), 'd_' + name)

    pb = [T(es.enter_context(nc.psum_tensor(f"pb{i}", [128, 512], F32))[:], f'pb{i}') for i in range(8)]
    dbg_out = []
    dumpables = {}

    def dump(name, t, shape, dt):
        od = T(nc.dram_tensor("dbg_" + name, list(shape), dt, kind="ExternalOutput").ap(), 'dbgo_' + name)
        dbg_out.append(name)
        o.dma('sp', od, t, sem='dbg')

    mult, add, sub = ALU.mult, ALU.add, ALU.subtract

    x_d = din("x", [TOK, D])
    xprev_d = [din(f"xprev{k}", [TOK, D]) for k in range(3)]
    valid_d = din("valid", [128, 3])
    out_d = T(nc.dram_tensor("out", [TOK, D], F32, kind="ExternalOutput").ap(), 'd_out')
    ident_d = din("ident", [128, 128])
    lng0_d = din("lng0", [128, 8])
    lnb0_d = din("lnb0", [128, 8])
    w_in_d = din("w_in", [128, 8, 2064])
    s5d = {n: din(n, sh) for n, sh in [
        ("s5_lam_re", [128, 4, 64]), ("s5_lam_im", [128, 4, 64]), ("s5_lst", [128, 4]),
        ("s5_bre", [128, 4, 64]), ("s5_bim", [128, 4, 64]), ("s5_cre", [128, 4, 64]), ("s5_cim", [128, 4, 64]),
        ("s5_d", [128, 4]), ("s5_maske", [128, 2]), ("s5p_lam_re", [128, 16]), ("s5p_lam_im", [128, 16]),
        ("s5p_lst", [128, 16]), ("kvec", [128, 129]),
        ("w_glu", [128, 4, 512]), ("b_glu", [128, 4])]}

    ident = alloc('ident', [128, 128])
    lng0 = alloc('lng0', [128, 8])
    lnb0 = alloc('lnb0', [128, 8])
    epsc = alloc('epsc', [128, 1])
    zrow = alloc('zrow', [128, 128], BF16)
    hT = alloc('hT', [128, 8, TOK], BF16)
    ymix_off = A.off
    ymix = alloc('ymix', [128, 8, TOK], BF16)
    OV = SubArena(A, ymix_off, 32 * 1024)

    def oalloc(name, shape, dt=F32):
        return T(OV.alloc(shape, dt), name)
    hTk = lambda kt, g: f'hT{kt}_{g}'

    o.dma('sp', ident, ident_d, sem='c')
    o.dma('sp', lng0, lng0_d, sem='c')
    o.dma('sp', lnb0, lnb0_d, sem='c')
    o.memset('dve', epsc, LN_EPS)
    o.memset('dve', zrow, 0.0)

    def finish():
        for nm_, (t_, shp_, dt_) in dumpables.items():
            if nm_ in debug:
                dump(nm_, t_, shp_, dt_)
        S.fence()
        S.final_waits('sp', ['dbgo_' + n for n in dbg_out])
        print("arena peak bytes", A.peak, "instr counts", S.cnt)
        S.finalize()
        sems = {}
        for k in S.semkeys:
            sems[k] = es.enter_context(nc.semaphore("s_" + "_".join(str(t_) for t_ in k)))
        with nc.Block() as block:
            S.replay(nc, block, sems)
        es.close()
        return nc, dbg_out

    mMix = A.mark()
    uT = alloc('uT', [128, 4, 16, 128], BF16)
    W = alloc('W', [128, 4, 16, 2, 128], BF16)
    dumpables['uT'] = (uT.key([f'uT{m}' for m in range(4)]), [128, 4, 16, 128], BF16)
    dumpables['W'] = (W.key([f'W{k}_{r}' for k in range(16) for r in range(2)]), [128, 4, 16, 2, 128], BF16)
    dumpables['ymix'] = (T(ymix.ap[:, 0:4, :], [f'ymix{a}_{c}' for a in range(4) for c in range(4)]), [128, 4, TOK], BF16)
    dumpables['hT'] = (hT.key([hTk(kt, g) for kt in range(8) for g in range(4)]), [128, 8, TOK], BF16)
    prm = {}
    prm["s5_d"] = alloc("s5_d", [128, 4])
    prm["s5_maske"] = alloc("s5_maske", [128, 2])
    valid = alloc("valid", [128, 3])
    o.dma('sp', valid, valid_d, sem='c')
    sh = [128, 4, 64]
    sh2 = [128, 4, 2, 64]
    shp = [128, 16]
    are, aim = alloc('are', sh), alloc('aim', sh)
    Bm_re, Bm_im = alloc('Bm_re', sh2), alloc('Bm_im', sh2)
    Cm_re, Cm_im = alloc('Cm_re', sh2), alloc('Cm_im', sh2)
    t1, t2, t3, t4 = (alloc(f't{i}', sh2) for i in range(4))
    s1, s2 = alloc('s1', sh), alloc('s2', sh)
    cur = [(alloc('cur_re0', sh), alloc('cur_im0', sh)), (alloc('cur_re1', sh), alloc('cur_im1', sh))]
    Us, Uc = alloc('Us', [128, 16, 129]), alloc('Uc', [128, 16, 129])
    rho, rho128 = alloc('rho', shp), alloc('rho128', shp)
    a128re, a128im = alloc('a128re', shp), alloc('a128im', shp)
    wglu = alloc('wglu', [128, 4, 512], BF16)
    bglu = alloc('bglu', [128, 4])
    o.dma('pool', wglu, s5d['w_glu'], sem='wglu')
    o.dma('sp', bglu, s5d['b_glu'], sem='c')
    win_u = alloc('win_u', [128, 8, 512], BF16)
    o.dma('pool', win_u, w_in_d[:, :, 0:512], sem='win_u')
    mP1 = A.mark()
    for n in ("s5_lam_re", "s5_lam_im", "s5_bre", "s5_bim", "s5_cre", "s5_cim"):
        prm[n] = alloc(n, [128, 4, 64])
    prm["s5_lst"] = alloc("s5_lst", [128, 4])
    for n in ("s5p_lam_re", "s5p_lam_im", "s5p_lst"):
        prm[n] = alloc(n, [128, 16])
    prm["kvec"] = alloc("kvec", [128, 129])
    for n in prm:
        o.dma('sp', prm[n], s5d[n], sem='c')

    def sincos(x, shape, nm, out_s, out_c, kI=None, kf=None):
        if kI is None:
            kI = alloc(nm + '_ki', shape, I32)
            kf = alloc(nm + '_kf', shape)
        for off, r in ((0.0, out_s), (math.pi / 2, out_c)):
            o.ts('dve', kI, x, 1.0 / TWO_PI, off / TWO_PI, mult, add)
            o.cp('dve', kf, kI)
            o.stt(r, kf, -TWO_PI, x, mult, add)
            o.ts('dve', r, r, off, math.pi, add, ALU.min)
            o.ts('dve', r, r, -math.pi, None, ALU.max)
            o.act(r, r, AF.Sin)

    lam_re, lam_im = prm["s5_lam_re"], prm["s5_lam_im"]
    step = alloc('step', [128, 4])
    o.act(step, prm["s5_lst"], AF.Exp)
    stepb = step.v(lambda a: bc(a, sh, 2))
    zre = alloc('zre', sh)
    zim = alloc('zim', sh)
    o.tt('dve', zre, lam_re, stepb, mult)
    o.tt('dve', zim, lam_im, stepb, mult)
    mag = alloc('mag', sh)
    o.act(mag, zre, AF.Exp)
    sinz, cosz = alloc('sinz', sh), alloc('cosz', sh)
    sincos(zim, sh, 'z', sinz, cosz)
    o.tt('dve', are, mag, cosz, mult)
    o.tt('dve', aim, mag, sinz, mult)
    den = alloc('den', sh)
    tA = alloc('tA', sh)
    tB = alloc('tB', sh)
    o.tt('dve', den, lam_re, lam_re, mult)
    o.tt('dve', tA, lam_im, lam_im, mult)
    o.tt('dve', den, den, tA, add)
    o.recip(den, den)
    nre = alloc('nre', sh)
    o.ts('dve', nre, are, -1.0, None, add)
    fre = alloc('fre', sh)
    fim = alloc('fim', sh)
    o.tt('dve', tA, nre, lam_re, mult)
    o.tt('dve', tB, aim, lam_im, mult)
    o.tt('dve', tA, tA, tB, add)
    o.tt('dve', fre, tA, den, mult)
    o.tt('dve', tA, aim, lam_re, mult)
    o.tt('dve', tB, nre, lam_im, mult)
    o.tt('dve', tA, tA, tB, sub)
    o.tt('dve', fim, tA, den, mult)
    bbre = alloc('bbre', sh)
    bbim = alloc('bbim', sh)
    Bre, Bim, Cre, Cim = prm["s5_bre"], prm["s5_bim"], prm["s5_cre"], prm["s5_cim"]
    o.tt('dve', tA, fre, Bre, mult)
    o.tt('dve', tB, fim, Bim, mult)
    o.tt('dve', bbre, tA, tB, sub)
    o.tt('dve', tA, fre, Bim, mult)
    o.tt('dve', tB, fim, Bre, mult)
    o.tt('dve', bbim, tA, tB, add)
    maske = prm["s5_maske"]
    for e in range(2):
        me = maske[:, e:e + 1]
        o.ts('dve', Bm_re[:, :, e, :], bbre, me, None, mult)
        o.ts('dve', Bm_im[:, :, e, :], bbim, me, None, mult)
        o.ts('dve', Cm_re[:, :, e, :], Cre, me, None, mult)
        o.ts('dve', Cm_im[:, :, e, :], Cim, me, None, mult)

    def bce(t):
        return t.v(lambda a: bc(a, sh2, 2))

    def cur_step(pe_, j):
        cr, ci = cur[j % 2]
        nr, ni = cur[(j + 1) % 2]
        o.tt(pe_, s1, cr, are, mult)
        o.tt(pe_, s2, ci, aim, mult)
        o.tt(pe_, nr, s1, s2, sub)
        o.tt(pe_, s1, cr, aim, mult)
        o.tt(pe_, s2, ci, are, mult)
        o.tt(pe_, ni, s1, s2, add)

    W6 = W.v(lambda a: a.rearrange("p m k r (e q) -> p m k r e q", e=2))

    def Wk(kap, ri):
        return T(W6.ap[:, :, kap, ri, :, :], f'W{kap}_{ri}')

    if 'stopP0' in debug:
        return finish()
    o.memset('pool', cur[0][0], 1.0)
    o.memset('pool', cur[0][1], 0.0)
    for j in range(16):
        kap = 15 - j
        cr, ci = cur[j % 2]
        o.tt('pool', t1, bce(cr), Bm_re, mult)
        o.tt('pool', t2, bce(ci), Bm_im, mult)
        o.tt('pool', Wk(kap, 0), t1, t2, sub)
        o.tt('pool', t3, bce(cr), Bm_im, mult)
        o.tt('pool', t4, bce(ci), Bm_re, mult)
        o.tt('pool', Wk(kap, 1), t3, t4, add)
        if j < 15:
            cur_step('pool', j)

    if 'stopP1' in debug:
        return finish()
    stepP = alloc('stepP', shp)
    o.act(stepP, prm["s5p_lst"], AF.Exp)
    zreP = alloc('zreP', shp)
    o.tt('dve', zreP, prm["s5p_lam_re"], stepP, mult)
    o.act(rho, zreP, AF.Exp, scale=16.0)
    o.act(rho128, zreP, AF.Exp, scale=2048.0)
    phi = alloc('phi', shp)
    o.tt('dve', phi, prm["s5p_lam_im"], stepP, mult)
    phk = alloc('phk', shp, I32)
    phf = alloc('phf', shp)
    o.ts('dve', phk, phi, 16.0 / TWO_PI, None, mult)
    o.cp('dve', phf, phk)
    o.ts('dve', phi, phi, 16.0, None, mult)
    o.stt(phi, phf, -TWO_PI, phi, mult, add)
    shU = [128, 8, 129]
    argk = alloc('argk', shU)
    ukI, ukf = alloc('ukI', shU, I32), alloc('ukf', shU)
    for hh in range(2):
        qs = slice(8 * hh, 8 * hh + 8)
        o.tt('dve', argk, phi[:, qs].v(lambda a: bc(a, shU, 2)), prm["kvec"].v(lambda a: bc(a, shU, 1)), mult)
        sincos(argk, shU, f'U{hh}', Us[:, qs, :], Uc[:, qs, :], ukI, ukf)
    o.tt('dve', a128re, rho128, Uc[:, :, 128], mult)
    o.tt('dve', a128im, rho128, Us[:, :, 128], mult)
    if 'stopP2' in debug:
        return finish()
    S.fence()
    A.release(mP1)
    Eend = alloc('Eend', [128, 16, 2])
    prev = [alloc(f'prev{k}', [128, 16, 2]) for k in range(3)]
    Sin_ = alloc('Sin', [128, 16, 2])
    Sn = alloc('Sn', [128, 16, 2])
    c1, c2 = alloc('c1', [128, 16]), alloc('c2', [128, 16])
    Kblk = alloc('Kblk', [128, 4, 16, 128], BF16)
    mSeg = A.mark()
    st = alloc('st', [128, NT, 12])
    mv = alloc('mv', [128, NT, 2])
    sd = alloc('sd', [128, NT])
    rstd = alloc('rstd', [128, NT])
    nmr = alloc('nmr', [128, NT])
    hTg = alloc('hTg', [128, 8, 512], BF16)
    for k_ in range(3):
        dumpables[f'prev{k_}'] = (prev[k_], [128, 16, 2], F32)
    dumpables['Sin'] = (Sin_, [128, 16, 2], F32)
    dumpables['Eend'] = (Eend, [128, 16, 2], F32)
    m1, m2, m3, m4 = (T(t_.ap.rearrange("p m e q -> p (m e q)").rearrange("p (a c) -> p a c", a=4), t_.k)
                      for t_ in (t1, t2, t3, t4))

    def ln_stats_tile(i, src):
        for hh in range(2):
            o.bn_stats(st[:, i, hh * 6:(hh + 1) * 6].key(f'st{i}'), src[:, hh * 512:(hh + 1) * 512])
        o.bn_aggr(mv[:, i, :].key(f'mv{i}'), st[:, i, :].key(f'st{i}'))

    def ln_stats_group(g):
        gs = slice(4 * g, 4 * g + 4)
        mvg = mv[:, gs, :].key([f'mv{4 * g + j}' for j in range(4)])
        o.act(sd[:, gs].key(f'sd{g}'), mvg[:, :, 1], AF.Sqrt, bias=epsc[:, 0:1], scale=1.0)
        o.recip(rstd[:, gs].key(f'rstd{g}'), sd[:, gs].key(f'sd{g}'))
        o.stt(nmr[:, gs].key(f'nmr{g}'), mvg[:, :, 0], -1.0, rstd[:, gs].key(f'rstd{g}'), mult, mult)

    def transposes_to(dst_fn, src_tiles, gT_, bT_):
        for kt in range(8):
            bank = pb[kt % 2]
            for j in range(4):
                o.tr(bank[:, j * 128:(j + 1) * 128], src_tiles[j][:, kt * 128:(kt + 1) * 128], ident)
            dst = dst_fn(kt)
            if kt % 2 == 0:
                o.act(dst, bank, AF.Identity, bias=bT_[:, kt:kt + 1], scale=gT_[:, kt:kt + 1])
            else:
                o.ts('dve', dst, bank, gT_[:, kt:kt + 1], bT_[:, kt:kt + 1], mult, add)

    def s5_segment(xsrc, own, E_dst, uT_dst, ukey):
        S.fence()
        OV.reset()
        xin = [oalloc(f'xin{i}', [128, D]) for i in range(4)]
        xng = oalloc('xng', [128, 4, D])
        for g in range(4):
            for j in range(4):
                i = 4 * g + j
                o.dma('sp', xin[j], xsrc[i * 128:(i + 1) * 128, :], sem=f'xin{j}')
                ln_stats_tile(i, xin[j])
            ln_stats_group(g)
            tiles = []
            for j in range(4):
                i = 4 * g + j
                dst = xng[:, j, :].key(f'xng{j}')
                o.act(dst, xin[j], AF.Identity, bias=nmr[:, i:i + 1].key(f'nmr{g}'), scale=rstd[:, i:i + 1].key(f'rstd{g}'))
                tiles.append(dst)
            if own:
                transposes_to(lambda kt: hT[:, kt, g * 512:(g + 1) * 512].key(hTk(kt, g)), tiles, lng0, lnb0)
                src_fn = lambda kt: hT[:, kt, g * 512:(g + 1) * 512].key(hTk(kt, g))
            else:
                transposes_to(lambda kt: hTg[:, kt, :].key(f'hTg{kt}'), tiles, lng0, lnb0)
                src_fn = lambda kt: hTg[:, kt, :].key(f'hTg{kt}')
            for m in range(4):
                bank = pb[2 + m % 2]
                for kt in range(8):
                    o.mm(bank, win_u[:, kt, m * 128:(m + 1) * 128], src_fn(kt), start=(kt == 0), stop=(kt == 7))
                o.cp('act', uT_dst[:, m, :, 32 * g:32 * g + 32].key(f'{ukey}{m}'),
                     bank.v(lambda a: a.rearrange("p (c t) -> p t c", t=16)))
        S.fence()
        OV.reset()
        Xp_ = oalloc('Xp', [128, 16, 2, 128])
        Vb_ = oalloc('Vb', [128, 16, 2, 128])
        if 'drain' in debug:
            S.op('pe', lambda e: e.drain(), (), ())
        for q in range(16):
            m, qq = divmod(q, 4)
            bank = pb[4 + qq]
            col0 = (m % 2) * 256
            rows = slice(32 * qq, 32 * qq + 32)
            for ri in range(2):
                for kap in range(16):
                    wk = [f'W{kap}_{ri}'] + (['xc_serial'] if ('serial' in debug and ri == 0 and kap == 0) else [])
                    o.mm(bank[:, col0 + ri * 128: col0 + (ri + 1) * 128],
                         T(W.ap[rows, m, kap, ri, :], wk),
                         uT_dst[rows, m, kap, :].key(f'{ukey}{m}'),
                         start=(kap == 0), stop=(kap == 15), tile_position=(32 * qq, 0))
            Xre, Xim = bank[:, col0:col0 + 128], bank[:, col0 + 128:col0 + 256]
            Uc1, Us1 = Uc[:, q, 1:129], Us[:, q, 1:129]
            a1, a2, a3, a4 = m1[:, qq, :], m2[:, qq, :], m3[:, qq, :], m4[:, qq, :]
            o.tt('dve', a1, Xre, Uc1, mult)
            o.tt('dve', a2, Xim, Us1, mult)
            o.tt('dve', Xp_[:, q, 0, :].key(f'Xp{q}'), a1, a2, add)
            o.tt('dve', a3, Xim, Uc1, mult)
            o.tt('dve', a4, Xre, Us1, mult)
            o.tt('dve', Xp_[:, q, 1, :].key([f'Xp{q}'] + (['xc_serial'] if 'serial' in debug else [])), a3, a4, sub)
            for ri in range(2):
                o.scan(Vb_[:, q, ri, :].key(f'Vb{q}'), rho[:, q:q + 1].v(lambda a: a.to_broadcast([128, 128])),
                       Xp_[:, q, ri, :].key(f'Xp{q}'), 0.0)
        if 'drain' in debug:
            S.op('pe', lambda e: e.drain(), (), ())
        Vall = Vb_.key([f'Vb{q}' for q in range(16)])
        Vre, Vim = Vall[:, :, 0, 127], Vall[:, :, 1, 127]
        Uc128, Us128 = Uc[:, :, 128], Us[:, :, 128]
        o.tt('dve', c1, Uc128, Vre, mult)
        o.tt('dve', c2, Us128, Vim, mult)
        o.tt('dve', E_dst[:, :, 0], c1, c2, sub)
        o.tt('dve', c1, Us128, Vre, mult)
        o.tt('dve', c2, Uc128, Vim, mult)
        o.tt('dve', E_dst[:, :, 1], c1, c2, add)
        return Xp_, Vb_

    uTp = T(Kblk.ap.rearrange("p m j c -> p (m j c)").rearrange("p (m t c) -> p m t c", m=4, t=16), 'uTp')
    for k in range(3):
        Xp_k, Vb_k = s5_segment(xprev_d[k], False, prev[k], uTp, 'uTp')
        pk = prev[k].v(lambda a: a.rearrange("p q r -> p (q r)"))
        o.ts('dve', pk, pk, valid[:, k:k + 1], None, mult)
        if k == 0 and 'stopK0' in debug:
            dumpables['uTp'] = (uTp.key([f'uTp{m}' for m in range(4)]), [128, 4, 16, 128], BF16)
            dumpables['XpK'] = (Xp_k.key([f'Xp{q}' for q in range(16)]), [128, 16, 2, 128], F32)
            dumpables['VbK'] = (Vb_k.key([f'Vb{q}' for q in range(16)]), [128, 16, 2, 128], F32)
            return finish()
    Xp, Vb = s5_segment(x_d, True, Eend, uT, 'uT')
    if 'stopB' in debug:
        return finish()
    S.fence()
    A.release(mSeg)
    E0 = alloc('E0', [128, 4, 2, 128], BF16)
    BTb = alloc('BTb', [128, 4, 2, 128], BF16)
    Ef = [alloc(f'Ef{i}', [128, 2, 4, 128]) for i in range(2)]
    Sprev = alloc('Sprev', [128, 16, 2, 128], BF16)
    sgt = [alloc(f'sgt{i}', [128, 512], BF16) for i in range(2)]
    dumpables['Sprev'] = (Sprev.key(['Sprev'] + [f'Sprev{m}' for m in range(4)]), [128, 16, 2, 128], BF16)
    dumpables['Kblk'] = (Kblk.key([f'Kblk{m}' for m in range(4)]), [128, 4, 16, 128], BF16)
    cursrc = prev[2]
    for k in (1, 0):
        o.tt('dve', c1, a128re, cursrc[:, :, 0], mult)
        o.tt('dve', c2, a128im, cursrc[:, :, 1], mult)
        o.tt('dve', c1, c1, c2, sub)
        dst = Sin_ if k == 0 else Sn
        o.tt('dve', dst[:, :, 0], c1, prev[k][:, :, 0], add)
        o.tt('dve', c1, a128re, cursrc[:, :, 1], mult)
        o.tt('dve', c2, a128im, cursrc[:, :, 0], mult)
        o.tt('dve', c1, c1, c2, add)
        o.tt('dve', dst[:, :, 1], c1, prev[k][:, :, 1], add)
        cursrc = dst
    S.fence()
    for q in range(16):
        for ri in range(2):
            o.scan(Vb[:, q, ri, :].key(f'Vb{q}'), rho[:, q:q + 1].v(lambda a: a.to_broadcast([128, 128])),
                   Xp[:, q, ri, :].key(f'Xp{q}'), Sin_[:, q, ri:ri + 1])
    S.fence()
    o.cp('dve', Sprev[:, :, 0, 0], Sin_[:, :, 0])
    o.ts('dve', Sprev[:, :, 1, 0], Sin_[:, :, 1], -1.0, None, mult)
    for m in range(4):
        qs = slice(4 * m, 4 * m + 4)
        Vm = Vb[:, qs, :, :].key([f'Vb{q}' for q in range(4 * m, 4 * m + 4)])
        Vr, Vi = Vm[:, :, 0, 0:127], Vm[:, :, 1, 0:127]
        Uc0, Us0 = Uc[:, qs, 1:128], Us[:, qs, 1:128]
        a1, a2, a3, a4 = m1[:, :, 0:127], m2[:, :, 0:127], m3[:, :, 0:127], m4[:, :, 0:127]
        o.tt('dve', a1, Uc0, Vr, mult)
        o.tt('dve', a2, Us0, Vi, mult)
        o.tt('dve', Sprev[:, qs, 0, 1:128].key(f'Sprev{m}'), a1, a2, sub)
        o.tt('dve', a3, Us0, Vr, mult)
        o.tt('dve', a4, Uc0, Vi, mult)
        o.stt(Sprev[:, qs, 1, 1:128].key(f'Sprev{m}'), a3, -1.0, a4, mult, sub)
    if 'stopC' in debug:
        return finish()
    SprevK = lambda m: ['Sprev', f'Sprev{m}']
    flat4 = lambda t_: t_.v(lambda a: a.rearrange("p m e q -> p m (e q)"))
    for ri, src, sc in (((0, Bm_re, 1.0), (1, Bm_im, -1.0)) if 'skipBT' not in debug else ()):
        bank = pb[6 + ri]
        for m in range(4):
            o.tr(bank[:, m * 128:(m + 1) * 128], flat4(src)[:, m, :], ident)
        o.act(BTb[:, :, ri, :], bank.v(lambda a: a.rearrange("p (m c) -> p m c", m=4)), AF.Identity, scale=sc)

    PE2 = 'dve'
    o.memset(PE2, cur[0][0], 1.0)
    o.memset(PE2, cur[0][1], 0.0)
    for j in (range(17) if 'skipEloop' not in debug else ()):
        cr, ci = cur[j % 2]
        Efj = Ef[j % 2]
        Ef6 = Efj.v(lambda a: a.rearrange("p r m (e q) -> p r m e q", e=2))
        o.tt(PE2, t1, bce(cr), Cm_re, mult)
        o.tt(PE2, t2, bce(ci), Cm_im, mult)
        o.tt(PE2, Ef6[:, 0], t1, t2, sub)
        o.tt(PE2, t3, bce(ci), Cm_re, mult)
        o.tt(PE2, t4, bce(cr), Cm_im, mult)
        o.tt(PE2, Ef6[:, 1], t3, t4, add)
        for ri in (range(2) if 'skipEtr' not in debug else ()):
            bank = pb[6 + ri]
            for m in range(4):
                o.tr(bank[:, m * 128:(m + 1) * 128], Efj[:, ri, m, :], ident)
            if j == 0:
                dst = E0[:, :, ri, :]
            else:
                dst = T(W.ap[:, :, j - 1, ri, :], f'W{j - 1}_{ri}')
            src = bank.v(lambda a: a.rearrange("p (m c) -> p m c", m=4))
            if ri == 0:
                o.cp('act', dst, src)
            else:
                o.cp('dve', dst, src)
        if j < 16:
            cur_step(PE2, j)

    if 'stopD0' in debug:
        return finish()
    for m in range(4):
        for j in range(16):
            t = j % 4
            bank = pb[4 + (m * 4 + j // 4) % 2]
            o.mm(bank[:, t * 128:(t + 1) * 128], zrow[0:1, 0:128], zrow[0:1, 0:128], start=True, stop=False)
            for qq in range(4):
                cols = slice(32 * qq, 32 * qq + 32)
                for ri in range(2):
                    if j == 0:
                        Es = E0[:, m, ri, cols]
                    else:
                        Es = T(W.ap[:, m, j - 1, ri, cols], f'W{j - 1}_{ri}')
                    o.mm(bank[32 * qq:32 * qq + 32, t * 128 + 32 * qq: t * 128 + 32 * qq + 32],
                         BTb[:, m, ri, cols], Es, start=False, stop=(ri == 1), tile_position=(0, 32 * qq))
            if t == 3:
                o.cp('act', Kblk[:, m, j - 3:j + 1, :].key(f'Kblk{m}'),
                     bank.v(lambda a: a.rearrange("p (t c) -> p t c", t=4)))
                if j == 3:
                    o.stt(Kblk[:, m, 0, :].key(f'Kblk{m}'), ident, prm["s5_d"][:, m:m + 1], bank[:, 0:128], mult, add)

    if 'stopD' in debug:
        return finish()

    def gT(m):
        return T(uT.ap[:, m].rearrange("p t c -> p (t c)"), f'uT{m}')

    for m in range(4):
        for bk in range(4):
            bank = pb[bk]
            first = True
            for j in range(0, 4 * bk + 4):
                tlo, thi = max(j, 4 * bk), 4 * bk + 3
                rhs = T(uT.ap[:, m, tlo - j:thi - j + 1, :].rearrange("p t c -> p (t c)"), f'uT{m}')
                o.mm(bank[:, (tlo - 4 * bk) * 128:512], Kblk[:, m, j, :].key(f'Kblk{m}'), rhs, start=first, stop=False)
                first = False
            for tau in range(4 * bk, 4 * bk + 4):
                for qq in range(4):
                    cols = slice(32 * qq, 32 * qq + 32)
                    for ri in range(2):
                        last = (tau == 4 * bk + 3 and ri == 1)
                        o.mm(bank[32 * qq:32 * qq + 32, (tau - 4 * bk) * 128:(tau - 4 * bk + 1) * 128],
                             T(W.ap[:, m, tau, ri, cols], f'W{tau}_{ri}'),
                             Sprev[:, 4 * m + qq, ri, :].key(SprevK(m)), start=False, stop=last,
                             tile_position=(0, 32 * qq))
        for bk in range(4):
            dst = gT(m).v(lambda a: a.rearrange("p (c t) -> p t c", t=16))[:, 4 * bk:4 * bk + 4, :]
            o.act(dst, pb[bk].v(lambda a: a.rearrange("p (t c) -> p t c", t=4)), AF.Gelu_apprx_tanh)

    if 'stopE' in debug:
        return finish()
    S.fence()
    for mo in range(4):
        for nb in range(4):
            idx = mo * 4 + nb
            bank = pb[4 + idx % 2]
            ns = slice(nb * 512, (nb + 1) * 512)
            for m in range(4):
                o.mm(bank, wglu[:, m, mo * 128:(mo + 1) * 128], gT(m)[:, ns], start=(m == 0), stop=(m == 3))
            sg = sgt[idx % 2]
            o.act(sg, bank, AF.Sigmoid, bias=bglu[:, mo:mo + 1])
            o.tt('dve', ymix[:, mo, ns].key(f'ymix{mo}_{nb}'), gT(mo)[:, ns], sg, mult)

    if 'nogla' in debug:
        return finish()
    S.fence()
    A.release(mMix)
    gd = {n: din(n, sh) for n, sh in [("gla_wgu", [16, 256]), ("gla_bg", [128, 2]), ("gla_ng", [128, 512]),
                                      ("tri", [128, 128]), ("rmask", [128, 512])]}
    valid = alloc('valid2', [128, 3])
    o.dma('sp', valid, valid_d, sem='c')
    st = alloc('st2', [128, NT, 12])
    mv = alloc('mv2', [128, NT, 2])
    sd = alloc('sd2', [128, NT])
    rstd = alloc('rstd2', [128, NT])
    nmr = alloc('nmr2', [128, NT])
    hTg = alloc('hTg2', [128, 8, 512], BF16)
    wk_ = alloc('wk', [128, 8, 256], BF16)
    wq_ = alloc('wq', [128, 8, 256], BF16)
    wv_ = alloc('wv', [128, 8, 512], BF16)
    wr_ = alloc('wr', [128, 8, 512], BF16)
    wg_ = alloc('wg', [128, 8, 16], BF16)
    o.dma('pool', wq_, w_in_d[:, :, 512:768], sem='wq')
    o.dma('pool', wk_, w_in_d[:, :, 768:1024], sem='wk')
    o.dma('pool', wv_, w_in_d[:, :, 1024:1536], sem='wv')
    o.dma('pool', wr_, w_in_d[:, :, 1536:2048], sem='wr')
    o.dma('pool', wg_, w_in_d[:, :, 2048:2064], sem='wg')
    wgu = alloc('wgu', [16, 256], BF16)
    o.dma('pool', wgu, gd['gla_wgu'], sem='wgu')
    bg = alloc('bg', [128, 2])
    nbg = alloc('nbg', [128, 2])
    gng = alloc('gng', [128, 512])
    tri = alloc('tri', [128, 128])
    rmask = alloc('rmask', [128, 512])
    onec = alloc('onec', [128, 1])
    o.dma('sp', bg, gd['gla_bg'], sem='c')
    o.dma('sp', gng, gd['gla_ng'], sem='c')
    o.dma('sp', tri, gd['tri'], sem='c')
    o.dma('sp', rmask, gd['rmask'], sem='c')
    o.ts('dve', nbg, bg, -1.0, None, mult)
    o.memset('dve', onec, 1.0)
    xin2 = [alloc(f'gxin{i}', [128, D]) for i in range(4)]
    xng2 = alloc('gxng', [128, 4, D])
    glr = alloc('glr', [16, 512], BF16)
    spl = alloc('spl', [128, 2, 512])
    cum = alloc('cum', [128, 2, 512])
    ekl = alloc('ekl', [128, 2, 512])
    eb = alloc('eb', [128, 2, 512])
    enb = alloc('enb', [128, 2, 512])
    klT = alloc('klT', [128, 2, 512])
    klt = alloc('klt', [128, 4, 256], BF16)
    vt = alloc('vt', [128, 4, 512], BF16)
    qeT = alloc('qeT', [128, 2, 512], BF16)
    keT = alloc('keT', [128, 2, 512], BF16)
    ncl = alloc('ncl', [128, 2, 4])
    dec = alloc('dec', [128, 2, 4])
    Sg_ = alloc('Sgla', [128, 2, 128])
    Sbf = alloc('Sbf', [128, 2, 128], BF16)
    scT = alloc('scT', [128, 4, 128], BF16)
    rsil = alloc('rsil', [128, 512])
    gnrs = alloc('gnrs', [128, 512])
    ysb = alloc('ysb', [128, 512])
    ssq = alloc('ssq', [128, 4])
    rs4 = alloc('rs4', [128, 4])
    junk = alloc('junk', [128, 128])
    dumpables['ygla'] = (T(ymix.ap[:, 4:8, :], [f'ymixg{c}' for c in range(NT)]), [128, 4, TOK], BF16)
    dumpables['Sgla'] = (Sg_, [128, 2, 128], F32)

    def gla_segment(xsrc, own):
        for g in range(4):
            if own:
                src_fn = lambda kt: hT[:, kt, g * 512:(g + 1) * 512].key(hTk(kt, g))
            else:
                for j in range(4):
                    i = 4 * g + j
                    o.dma('sp', xin2[j], xsrc[i * 128:(i + 1) * 128, :], sem=f'gxin{j}')
                    ln_stats_tile(i, xin2[j])
                ln_stats_group(g)
                tiles = []
                for j in range(4):
                    i = 4 * g + j
                    dst = xng2[:, j, :].key(f'gxng{j}')
                    o.act(dst, xin2[j], AF.Identity, bias=nmr[:, i:i + 1].key(f'nmr{g}'), scale=rstd[:, i:i + 1].key(f'rstd{g}'))
                    tiles.append(dst)
                transposes_to(lambda kt: hTg[:, kt, :].key(f'hTg{kt}'), tiles, lng0, lnb0)
                src_fn = lambda kt: hTg[:, kt, :].key(f'hTg{kt}')
            bank = pb[2]
            for kt in range(8):
                o.mm(bank[0:16, :], wg_[:, kt, :], src_fn(kt), start=(kt == 0), stop=(kt == 7))
            o.cp('act', glr, bank[0:16, :])
            for t in range(2):
                bk = pb[3]
                o.mm(bk, wgu[:, t * 128:(t + 1) * 128], glr, start=True, stop=True)
                o.act(spl[:, t, :], bk, AF.Exp, bias=nbg[:, t:t + 1], scale=-1.0)
                o.act(spl[:, t, :], spl[:, t, :], AF.Ln, bias=onec[:, 0:1], scale=1.0)
                o.scan(cum[:, t, :], rmask, spl[:, t, :], 0.0)
                cl = cum[:, t, :].v(lambda a: a.rearrange("p (c k) -> p c k", k=128))[:, :, 127]
                o.ts('dve', ncl[:, t, :], cl, -1.0 / 16.0, None, mult)
                o.act(dec[:, t, :], ncl[:, t, :], AF.Exp)
                for cc in range(4):
                    cs = slice(cc * 128, (cc + 1) * 128)
                    o.act(ekl[:, t, cs], cum[:, t, cs], AF.Exp, bias=ncl[:, t, cc:cc + 1], scale=1.0 / 16.0)
                if own:
                    o.act(eb[:, t, :], cum[:, t, :], AF.Exp, scale=-1.0 / 16.0)
                    o.act(enb[:, t, :], cum[:, t, :], AF.Exp, scale=1.0 / 16.0)
            for t in range(2):
                bk = pb[4 + t]
                for kt in range(8):
                    o.mm(bk, wk_[:, kt, t * 128:(t + 1) * 128], src_fn(kt), start=(kt == 0), stop=(kt == 7))
                o.tt('dve', klT[:, t, :], bk, ekl[:, t, :], mult)
                if own:
                    o.tt('dve', keT[:, t, :], bk, enb[:, t, :], mult)
                    bq = pb[6 + t]
                    for kt in range(8):
                        o.mm(bq, wq_[:, kt, t * 128:(t + 1) * 128], src_fn(kt), start=(kt == 0), stop=(kt == 7))
                    o.stt(qeT[:, t, :], bq, 0.125, eb[:, t, :], mult, mult)
            for j in range(4):
                c = 4 * g + j
                tsl = slice(j * 128, (j + 1) * 128)
                bv = pb[0]
                for kt in range(8):
                    o.mm(bv, src_fn(kt)[:, tsl], wv_[:, kt, :], start=(kt == 0), stop=(kt == 7))
                o.cp('act', vt[:, j, :], bv)
                bt = pb[1]
                for t in range(2):
                    o.tr(bt[:, t * 128:(t + 1) * 128], klT[:, t, tsl], ident)
                o.cp('dve', klt[:, j, :], bt[:, 0:256])
                if own:
                    bsb = (pb[2], pb[7])
                    for h in range(4):
                        t, r0 = divmod(h, 2)
                        rows = slice(64 * r0, 64 * r0 + 64)
                        o.mm(bsb[r0][:, t * 128:(t + 1) * 128], keT[rows, t, tsl], qeT[rows, t, tsl], start=True, stop=True)
                    for r0 in range(2):
                        o.tt('dve', T(scT.ap.rearrange("p (t r) i -> p r t i", r=2)[:, r0], scT.k),
                             bsb[r0][:, 0:256].v(lambda a: a.rearrange("p (t i) -> p t i", t=2)),
                             tri.v(lambda a: bc(a, [128, 2, 128], 1)), mult)
                    bob = (pb[3], pb[5])
                    for h in range(4):
                        t, r0 = divmod(h, 2)
                        rows = slice(64 * r0, 64 * r0 + 64)
                        hs = slice(h * 128, (h + 1) * 128)
                        ob = bob[r0][:, t * 128:(t + 1) * 128]
                        o.mm(ob, scT[:, h, :], vt[:, j, hs], start=True, stop=False)
                        o.mm(ob, qeT[rows, t, tsl], Sbf[rows, t, :], start=False, stop=True)
                    br = pb[4]
                    for kt in range(8):
                        o.mm(br, src_fn(kt)[:, tsl], wr_[:, kt, :], start=(kt == 0), stop=(kt == 7))
                    o.act(rsil, br, AF.Silu)
                    o.tt('dve', gnrs, rsil, gng, mult)
                    for h in range(4):
                        t, r0 = divmod(h, 2)
                        o.act(junk, bob[r0][:, t * 128:(t + 1) * 128], AF.Square, accum_out=ssq[:, h:h + 1])
                    o.ts('dve', rs4, ssq, 1.0 / 128.0, LN_EPS, mult, add)
                    o.act(rs4, rs4, AF.Sqrt)
                    o.recip(rs4, rs4)
                    for h in range(4):
                        t, r0 = divmod(h, 2)
                        hs = slice(h * 128, (h + 1) * 128)
                        o.stt(ysb[:, hs], bob[r0][:, t * 128:(t + 1) * 128], rs4[:, h:h + 1], gnrs[:, hs], mult, mult)
                    bt2 = pb[1]
                    for h in range(4):
                        o.tr(bt2[:, h * 128:(h + 1) * 128], ysb[:, h * 128:(h + 1) * 128], ident)
                    o.cp('act', T(ymix.ap[:, 4:8, c * 128:(c + 1) * 128], f'ymixg{c}'),
                         bt2.v(lambda a: a.rearrange("p (h k) -> p h k", h=4)))
                bu_ = pb[6]
                for h in range(4):
                    t, r0 = divmod(h, 2)
                    o.mm(bu_[64 * r0:64 * r0 + 64, t * 128:(t + 1) * 128], klt[:, j, h * 64:(h + 1) * 64],
                         vt[:, j, h * 128:(h + 1) * 128], start=True, stop=True)
                for t in range(2):
                    o.stt(Sg_[:, t, :], Sg_[:, t, :], dec[:, t, j:j + 1], bu_[:, t * 128:(t + 1) * 128], mult, add)
                o.cp('act', Sbf, Sg_)

    o.memset('dve', Sg_, 0.0)
    o.memset('dve', Sbf, 0.0)
    Sflat = Sg_.v(lambda a: a.rearrange("p t v -> p (t v)"))
    for k in (2, 1, 0):
        gla_segment(xprev_d[k], False)
        o.ts('dve', Sflat, Sflat, valid[:, k:k + 1], None, mult)
        o.cp('act', Sbf, Sg_)
    if 'stopG0' in debug:
        return finish()
    gla_segment(x_d, True)
    if 'stopMix' in debug:
        return finish()
    S.fence()
    A.release(mMix)
    bd = {n: din(n, sh) for n, sh in [
        ("g0rep", [128, D]), ("b0rep", [128, D]), ("g1rep", [128, D]), ("b1rep", [128, D]),
        ("g2rep", [128, D]), ("b2rep", [128, D]), ("g3rep", [128, D]), ("b3rep", [128, D]),
        ("g1col", [128, 8]), ("b1col", [128, 8]), ("g2col", [128, 8]), ("b2col", [128, 8]),
        ("gmcol", [128, 8]), ("bmcol", [128, 8]), ("ones", [128, 128]), ("mem", [256, D]),
        ("w_out", [128, 8, 1024]), ("w_mq", [128, 8, 1024]), ("w_mo", [128, 8, 1024]), ("w_mkv", [128, 8, 2048]),
        ("w_ff1", [128, 8, 4096]), ("w_ff2", [128, 32, 1024])]}
    xn = alloc('xn', [128, NT, D])
    st = alloc('st3', [128, NT, 12])
    mv = alloc('mv3', [128, NT, 2])
    sd = alloc('sd3', [128, NT])
    rstd = alloc('rstd3', [128, NT])
    nmr = alloc('nmr3', [128, NT])
    grep_, brep_ = alloc('grep', [128, D]), alloc('brep', [128, D])
    gcol, bcol = alloc('gcol', [128, 8]), alloc('bcol', [128, 8])
    tb = [alloc(f'tb{i}', [128, D]) for i in range(4)]
    wA = alloc('wA', [128, 8, 1024], BF16)
    mv0, sc0 = alloc('mv0', [128, 2]), alloc('sc0', [128, 4])
    st0 = alloc('st0', [128, 12])
    dumpables['xn'] = (xn.key([f'xn{i}' for i in range(NT)]), [128, NT, D], F32)

    def ln_tile_inplace(buf):
        for hh in range(2):
            o.bn_stats(st0[:, hh * 6:(hh + 1) * 6], buf[:, hh * 512:(hh + 1) * 512])
        o.bn_aggr(mv0, st0)
        o.act(sc0[:, 0:1], mv0[:, 1:2], AF.Sqrt, bias=epsc[:, 0:1], scale=1.0)
        o.recip(sc0[:, 1:2], sc0[:, 0:1])
        o.stt(sc0[:, 2:3], mv0[:, 0:1], -1.0, sc0[:, 1:2], mult, mult)
        o.act(buf, buf, AF.Identity, bias=sc0[:, 2:3], scale=sc0[:, 1:2])

    def res_from_x(i, dst):
        o.dma('sp', dst, x_d[i * 128:(i + 1) * 128, :], sem=f'oxin{i % 4}')
        ln_tile_inplace(dst)
        o.tt('dve', dst, dst, grep_, mult)
        o.tt('dve', dst, dst, brep_, add)

    def res_from_xn(i, dst):
        o.tt('dve', dst, xn[:, i, :].key(f'xn{i}'), grep_, mult)
        o.tt('dve', dst, dst, brep_, add)

    def post_ln_group(g, write_hT):
        ln_stats_group(g)
        tiles = []
        for j in range(4):
            i = 4 * g + j
            dst = xn[:, i, :].key(f'xn{i}')
            o.act(dst, tb[j], AF.Identity, bias=nmr[:, i:i + 1].key(f'nmr{g}'), scale=rstd[:, i:i + 1].key(f'rstd{g}'))
            tiles.append(dst)
        if write_hT:
            transposes_to(lambda kt: hT[:, kt, g * 512:(g + 1) * 512].key(hTk(kt, g)), tiles, gcol, bcol)

    def dense_res_ln(srcT_fn, nk, w_fn, res_fn, write_hT=True):
        for g in range(4):
            for j in range(4):
                i = 4 * g + j
                banks = (pb[2], pb[3])
                for half in range(2):
                    for kt in range(nk):
                        o.mm(banks[half], srcT_fn(kt, i), w_fn(kt, half), start=(kt == 0), stop=(kt == nk - 1))
                res_fn(i, tb[j])
                for half in range(2):
                    hs = slice(half * 512, (half + 1) * 512)
                    o.stt(tb[j][:, hs], tb[j][:, hs], DN_ALPHA, banks[half], mult, add)
                ln_stats_tile(i, tb[j])
            post_ln_group(g, write_hT)

    o.dma('pool', wA, bd['w_out'], sem='wA')
    o.dma('sp', grep_, bd['g0rep'], sem='c')
    o.dma('sp', brep_, bd['b0rep'], sem='c')
    o.dma('sp', gcol, bd['g1col'], sem='c')
    o.dma('sp', bcol, bd['b1col'], sem='c')

    def ymixT(kt, i):
        key = f'ymix{kt}_{i // 4}' if kt < 4 else f'ymixg{i}'
        return T(ymix.ap[:, kt, i * 128:(i + 1) * 128], key)

    dense_res_ln(ymixT, 8, lambda kt, half: wA[:, kt, half * 512:(half + 1) * 512], res_from_x)
    if 'stopO' in debug:
        return finish()
    S.fence()
    mX = A.mark()
    kT = alloc('kT', [128, 8, 256], BF16)
    vtk = alloc('vtk', [128, 2, 1024], BF16)
    wkvq = alloc('wkvq', [128, 8, 512], BF16)
    qTh = alloc('qTh', [128, 2, TOK], BF16)
    pT1 = alloc('pT1', [128, 2, 512], BF16)
    rrec = alloc('rrec', [128, 512])
    onesb = alloc('onesb', [128, 128], BF16)
    gmc, bmc = alloc('gmc', [128, 8]), alloc('bmc', [128, 8])
    o.dma('pool', onesb, bd['ones'], sem='onesb')
    o.dma('sp', gmc, bd['gmcol'], sem='c')
    o.dma('sp', bmc, bd['bmcol'], sem='c')
    oT = T(ymix.ap, 'oT')
    mS_ = A.mark()
    memT = alloc('memT', [128, 8, 256], BF16)
    for mt in range(2):
        o.dma('sp', tb[mt], bd['mem'][mt * 128:(mt + 1) * 128, :], sem=f'memin{mt}')
        ln_tile_inplace(tb[mt])
    for kt in range(8):
        bank = pb[kt % 2]
        for mt in range(2):
            o.tr(bank[:, mt * 128:(mt + 1) * 128], tb[mt][:, kt * 128:(kt + 1) * 128], ident)
        o.act(memT[:, kt, :], bank[:, 0:256], AF.Identity, bias=bmc[:, kt:kt + 1], scale=gmc[:, kt:kt + 1])
    for qd in range(4):
        o.dma('pool', wkvq, bd['w_mkv'][:, :, qd * 512:(qd + 1) * 512], sem='wkvq')
        if qd < 2:
            for c4 in range(4):
                bank = pb[4 + c4 % 2]
                for kt in range(8):
                    o.mm(bank[:, 0:256], wkvq[:, kt, c4 * 128:(c4 + 1) * 128], memT[:, kt, :], start=(kt == 0), stop=(kt == 7))
                o.cp('act', kT[:, qd * 4 + c4, :], bank[:, 0:256])
        else:
            for mt in range(2):
                bank = pb[6 + mt]
                for kt in range(8):
                    o.mm(bank, memT[:, kt, mt * 128:(mt + 1) * 128], wkvq[:, kt, :], start=(kt == 0), stop=(kt == 7))
                o.cp('act', vtk[:, mt, (qd - 2) * 512:(qd - 1) * 512], bank)
    S.fence()
    A.release(mS_)
    o.dma('pool', wA, bd['w_mq'], sem='wA')
    for h in range(4):
        for c2 in range(2):
            c = 2 * h + c2
            for g in range(4):
                bank = pb[4 + g % 2]
                for kt in range(8):
                    o.mm(bank, wA[:, kt, c * 128:(c + 1) * 128], hT[:, kt, g * 512:(g + 1) * 512].key(hTk(kt, g)),
                         start=(kt == 0), stop=(kt == 7))
                o.act(qTh[:, c2, g * 512:(g + 1) * 512], bank, AF.Identity, scale=1.0 / 16.0)
        for g in range(4):
            gs = slice(g * 512, (g + 1) * 512)
            for mt in range(2):
                bank = pb[6 + mt]
                for c2 in range(2):
                    o.mm(bank, kT[:, 2 * h + c2, mt * 128:(mt + 1) * 128], qTh[:, c2, gs], start=(c2 == 0), stop=(c2 == 1))
                o.act(pT1[:, mt, :], bank, AF.Exp)
            bank = pb[0]
            for mt in range(2):
                o.mm(bank, onesb, pT1[:, mt, :], start=(mt == 0), stop=(mt == 1))
            o.recip(rrec, bank)
            for c2 in range(2):
                c = 2 * h + c2
                bank = pb[1 + c2]
                for mt in range(2):
                    o.mm(bank, vtk[:, mt, c * 128:(c + 1) * 128], pT1[:, mt, :], start=(mt == 0), stop=(mt == 1))
                o.tt('dve', T(oT.ap[:, c, gs], f'oT{c}_{g}'), bank, rrec, mult)
    o.dma('pool', wA, bd['w_mo'], sem='wA')
    o.dma('sp', grep_, bd['g1rep'], sem='c')
    o.dma('sp', brep_, bd['b1rep'], sem='c')
    o.dma('sp', gcol, bd['g2col'], sem='c')
    o.dma('sp', bcol, bd['b2col'], sem='c')
    dense_res_ln(lambda kt, i: T(oT.ap[:, kt, i * 128:(i + 1) * 128], f'oT{kt}_{i // 4}'), 8,
                 lambda kt, half: wA[:, kt, half * 512:(half + 1) * 512], res_from_xn)
    if 'stopX' in debug:
        return finish()

    S.fence()
    A.release(mX)
    hid = T(ymix.ap, 'hid')
    w1q = alloc('w1q', [128, 8, 1024], BF16)
    w2q = alloc('w2q', [128, 8, 1024], BF16)
    rl = [T(tb[k_].ap[:, 0:512], tb[k_].k) for k_ in range(4)]
    o.dma('sp', grep_, bd['g2rep'], sem='c')
    o.dma('sp', brep_, bd['b2rep'], sem='c')
    for qt in range(4):
        o.dma('pool', w1q, bd['w_ff1'][:, :, qt * 1024:(qt + 1) * 1024], sem='w1q')
        o.dma('pool', w2q, bd['w_ff2'][:, qt * 8:(qt + 1) * 8, :], sem='w2q')
        for ft in range(8):
            for g in range(4):
                idx = ft * 4 + g
                bank = pb[4 + idx % 2]
                for kt in range(8):
                    o.mm(bank, w1q[:, kt, ft * 128:(ft + 1) * 128], hT[:, kt, g * 512:(g + 1) * 512].key(hTk(kt, g)),
                         start=(kt == 0), stop=(kt == 7))
                r_ = rl[idx % 4]
                o.act(r_, bank, AF.Relu)
                o.tt('dve', T(hid.ap[:, ft, g * 512:(g + 1) * 512], f'hid{ft}_{g}'), r_, r_, mult)
        if qt == 1:
            o.dma('sp', grep_, bd['g3rep'], sem='c')
            o.dma('sp', brep_, bd['b3rep'], sem='c')
        for g in range(4):
            for j in range(4):
                i = 4 * g + j
                banks = (pb[2], pb[3])
                for half in range(2):
                    for ft in range(8):
                        o.mm(banks[half], T(hid.ap[:, ft, i * 128:(i + 1) * 128], f'hid{ft}_{g}'),
                             w2q[:, ft, half * 512:(half + 1) * 512], start=(ft == 0), stop=(ft == 7))
                xi = xn[:, i, :].key(f'xn{i}')
                if qt == 0:
                    res_from_xn(i, xi)
                for half in range(2):
                    hs = slice(half * 512, (half + 1) * 512)
                    if qt == 0:
                        o.stt(xi[:, hs], xi[:, hs], DN_ALPHA, banks[half], mult, add)
                    elif qt < 3:
                        o.tt('dve', xi[:, hs], xi[:, hs], banks[half], add)
                    else:
                        o.tt('dve', tb[j][:, hs], xi[:, hs], banks[half], add)
                if qt == 3:
                    ln_stats_tile(i, tb[j])
            if qt == 3:
                ln_stats_group(g)
                for j in range(4):
                    i = 4 * g + j
                    o.act(tb[j], tb[j], AF.Identity, bias=nmr[:, i:i + 1].key(f'nmr{g}'), scale=rstd[:, i:i + 1].key(f'rstd{g}'))
                    o.tt('dve', tb[j], tb[j], grep_, mult)
                    o.tt('dve', tb[j], tb[j], brep_, add)
                    o.dma('sp', T(out_d.ap[i * 128:(i + 1) * 128, :], f'out{i}'), tb[j], sem=f'outs{j}')
    return finish()


def host_inputs(inp):
    f32 = np.float32
    x = np.ascontiguousarray(inp['x'], dtype=f32)

    def cols(v):
        v = np.asarray(v, f32).reshape(-1)
        return np.ascontiguousarray(v.reshape(-1, 128).T)

    def ktile(wm):
        K, N = wm.shape
        return np.ascontiguousarray(np.asarray(wm, f32).reshape(K // 128, 128, N).transpose(1, 0, 2))

    def rep(v):
        return np.ascontiguousarray(np.broadcast_to(np.asarray(v, f32).reshape(1, -1), (128, D)))

    def gh_layout(a_gp):
        a = np.asarray(a_gp, f32).reshape(4, 8, 64)
        a = np.broadcast_to(a.transpose(1, 0, 2)[:, None, :, :], (8, 16, 4, 64))
        return np.ascontiguousarray(a.reshape(128, 4, 64))

    def pair_layout(a_gp):
        a = np.asarray(a_gp, f32).reshape(16, 2, 64)
        return np.ascontiguousarray(a.transpose(1, 2, 0).reshape(128, 16))

    lst = np.asarray(inp['ssm_log_step'][0], f32)
    b_re = np.asarray(inp['ssm_b_re'][0], f32).reshape(4, 8, 64, 16)
    b_im = np.asarray(inp['ssm_b_im'][0], f32).reshape(4, 8, 64, 16)
    c_re = np.asarray(inp['ssm_c_re'][0], f32).reshape(4, 8, 16, 64)
    c_im = np.asarray(inp['ssm_c_im'][0], f32).reshape(4, 8, 16, 64)
    maske = np.zeros((8, 16, 2), f32)
    for gp in range(8):
        maske[gp, :, gp % 2] = 1.0
    common = {
        'ident': np.eye(128, dtype=f32),
        'lng0': cols(inp['emb_ln_g']),
        'lnb0': cols(inp['emb_ln_b']),
        'w_in': ktile(inp['w_in'][0]),
        's5_lam_re': gh_layout(inp['ssm_lam_re'][0]),
        's5_lam_im': gh_layout(inp['ssm_lam_im'][0]),
        's5_lst': np.ascontiguousarray(np.broadcast_to(lst.reshape(4, 8).T[:, None, :], (8, 16, 4)).reshape(128, 4)),
        's5_bre': np.ascontiguousarray(b_re.transpose(1, 3, 0, 2).reshape(128, 4, 64)),
        's5_bim': np.ascontiguousarray(b_im.transpose(1, 3, 0, 2).reshape(128, 4, 64)),
        's5_cre': np.ascontiguousarray(c_re.transpose(1, 2, 0, 3).reshape(128, 4, 64)),
        's5_cim': np.ascontiguousarray(c_im.transpose(1, 2, 0, 3).reshape(128, 4, 64)),
        's5_d': np.ascontiguousarray(np.asarray(inp['ssm_d'][0], f32).reshape(4, 8, 16).transpose(1, 2, 0).reshape(128, 4)),
        's5_maske': np.ascontiguousarray(maske.reshape(128, 2)),
        's5p_lam_re': pair_layout(inp['ssm_lam_re'][0]),
        's5p_lam_im': pair_layout(inp['ssm_lam_im'][0]),
        's5p_lst': np.ascontiguousarray(np.broadcast_to(lst.reshape(16, 2)[:, :, None], (16, 2, 64)).transpose(1, 2, 0).reshape(128, 16)),
        'kvec': np.ascontiguousarray(np.broadcast_to(np.arange(129, dtype=f32)[None, :], (128, 129))),
        'w_glu': ktile(inp['w_glu'][0]),
        'gla_wgu': np.ascontiguousarray(np.asarray(inp['w_gate_up'][0], f32)),
        'gla_bg': cols(inp['b_gate'][0]),
        'gla_ng': np.ascontiguousarray(np.broadcast_to(np.asarray(inp['gla_norm_g'][0], f32)[None, :], (128, 512))),
        'tri': np.triu(np.ones((128, 128), f32)),
        'ones': np.ones((128, 128), f32),
        'g0rep': rep(inp['emb_ln_g']), 'b0rep': rep(inp['emb_ln_b']),
        'g1rep': rep(inp['ln1_g'][0]), 'b1rep': rep(inp['ln1_b'][0]),
        'g2rep': rep(inp['ln2_g'][0]), 'b2rep': rep(inp['ln2_b'][0]),
        'g3rep': rep(inp['ln3_g'][0]), 'b3rep': rep(inp['ln3_b'][0]),
        'g1col': cols(inp['ln1_g'][0]), 'b1col': cols(inp['ln1_b'][0]),
        'g2col': cols(inp['ln2_g'][0]), 'b2col': cols(inp['ln2_b'][0]),
        'gmcol': cols(inp['mem_ln_g']), 'bmcol': cols(inp['mem_ln_b']),
        'w_out': ktile(inp['w_out'][0]), 'w_mq': ktile(inp['w_mq'][0]), 'w_mo': ktile(inp['w_mo'][0]),
        'w_mkv': ktile(inp['w_mkv'][0]), 'w_ff1': ktile(inp['w_ff1'][0]), 'w_ff2': ktile(inp['w_ff2'][0]),
        'rmask': np.ascontiguousarray(np.broadcast_to((np.arange(512) % 128 != 0).astype(f32)[None, :], (128, 512))),
        'b_glu': cols(inp['b_glu'][0]),
    }
    maps = []
    for c in range(NCORES):
        bb, j = divmod(c, 4)
        m = dict(common)
        m['x'] = np.ascontiguousarray(x[bb, j * TOK:(j + 1) * TOK, :])
        m['mem'] = np.ascontiguousarray(np.asarray(inp['mem'], f32)[bb])
        val = np.zeros((128, 3), f32)
        for k in range(3):
            src = j - 1 - k
            if src >= 0:
                m[f'xprev{k}'] = np.array(x[bb, src * TOK:(src + 1) * TOK, :], dtype=f32, order='C', copy=True)
                val[:, k] = 1.0
            else:
                m[f'xprev{k}'] = np.array(x[bb, j * TOK:(j + 1) * TOK, :], dtype=f32, order='C', copy=True)
        m['valid'] = val
        maps.append(m)
    return maps


_CACHE = {}


def kernel(**inputs):
    if 'nc' not in _CACHE:
        _CACHE['nc'] = build_program()[0]
    nc = _CACHE['nc']
    maps = host_inputs(inputs)
    res = run_bass_kernel_spmd(nc, maps, core_ids=list(range(NCORES)))
    out = np.empty((2, 8192, D), np.float32)
    for c in range(NCORES):
        bb, j = divmod(c, 4)
        out[bb, j * TOK:(j + 1) * TOK, :] = res.results[c]['out']
    return out
```

```python
import os
import math
import numpy as np
from contextlib import ExitStack
import concourse.bass as bass
import concourse.mybir as mybir
from concourse.bass_utils import run_bass_kernel_spmd

F32 = mybir.dt.float32
BF16 = mybir.dt.bfloat16
ALU = mybir.AluOpType
AF = mybir.ActivationFunctionType
AX = mybir.AxisListType

NCORES = 8
TOK = 2048
D = 1024
NT = TOK // 128
LN_EPS = 1e-5
DN_ALPHA = 2.0 ** 0.25
PI = math.pi


class Sched:
    EPOCH = 8000
    ENG = ('pe', 'act', 'dve', 'pool', 'sp')

    def __init__(self):
        self.q = {e: [] for e in self.ENG}
        self.cnt = {e: 0 for e in self.ENG}
        self.waited = {}
        self.lastw = {}
        self.readers = {}
        self.dmacnt = {}
        self.semkeys = []
        self._semset = set()
        self.targets = {e: set() for e in self.ENG}

    def _sem(self, k):
        if k not in self._semset:
            self._semset.add(k)
            self.semkeys.append(k)

    def _filter(self, eng, need):
        waits = []
        for s, v in need.items():
            if eng == 'pe' and s == ('eng', 'pe'):
                continue
            if self.waited.get((eng, s), -1) >= v:
                continue
            self.waited[(eng, s)] = v
            waits.append((s, v))
            if s[0] == 'eng':
                self.targets[s[1]].add(v)
        return waits

    def _deps(self, eng, reads, writes):
        need = {}

        def add(ev):
            s, v = ev
            if need.get(s, -1) < v:
                need[s] = v
        for k in reads:
            if k in self.lastw:
                add(self.lastw[k])
        for k in writes:
            if k in self.lastw:
                add(self.lastw[k])
            for ev in self.readers.get(k, ()):
                add(ev)
        return self._filter(eng, need)

    def _register(self, ev, reads, writes):
        for k in reads:
            self.readers.setdefault(k, []).append(ev)
        for k in writes:
            self.lastw[k] = ev
            self.readers[k] = []

    def op(self, eng, fn, r=(), w=()):
        waits = self._deps(eng, r, w)
        idx = self.cnt[eng]
        self.cnt[eng] += 1
        ev = (('eng', eng), idx)
        self._register(ev, r, w)
        self.q[eng].append([waits, fn, 'op', idx])

    def dma(self, qeng, fn, r=(), w=(), sem=None, inc=16, kind='dma'):
        waits = self._deps(qeng, r, w)
        s = (kind, sem)
        self._sem(s)
        self.dmacnt[s] = self.dmacnt.get(s, 0) + inc
        ev = (s, self.dmacnt[s])
        self._register(ev, r, w)
        self.q[qeng].append([waits, fn, 'dma', (s, inc)])

    def fence(self):
        latest = {}
        for e in self.ENG:
            if self.cnt[e] > 0:
                latest[('eng', e)] = self.cnt[e] - 1
        for s_, v in self.dmacnt.items():
            latest[s_] = v
        for e in self.ENG:
            waits = self._filter(e, dict(latest))
            if waits:
                self.q[e].append([waits, None, 'nop', None])
        self.lastw = {}
        self.readers = {}

    def final_waits(self, qeng, keys):
        waits = self._deps(qeng, keys, keys)
        self.q[qeng].append([waits, None, 'nop', None])

    def finalize(self):
        self.rank = {}
        for e in self.ENG:
            self.rank[e] = {idx: r for r, idx in enumerate(sorted(self.targets[e]))}
            n = len(self.rank[e])
            for ep in range((n + self.EPOCH - 1) // self.EPOCH):
                self._sem(('eng', e, ep))

    def _semval(self, e, idx):
        r = self.rank[e][idx]
        return ('eng', e, r // self.EPOCH), r % self.EPOCH + 1

    def replay(self, nc, block, sems):
        def run(engname):
            def f(eng):
                for waits, fn, kind, info in self.q[engname]:
                    for (ws, wv) in waits:
                        if ws[0] == 'eng':
                            k, v = self._semval(ws[1], wv)
                            eng.wait_ge(sems[k], v)
                        else:
                            eng.wait_ge(sems[ws], wv)
                    if fn is None:
                        continue
                    ins = fn(eng)
                    if kind == 'op':
                        if info in self.rank[engname]:
                            k, _ = self._semval(engname, info)
                            ins.then_inc(sems[k], 1)
                    else:
                        ins.then_inc(sems[info[0]], info[1])
            return f
        block.tensor(run('pe'))
        block.scalar(run('act'))
        block.vector(run('dve'))
        block.gpsimd(run('pool'))
        block.sync(run('sp'))


class B:
    def __init__(self, nc, sched):
        self.nc = nc
        self.s = sched

    def dma(self, q, out, in_, r=(), w=(), sem=None):
        self.s.dma(q, lambda e: e.dma_start(out=out, in_=in_), r, w, sem)

    def mm(self, out, lhsT, rhs, start, stop, r=(), w=(), **kw):
        self.s.op('pe', lambda e: e.matmul(out, lhsT, rhs, start=start, stop=stop, **kw), r, w)

    def tr(self, out, in_, ident, r=(), w=()):
        self.s.op('pe', lambda e: e.transpose(out, in_, ident), r, w)

    def act(self, out, in_, func, bias=0.0, scale=1.0, r=(), w=(), accum_out=None):
        if accum_out is None:
            self.s.op('act', lambda e: e.activation(out=out, in_=in_, func=func, bias=bias, scale=scale), r, w)
        else:
            self.s.op('act', lambda e: e.activation(out=out, in_=in_, func=func, bias=bias, scale=scale,
                                                    accum_out=accum_out), r, w)

    def tt(self, eng, out, in0, in1, op, r=(), w=()):
        self.s.op(eng, lambda e: e.tensor_tensor(out=out, in0=in0, in1=in1, op=op), r, w)

    def ts(self, eng, out, in0, s1, s2, op0, op1=None, r=(), w=()):
        if op1 is None:
            self.s.op(eng, lambda e: e.tensor_scalar(out=out, in0=in0, scalar1=s1, scalar2=None, op0=op0), r, w)
        else:
            self.s.op(eng, lambda e: e.tensor_scalar(out=out, in0=in0, scalar1=s1, scalar2=s2, op0=op0, op1=op1), r, w)

    def stt(self, out, in0, scalar, in1, op0, op1, r=(), w=()):
        self.s.op('dve', lambda e: e.scalar_tensor_tensor(out=out, in0=in0, scalar=scalar, in1=in1, op0=op0, op1=op1), r, w)

    def copy(self, eng, out, in_, r=(), w=()):
        if eng == 'act':
            self.s.op('act', lambda e: e.copy(out=out, in_=in_), r, w)
        else:
            self.s.op(eng, lambda e: e.tensor_copy(out=out, in_=in_), r, w)

    def memset(self, eng, ap, val, w=()):
        self.s.op(eng, lambda e: e.memset(ap, val), (), w)

    def scan(self, out, d0, d1, init, op0, op1, r=(), w=()):
        self.s.op('dve', lambda e: e.tensor_tensor_scan(out=out, data0=d0, data1=d1, initial=init, op0=op0, op1=op1), r, w)

    def bn_stats(self, out, in_, r=(), w=()):
        self.s.op('dve', lambda e: e.bn_stats(out=out, in_=in_), r, w)

    def bn_aggr(self, out, in_, r=(), w=()):
        self.s.op('dve', lambda e: e.bn_aggr(out=out, in_=in_), r, w)

    def reduce(self, out, in_, op, r=(), w=()):
        self.s.op('dve', lambda e: e.tensor_reduce(out=out, in_=in_, axis=AX.X, op=op), r, w)

    def cc_allreduce(self, out, in_, r=(), w=()):
        groups = [list(range(NCORES))]
        n = sum(1 for k in self.s.semkeys if k[0] == 'cc')
        self.s.dma('pool', lambda e: e.collective_compute("AllReduce", ALU.add, replica_groups=groups,
                                                          ins=[in_], outs=[out]), r, w, sem=n, inc=1, kind='cc')

    def recip(self, out, in_, r=(), w=()):
        self.s.op('dve', lambda e: e.reciprocal(out=out, in_=in_), r, w)


I32 = mybir.dt.int32
TWO_PI = 2.0 * math.pi


def _prod(xs):
    r = 1
    for v in xs:
        r *= v
    return r


class Arena:
    def __init__(self, nc, es, nbytes):
        self.t = es.enter_context(nc.sbuf_tensor("arena", [128, nbytes // 4], F32))
        self.h = {F32: self.t, BF16: self.t.bitcast(BF16), I32: self.t.bitcast(I32)}
        self.off = 0
        self.cap = nbytes
        self.peak = 0

    def mark(self):
        return self.off

    def release(self, m):
        self.off = m

    def alloc(self, shape, dt=F32):
        sz = 2 if dt == BF16 else 4
        n = _prod(shape[1:])
        nbytes = (n * sz + 31) // 32 * 32
        assert self.off + nbytes <= self.cap, f"arena overflow: {self.off}+{nbytes}>{self.cap}"
        lo = self.off // sz
        ap = self.h[dt][:, lo:lo + n]
        self.off += nbytes
        self.peak = max(self.peak, self.off)
        if len(shape) > 2:
            names = 'abcdef'[:len(shape) - 1]
            pat = "p (" + " ".join(names) + ") -> p " + " ".join(names)
            ap = ap.rearrange(pat, **{k: v for k, v in zip(names, shape[1:])})
        if shape[0] != 128:
            ap = ap[0:shape[0]]
        return ap


def bc(ap, shape, axis):
    return ap.unsqueeze(axis).to_broadcast(list(shape))


class SubArena:
    def __init__(self, A, start, nbytes):
        self.A = A
        self.start = start
        self.cap = nbytes
        self.off = 0

    def reset(self):
        self.off = 0

    def alloc(self, shape, dt=F32):
        save_off, save_cap, save_peak = self.A.off, self.A.cap, self.A.peak
        self.A.off = self.start + self.off
        self.A.cap = self.start + self.cap
        ap = self.A.alloc(shape, dt)
        self.off = self.A.off - self.start
        self.A.off, self.A.cap, self.A.peak = save_off, save_cap, save_peak
        return ap
class T:
    def __init__(self, ap, keys):
        self.ap = ap
        self.k = [keys] if isinstance(keys, str) else list(keys)

    def __getitem__(self, idx):
        return T(self.ap[idx], self.k)

    def v(self, fn):
        return T(fn(self.ap), self.k)

    def key(self, keys):
        return T(self.ap, keys)


def _k(x):
    return x.k if isinstance(x, T) else []


def _a(x):
    return x.ap if isinstance(x, T) else x


class Ops:
    def __init__(self, b):
        self.b = b

    def dma(self, q, out, in_, sem):
        if sem in ('c', 'dbg'):
            self._uniq = getattr(self, '_uniq', 0) + 1
            sem = f'{sem}{self._uniq}'
        self.b.dma(q, _a(out), _a(in_), r=_k(in_), w=_k(out), sem=sem)

    def mm(self, out, lhsT, rhs, start, stop, **kw):
        self.b.mm(_a(out), _a(lhsT), _a(rhs), start, stop, r=_k(lhsT) + _k(rhs), w=_k(out), **kw)

    def tr(self, out, in_, ident):
        self.b.tr(_a(out), _a(in_), _a(ident), r=_k(in_) + _k(ident), w=_k(out))

    def act(self, out, in_, func, bias=0.0, scale=1.0, accum_out=None):
        self.b.act(_a(out), _a(in_), func, bias=_a(bias), scale=_a(scale), r=_k(in_) + _k(bias) + _k(scale),
                   w=_k(out) + _k(accum_out), accum_out=_a(accum_out) if accum_out is not None else None)

    def tt(self, eng, out, a, c, op):
        self.b.tt(eng, _a(out), _a(a), _a(c), op, r=_k(a) + _k(c), w=_k(out))

    def ts(self, eng, out, a, s1, s2, op0, op1=None):
        self.b.ts(eng, _a(out), _a(a), _a(s1), _a(s2), op0, op1, r=_k(a) + _k(s1) + _k(s2), w=_k(out))

    def stt(self, out, a, scalar, c, op0, op1):
        self.b.stt(_a(out), _a(a), _a(scalar), _a(c), op0, op1, r=_k(a) + _k(scalar) + _k(c), w=_k(out))

    def cp(self, eng, out, a):
        self.b.copy(eng, _a(out), _a(a), r=_k(a), w=_k(out))

    def memset(self, eng, out, val):
        self.b.memset(eng, _a(out), val, w=_k(out))

    def scan(self, out, d0, d1, init, op0=None, op1=None):
        self.b.scan(_a(out), _a(d0), _a(d1), _a(init), op0 or ALU.mult, op1 or ALU.add,
                    r=_k(d0) + _k(d1) + _k(init), w=_k(out))

    def reduce(self, out, in_, op):
        self.b.reduce(_a(out), _a(in_), op, r=_k(in_), w=_k(out))

    def recip(self, out, in_):
        self.b.recip(_a(out), _a(in_), r=_k(in_), w=_k(out))

    def bn_stats(self, out, in_):
        self.b.bn_stats(_a(out), _a(in_), r=_k(in_), w=_k(out))

    def bn_aggr(self, out, in_):
        self.b.bn_aggr(_a(out), _a(in_), r=_k(in_), w=_k(out))

    def allreduce(self, out, in_):
        self.b.cc_allreduce(_a(out), _a(in_), r=_k(in_), w=_k(out))
def build_program(debug=()):
    nc = bass.Bass("TRN2", target_bir_lowering=False)
    es = ExitStack()
    S = Sched()
    b = B(nc, S)
    o = Ops(b)
    A = Arena(nc, es, 206 * 1024)

    def alloc(name, shape, dt=F32):
        return T(A.alloc(shape, dt), name)

    def din(name, shape, dt=F32):
        return T(nc.dram_tensor(name, list(shape), dt, kind="ExternalInput").ap(), 'd_' + name)

    pb = [T(es.enter_context(nc.psum_tensor(f"pb{i}", [128, 512], F32))[:], f'pb{i}') for i in range(8)]
    dbg_out = []
    dumpables = {}

    def dump(name, t, shape, dt):
        od = T(nc.dram_tensor("dbg_" + name, list(shape), dt, kind="ExternalOutput").ap(), 'dbgo_' + name)
        dbg_out.append(name)
        o.dma('sp', od, t, sem='dbg')

    mult, add, sub = ALU.mult, ALU.add, ALU.subtract

    x_d = din("x", [TOK, D])
    xprev_d = [din(f"xprev{k}", [TOK, D]) for k in range(3)]
    valid_d = din("valid", [128, 3])
    out_d = T(nc.dram_tensor("out", [TOK, D], F32, kind="ExternalOutput").ap(), 'd_out')
    ident_d = din("ident", [128, 128])
    lng0_d = din("lng0", [128, 8])
    lnb0_d = din("lnb0", [128, 8])
    w_in_d = din("w_in", [128, 8, 2064])
    s5d = {n: din(n, sh) for n, sh in [
        ("s5_lam_re", [128, 4, 64]), ("s5_lam_im", [128, 4, 64]), ("s5_lst", [128, 4]),
        ("s5_bre", [128, 4, 64]), ("s5_bim", [128, 4, 64]), ("s5_cre", [128, 4, 64]), ("s5_cim", [128, 4, 64]),
        ("s5_d", [128, 4]), ("s5_maske", [128, 2]), ("s5p_lam_re", [128, 16]), ("s5p_lam_im", [128, 16]),
        ("s5p_lst", [128, 16]), ("kvec", [128, 129]),
        ("w_glu", [128, 4, 512]), ("b_glu", [128, 4])]}

    ident = alloc('ident', [128, 128])
    lng0 = alloc('lng0', [128, 8])
    lnb0 = alloc('lnb0', [128, 8])
    epsc = alloc('epsc', [128, 1])
    zrow = alloc('zrow', [128, 128], BF16)
    hT = alloc('hT', [128, 8, TOK], BF16)
    ymix_off = A.off
    ymix = alloc('ymix', [128, 8, TOK], BF16)
    OV = SubArena(A, ymix_off, 32 * 1024)

    def oalloc(name, shape, dt=F32):
        return T(OV.alloc(shape, dt), name)
    hTk = lambda kt, g: f'hT{kt}_{g}'

    o.dma('sp', ident, ident_d, sem='c')
    o.dma('sp', lng0, lng0_d, sem='c')
    o.dma('sp', lnb0, lnb0_d, sem='c')
    o.memset('dve', epsc, LN_EPS)
    o.memset('dve', zrow, 0.0)

    def finish():
        for nm_, (t_, shp_, dt_) in dumpables.items():
            if nm_ in debug:
                dump(nm_, t_, shp_, dt_)
        S.fence()
        S.final_waits('sp', ['dbgo_' + n for n in dbg_out])
        print("arena peak bytes", A.peak, "instr counts", S.cnt)
        S.finalize()
        sems = {}
        for k in S.semkeys:
            sems[k] = es.enter_context(nc.semaphore("s_" + "_".join(str(t_) for t_ in k)))
        with nc.Block() as block:
            S.replay(nc, block, sems)
        es.close()
        return nc, dbg_out

    mMix = A.mark()
    uT = alloc('uT', [128, 4, 16, 128], BF16)
    W = alloc('W', [128, 4, 16, 2, 128], BF16)
    dumpables['uT'] = (uT.key([f'uT{m}' for m in range(4)]), [128, 4, 16, 128], BF16)
    dumpables['W'] = (W.key([f'W{k}_{r}' for k in range(16) for r in range(2)]), [128, 4, 16, 2, 128], BF16)
    dumpables['ymix'] = (T(ymix.ap[:, 0:4, :], [f'ymix{a}_{c}' for a in range(4) for c in range(4)]), [128, 4, TOK], BF16)
    dumpables['hT'] = (hT.key([hTk(kt, g) for kt in range(8) for g in range(4)]), [128, 8, TOK], BF16)
    prm = {}
    prm["s5_d"] = alloc("s5_d", [128, 4])
    prm["s5_maske"] = alloc("s5_maske", [128, 2])
    valid = alloc("valid", [128, 3])
    o.dma('sp', valid, valid_d, sem='c')
    sh = [128, 4, 64]
    sh2 = [128, 4, 2, 64]
    shp = [128, 16]
    are, aim = alloc('are', sh), alloc('aim', sh)
    Bm_re, Bm_im = alloc('Bm_re', sh2), alloc('Bm_im', sh2)
    Cm_re, Cm_im = alloc('Cm_re', sh2), alloc('Cm_im', sh2)
    t1, t2, t3, t4 = (alloc(f't{i}', sh2) for i in range(4))
    s1, s2 = alloc('s1', sh), alloc('s2', sh)
    cur = [(alloc('cur_re0', sh), alloc('cur_im0', sh)), (alloc('cur_re1', sh), alloc('cur_im1', sh))]
    Us, Uc = alloc('Us', [128, 16, 129]), alloc('Uc', [128, 16, 129])
    rho, rho128 = alloc('rho', shp), alloc('rho128', shp)
    a128re, a128im = alloc('a128re', shp), alloc('a128im', shp)
    wglu = alloc('wglu', [128, 4, 512], BF16)
    bglu = alloc('bglu', [128, 4])
    o.dma('pool', wglu, s5d['w_glu'], sem='wglu')
    o.dma('sp', bglu, s5d['b_glu'], sem='c')
    win_u = alloc('win_u', [128, 8, 512], BF16)
    o.dma('pool', win_u, w_in_d[:, :, 0:512], sem='win_u')
    mP1 = A.mark()
    for n in ("s5_lam_re", "s5_lam_im", "s5_bre", "s5_bim", "s5_cre", "s5_cim"):
        prm[n] = alloc(n, [128, 4, 64])
    prm["s5_lst"] = alloc("s5_lst", [128, 4])
    for n in ("s5p_lam_re", "s5p_lam_im", "s5p_lst"):
        prm[n] = alloc(n, [128, 16])
    prm["kvec"] = alloc("kvec", [128, 129])
    for n in prm:
        o.dma('sp', prm[n], s5d[n], sem='c')

    def sincos(x, shape, nm, out_s, out_c, kI=None, kf=None):
        if kI is None:
            kI = alloc(nm + '_ki', shape, I32)
            kf = alloc(nm + '_kf', shape)
        for off, r in ((0.0, out_s), (math.pi / 2, out_c)):
            o.ts('dve', kI, x, 1.0 / TWO_PI, off / TWO_PI, mult, add)
            o.cp('dve', kf, kI)
            o.stt(r, kf, -TWO_PI, x, mult, add)
            o.ts('dve', r, r, off, math.pi, add, ALU.min)
            o.ts('dve', r, r, -math.pi, None, ALU.max)
            o.act(r, r, AF.Sin)

    lam_re, lam_im = prm["s5_lam_re"], prm["s5_lam_im"]
    step = alloc('step', [128, 4])
    o.act(step, prm["s5_lst"], AF.Exp)
    stepb = step.v(lambda a: bc(a, sh, 2))
    zre = alloc('zre', sh)
    zim = alloc('zim', sh)
    o.tt('dve', zre, lam_re, stepb, mult)
    o.tt('dve', zim, lam_im, stepb, mult)
    mag = alloc('mag', sh)
    o.act(mag, zre, AF.Exp)
    sinz, cosz = alloc('sinz', sh), alloc('cosz', sh)
    sincos(zim, sh, 'z', sinz, cosz)
    o.tt('dve', are, mag, cosz, mult)
    o.tt('dve', aim, mag, sinz, mult)
    den = alloc('den', sh)
    tA = alloc('tA', sh)
    tB = alloc('tB', sh)
    o.tt('dve', den, lam_re, lam_re, mult)
    o.tt('dve', tA, lam_im, lam_im, mult)
    o.tt('dve', den, den, tA, add)
    o.recip(den, den)
    nre = alloc('nre', sh)
    o.ts('dve', nre, are, -1.0, None, add)
    fre = alloc('fre', sh)
    fim = alloc('fim', sh)
    o.tt('dve', tA, nre, lam_re, mult)
    o.tt('dve', tB, aim, lam_im, mult)
    o.tt('dve', tA, tA, tB, add)
    o.tt('dve', fre, tA, den, mult)
    o.tt('dve', tA, aim, lam_re, mult)
    o.tt('dve', tB, nre, lam_im, mult)
    o.tt('dve', tA, tA, tB, sub)
    o.tt('dve', fim, tA, den, mult)
    bbre = alloc('bbre', sh)
    bbim = alloc('bbim', sh)
    Bre, Bim, Cre, Cim = prm["s5_bre"], prm["s5_bim"], prm["s5_cre"], prm["s5_cim"]
    o.tt('dve', tA, fre, Bre, mult)
    o.tt('dve', tB, fim, Bim, mult)
    o.tt('dve', bbre, tA, tB, sub)
    o.tt('dve', tA, fre, Bim, mult)
    o.tt('dve', tB, fim, Bre, mult)
    o.tt('dve', bbim, tA, tB, add)
    maske = prm["s5_maske"]
    for e in range(2):
        me = maske[:, e:e + 1]
        o.ts('dve', Bm_re[:, :, e, :], bbre, me, None, mult)
        o.ts('dve', Bm_im[:, :, e, :], bbim, me, None, mult)
        o.ts('dve', Cm_re[:, :, e, :], Cre, me, None, mult)
        o.ts('dve', Cm_im[:, :, e, :], Cim, me, None, mult)

    def bce(t):
        return t.v(lambda a: bc(a, sh2, 2))

    def cur_step(pe_, j):
        cr, ci = cur[j % 2]
        nr, ni = cur[(j + 1) % 2]
        o.tt(pe_, s1, cr, are, mult)
        o.tt(pe_, s2, ci, aim, mult)
        o.tt(pe_, nr, s1, s2, sub)
        o.tt(pe_, s1, cr, aim, mult)
        o.tt(pe_, s2, ci, are, mult)
        o.tt(pe_, ni, s1, s2, add)

    W6 = W.v(lambda a: a.rearrange("p m k r (e q) -> p m k r e q", e=2))

    def Wk(kap, ri):
        return T(W6.ap[:, :, kap, ri, :, :], f'W{kap}_{ri}')

    if 'stopP0' in debug:
        return finish()
    o.memset('pool', cur[0][0], 1.0)
    o.memset('pool', cur[0][1], 0.0)
    for j in range(16):
        kap = 15 - j
        cr, ci = cur[j % 2]
        o.tt('pool', t1, bce(cr), Bm_re, mult)
        o.tt('pool', t2, bce(ci), Bm_im, mult)
        o.tt('pool', Wk(kap, 0), t1, t2, sub)
        o.tt('pool', t3, bce(cr), Bm_im, mult)
        o.tt('pool', t4, bce(ci), Bm_re, mult)
        o.tt('pool', Wk(kap, 1), t3, t4, add)
        if j < 15:
            cur_step('pool', j)

    if 'stopP1' in debug:
        return finish()
    stepP = alloc('stepP', shp)
    o.act(stepP, prm["s5p_lst"], AF.Exp)
    zreP = alloc('zreP', shp)
    o.tt('dve', zreP, prm["s5p_lam_re"], stepP, mult)
    o.act(rho, zreP, AF.Exp, scale=16.0)
    o.act(rho128, zreP, AF.Exp, scale=2048.0)
    phi = alloc('phi', shp)
    o.tt('dve', phi, prm["s5p_lam_im"], stepP, mult)
    phk = alloc('phk', shp, I32)
    phf = alloc('phf', shp)
    o.ts('dve', phk, phi, 16.0 / TWO_PI, None, mult)
    o.cp('dve', phf, phk)
    o.ts('dve', phi, phi, 16.0, None, mult)
    o.stt(phi, phf, -TWO_PI, phi, mult, add)
    shU = [128, 8, 129]
    argk = alloc('argk', shU)
    ukI, ukf = alloc('ukI', shU, I32), alloc('ukf', shU)
    for hh in range(2):
        qs = slice(8 * hh, 8 * hh + 8)
        o.tt('dve', argk, phi[:, qs].v(lambda a: bc(a, shU, 2)), prm["kvec"].v(lambda a: bc(a, shU, 1)), mult)
        sincos(argk, shU, f'U{hh}', Us[:, qs, :], Uc[:, qs, :], ukI, ukf)
    o.tt('dve', a128re, rho128, Uc[:, :, 128], mult)
    o.tt('dve', a128im, rho128, Us[:, :, 128], mult)
    if 'stopP2' in debug:
        return finish()
    S.fence()
    A.release(mP1)
    Eend = alloc('Eend', [128, 16, 2])
    prev = [alloc(f'prev{k}', [128, 16, 2]) for k in range(3)]
    Sin_ = alloc('Sin', [128, 16, 2])
    Sn = alloc('Sn', [128, 16, 2])
    c1, c2 = alloc('c1', [128, 16]), alloc('c2', [128, 16])
    Kblk = alloc('Kblk', [128, 4, 16, 128], BF16)
    mSeg = A.mark()
    st = alloc('st', [128, NT, 12])
    mv = alloc('mv', [128, NT, 2])
    sd = alloc('sd', [128, NT])
    rstd = alloc('rstd', [128, NT])
    nmr = alloc('nmr', [128, NT])
    hTg = alloc('hTg', [128, 8, 512], BF16)
    for k_ in range(3):
        dumpables[f'prev{k_}'] = (prev[k_], [128, 16, 2], F32)
    dumpables['Sin'] = (Sin_, [128, 16, 2], F32)
    dumpables['Eend'] = (Eend, [128, 16, 2], F32)
    m1, m2, m3, m4 = (T(t_.ap.rearrange("p m e q -> p (m e q)").rearrange("p (a c) -> p a c", a=4), t_.k)
                      for t_ in (t1, t2, t3, t4))

    def ln_stats_tile(i, src):
        for hh in range(2):
            o.bn_stats(st[:, i, hh * 6:(hh + 1) * 6].key(f'st{i}'), src[:, hh * 512:(hh + 1) * 512])
        o.bn_aggr(mv[:, i, :].key(f'mv{i}'), st[:, i, :].key(f'st{i}'))

    def ln_stats_group(g):
        gs = slice(4 * g, 4 * g + 4)
        mvg = mv[:, gs, :].key([f'mv{4 * g + j}' for j in range(4)])
        o.act(sd[:, gs].key(f'sd{g}'), mvg[:, :, 1], AF.Sqrt, bias=epsc[:, 0:1], scale=1.0)
        o.recip(rstd[:, gs].key(f'rstd{g}'), sd[:, gs].key(f'sd{g}'))
        o.stt(nmr[:, gs].key(f'nmr{g}'), mvg[:, :, 0], -1.0, rstd[:, gs].key(f'rstd{g}'), mult, mult)

    def transposes_to(dst_fn, src_tiles, gT_, bT_):
        for kt in range(8):
            bank = pb[kt % 2]
            for j in range(4):
                o.tr(bank[:, j * 128:(j + 1) * 128], src_tiles[j][:, kt * 128:(kt + 1) * 128], ident)
            dst = dst_fn(kt)
            if kt % 2 == 0:
                o.act(dst, bank, AF.Identity, bias=bT_[:, kt:kt + 1], scale=gT_[:, kt:kt + 1])
            else:
                o.ts('dve', dst, bank, gT_[:, kt:kt + 1], bT_[:, kt:kt + 1], mult, add)

    def s5_segment(xsrc, own, E_dst, uT_dst, ukey):
        S.fence()
        OV.reset()
        xin = [oalloc(f'xin{i}', [128, D]) for i in range(4)]
        xng = oalloc('xng', [128, 4, D])
        for g in range(4):
            for j in range(4):
                i = 4 * g + j
                o.dma('sp', xin[j], xsrc[i * 128:(i + 1) * 128, :], sem=f'xin{j}')
                ln_stats_tile(i, xin[j])
            ln_stats_group(g)
            tiles = []
            for j in range(4):
                i = 4 * g + j
                dst = xng[:, j, :].key(f'xng{j}')
                o.act(dst, xin[j], AF.Identity, bias=nmr[:, i:i + 1].key(f'nmr{g}'), scale=rstd[:, i:i + 1].key(f'rstd{g}'))
                tiles.append(dst)
            if own:
                transposes_to(lambda kt: hT[:, kt, g * 512:(g + 1) * 512].key(hTk(kt, g)), tiles, lng0, lnb0)
                src_fn = lambda kt: hT[:, kt, g * 512:(g + 1) * 512].key(hTk(kt, g))
            else:
                transposes_to(lambda kt: hTg[:, kt, :].key(f'hTg{kt}'), tiles, lng0, lnb0)
                src_fn = lambda kt: hTg[:, kt, :].key(f'hTg{kt}')
            for m in range(4):
                bank = pb[2 + m % 2]
                for kt in range(8):
                    o.mm(bank, win_u[:, kt, m * 128:(m + 1) * 128], src_fn(kt), start=(kt == 0), stop=(kt == 7))
                o.cp('act', uT_dst[:, m, :, 32 * g:32 * g + 32].key(f'{ukey}{m}'),
                     bank.v(lambda a: a.rearrange("p (c t) -> p t c", t=16)))
        S.fence()
        OV.reset()
        Xp_ = oalloc('Xp', [128, 16, 2, 128])
        Vb_ = oalloc('Vb', [128, 16, 2, 128])
        if 'drain' in debug:
            S.op('pe', lambda e: e.drain(), (), ())
        for q in range(16):
            m, qq = divmod(q, 4)
            bank = pb[4 + qq]
            col0 = (m % 2) * 256
            rows = slice(32 * qq, 32 * qq + 32)
            for ri in range(2):
                for kap in range(16):
                    wk = [f'W{kap}_{ri}'] + (['xc_serial'] if ('serial' in debug and ri == 0 and kap == 0) else [])
                    o.mm(bank[:, col0 + ri * 128: col0 + (ri + 1) * 128],
                         T(W.ap[rows, m, kap, ri, :], wk),
                         uT_dst[rows, m, kap, :].key(f'{ukey}{m}'),
                         start=(kap == 0), stop=(kap == 15), tile_position=(32 * qq, 0))
            Xre, Xim = bank[:, col0:col0 + 128], bank[:, col0 + 128:col0 + 256]
            Uc1, Us1 = Uc[:, q, 1:129], Us[:, q, 1:129]
            a1, a2, a3, a4 = m1[:, qq, :], m2[:, qq, :], m3[:, qq, :], m4[:, qq, :]
            o.tt('dve', a1, Xre, Uc1, mult)
            o.tt('dve', a2, Xim, Us1, mult)
            o.tt('dve', Xp_[:, q, 0, :].key(f'Xp{q}'), a1, a2, add)
            o.tt('dve', a3, Xim, Uc1, mult)
            o.tt('dve', a4, Xre, Us1, mult)
            o.tt('dve', Xp_[:, q, 1, :].key([f'Xp{q}'] + (['xc_serial'] if 'serial' in debug else [])), a3, a4, sub)
            for ri in range(2):
                o.scan(Vb_[:, q, ri, :].key(f'Vb{q}'), rho[:, q:q + 1].v(lambda a: a.to_broadcast([128, 128])),
                       Xp_[:, q, ri, :].key(f'Xp{q}'), 0.0)
        if 'drain' in debug:
            S.op('pe', lambda e: e.drain(), (), ())
        Vall = Vb_.key([f'Vb{q}' for q in range(16)])
        Vre, Vim = Vall[:, :, 0, 127], Vall[:, :, 1, 127]
        Uc128, Us128 = Uc[:, :, 128], Us[:, :, 128]
        o.tt('dve', c1, Uc128, Vre, mult)
        o.tt('dve', c2, Us128, Vim, mult)
        o.tt('dve', E_dst[:, :, 0], c1, c2, sub)
        o.tt('dve', c1, Us128, Vre, mult)
        o.tt('dve', c2, Uc128, Vim, mult)
        o.tt('dve', E_dst[:, :, 1], c1, c2, add)
        return Xp_, Vb_

    uTp = T(Kblk.ap.rearrange("p m j c -> p (m j c)").rearrange("p (m t c) -> p m t c", m=4, t=16), 'uTp')
    for k in range(3):
        Xp_k, Vb_k = s5_segment(xprev_d[k], False, prev[k], uTp, 'uTp')
        pk = prev[k].v(lambda a: a.rearrange("p q r -> p (q r)"))
        o.ts('dve', pk, pk, valid[:, k:k + 1], None, mult)
        if k == 0 and 'stopK0' in debug:
            dumpables['uTp'] = (uTp.key([f'uTp{m}' for m in range(4)]), [128, 4, 16, 128], BF16)
            dumpables['XpK'] = (Xp_k.key([f'Xp{q}' for q in range(16)]), [128, 16, 2, 128], F32)
            dumpables['VbK'] = (Vb_k.key([f'Vb{q}' for q in range(16)]), [128, 16, 2, 128], F32)
            return finish()
    Xp, Vb = s5_segment(x_d, True, Eend, uT, 'uT')
    if 'stopB' in debug:
        return finish()
    S.fence()
    A.release(mSeg)
    E0 = alloc('E0', [128, 4, 2, 128], BF16)
    BTb = alloc('BTb', [128, 4, 2, 128], BF16)
    Ef = [alloc(f'Ef{i}', [128, 2, 4, 128]) for i in range(2)]
    Sprev = alloc('Sprev', [128, 16, 2, 128], BF16)
    sgt = [alloc(f'sgt{i}', [128, 512], BF16) for i in range(2)]
    dumpables['Sprev'] = (Sprev.key(['Sprev'] + [f'Sprev{m}' for m in range(4)]), [128, 16, 2, 128], BF16)
    dumpables['Kblk'] = (Kblk.key([f'Kblk{m}' for m in range(4)]), [128, 4, 16, 128], BF16)
    cursrc = prev[2]
    for k in (1, 0):
        o.tt('dve', c1, a128re, cursrc[:, :, 0], mult)
        o.tt('dve', c2, a128im, cursrc[:, :, 1], mult)
        o.tt('dve', c1, c1, c2, sub)
        dst = Sin_ if k == 0 else Sn
        o.tt('dve', dst[:, :, 0], c1, prev[k][:, :, 0], add)
        o.tt('dve', c1, a128re, cursrc[:, :, 1], mult)
        o.tt('dve', c2, a128im, cursrc[:, :, 0], mult)
        o.tt('dve', c1, c1, c2, add)
        o.tt('dve', dst[:, :, 1], c1, prev[k][:, :, 1], add)
        cursrc = dst
    for q in range(16):
        for ri in range(2):
            o.scan(Vb[:, q, ri, :].key(f'Vb{q}'), rho[:, q:q + 1].v(lambda a: a.to_broadcast([128, 128])),
                   Xp[:, q, ri, :].key(f'Xp{q}'), Sin_[:, q, ri:ri + 1])
    o.cp('dve', Sprev[:, :, 0, 0], Sin_[:, :, 0])
    o.ts('dve', Sprev[:, :, 1, 0], Sin_[:, :, 1], -1.0, None, mult)
    for m in range(4):
        qs = slice(4 * m, 4 * m + 4)
        Vm = Vb[:, qs, :, :].key([f'Vb{q}' for q in range(4 * m, 4 * m + 4)])
        Vr, Vi = Vm[:, :, 0, 0:127], Vm[:, :, 1, 0:127]
        Uc0, Us0 = Uc[:, qs, 1:128], Us[:, qs, 1:128]
        a1, a2, a3, a4 = m1[:, :, 0:127], m2[:, :, 0:127], m3[:, :, 0:127], m4[:, :, 0:127]
        o.tt('dve', a1, Uc0, Vr, mult)
        o.tt('dve', a2, Us0, Vi, mult)
        o.tt('dve', Sprev[:, qs, 0, 1:128].key(f'Sprev{m}'), a1, a2, sub)
        o.tt('dve', a3, Us0, Vr, mult)
        o.tt('dve', a4, Uc0, Vi, mult)
        o.stt(Sprev[:, qs, 1, 1:128].key(f'Sprev{m}'), a3, -1.0, a4, mult, sub)
    if 'stopC' in debug:
        return finish()
    SprevK = lambda m: ['Sprev', f'Sprev{m}']
    flat4 = lambda t_: t_.v(lambda a: a.rearrange("p m e q -> p m (e q)"))
    for ri, src, sc in (((0, Bm_re, 1.0), (1, Bm_im, -1.0)) if 'skipBT' not in debug else ()):
        bank = pb[6 + ri]
        for m in range(4):
            o.tr(bank[:, m * 128:(m + 1) * 128], flat4(src)[:, m, :], ident)
        o.act(BTb[:, :, ri, :], bank.v(lambda a: a.rearrange("p (m c) -> p m c", m=4)), AF.Identity, scale=sc)

    PE2 = 'dve'
    o.memset(PE2, cur[0][0], 1.0)
    o.memset(PE2, cur[0][1], 0.0)
    for j in (range(17) if 'skipEloop' not in debug else ()):
        cr, ci = cur[j % 2]
        Efj = Ef[j % 2]
        Ef6 = Efj.v(lambda a: a.rearrange("p r m (e q) -> p r m e q", e=2))
        o.tt(PE2, t1, bce(cr), Cm_re, mult)
        o.tt(PE2, t2, bce(ci), Cm_im, mult)
        o.tt(PE2, Ef6[:, 0], t1, t2, sub)
        o.tt(PE2, t3, bce(ci), Cm_re, mult)
        o.tt(PE2, t4, bce(cr), Cm_im, mult)
        o.tt(PE2, Ef6[:, 1], t3, t4, add)
        for ri in (range(2) if 'skipEtr' not in debug else ()):
            bank = pb[6 + ri]
            for m in range(4):
                o.tr(bank[:, m * 128:(m + 1) * 128], Efj[:, ri, m, :], ident)
            if j == 0:
                dst = E0[:, :, ri, :]
            else:
                dst = T(W.ap[:, :, j - 1, ri, :], f'W{j - 1}_{ri}')
            src = bank.v(lambda a: a.rearrange("p (m c) -> p m c", m=4))
            if ri == 0:
                o.cp('act', dst, src)
            else:
                o.cp('dve', dst, src)
        if j < 16:
            cur_step(PE2, j)

    if 'stopD0' in debug:
        return finish()
    for m in range(4):
        for j in range(16):
            t = j % 4
            bank = pb[4 + (m * 4 + j // 4) % 2]
            o.mm(bank[:, t * 128:(t + 1) * 128], zrow[0:1, 0:128], zrow[0:1, 0:128], start=True, stop=False)
            for qq in range(4):
                cols = slice(32 * qq, 32 * qq + 32)
                for ri in range(2):
                    if j == 0:
                        Es = E0[:, m, ri, cols]
                    else:
                        Es = T(W.ap[:, m, j - 1, ri, cols], f'W{j - 1}_{ri}')
                    o.mm(bank[32 * qq:32 * qq + 32, t * 128 + 32 * qq: t * 128 + 32 * qq + 32],
                         BTb[:, m, ri, cols], Es, start=False, stop=(ri == 1), tile_position=(0, 32 * qq))
            if t == 3:
                o.cp('act', Kblk[:, m, j - 3:j + 1, :].key(f'Kblk{m}'),
                     bank.v(lambda a: a.rearrange("p (t c) -> p t c", t=4)))
                if j == 3:
                    o.stt(Kblk[:, m, 0, :].key(f'Kblk{m}'), ident, prm["s5_d"][:, m:m + 1], bank[:, 0:128], mult, add)

    if 'stopD' in debug:
        return finish()

    def gT(m):
        return T(uT.ap[:, m].rearrange("p t c -> p (t c)"), f'uT{m}')

    for m in range(4):
        for bk in range(4):
            bank = pb[bk]
            first = True
            for j in range(0, 4 * bk + 4):
                tlo, thi = max(j, 4 * bk), 4 * bk + 3
                rhs = T(uT.ap[:, m, tlo - j:thi - j + 1, :].rearrange("p t c -> p (t c)"), f'uT{m}')
                o.mm(bank[:, (tlo - 4 * bk) * 128:512], Kblk[:, m, j, :].key(f'Kblk{m}'), rhs, start=first, stop=False)
                first = False
            for tau in range(4 * bk, 4 * bk + 4):
                for qq in range(4):
                    cols = slice(32 * qq, 32 * qq + 32)
                    for ri in range(2):
                        last = (tau == 4 * bk + 3 and ri == 1)
                        o.mm(bank[32 * qq:32 * qq + 32, (tau - 4 * bk) * 128:(tau - 4 * bk + 1) * 128],
                             T(W.ap[:, m, tau, ri, cols], f'W{tau}_{ri}'),
                             Sprev[:, 4 * m + qq, ri, :].key(SprevK(m)), start=False, stop=last,
                             tile_position=(0, 32 * qq))
        for bk in range(4):
            dst = gT(m).v(lambda a: a.rearrange("p (c t) -> p t c", t=16))[:, 4 * bk:4 * bk + 4, :]
            o.act(dst, pb[bk].v(lambda a: a.rearrange("p (t c) -> p t c", t=4)), AF.Gelu_apprx_tanh)

    if 'stopE' in debug:
        return finish()
    S.fence()
    for mo in range(4):
        for nb in range(4):
            idx = mo * 4 + nb
            bank = pb[4 + idx % 2]
            ns = slice(nb * 512, (nb + 1) * 512)
            for m in range(4):
                o.mm(bank, wglu[:, m, mo * 128:(mo + 1) * 128], gT(m)[:, ns], start=(m == 0), stop=(m == 3))
            sg = sgt[idx % 2]
            o.act(sg, bank, AF.Sigmoid, bias=bglu[:, mo:mo + 1])
            o.tt('dve', ymix[:, mo, ns].key(f'ymix{mo}_{nb}'), gT(mo)[:, ns], sg, mult)

    if 'nogla' in debug:
        return finish()
    S.fence()
    A.release(mMix)
    gd = {n: din(n, sh) for n, sh in [("gla_wgu", [16, 256]), ("gla_bg", [128, 2]), ("gla_ng", [128, 512]),
                                      ("tri", [128, 128]), ("rmask", [128, 512])]}
    valid = alloc('valid2', [128, 3])
    o.dma('sp', valid, valid_d, sem='c')
    st = alloc('st2', [128, NT, 12])
    mv = alloc('mv2', [128, NT, 2])
    sd = alloc('sd2', [128, NT])
    rstd = alloc('rstd2', [128, NT])
    nmr = alloc('nmr2', [128, NT])
    hTg = alloc('hTg2', [128, 8, 512], BF16)
    wk_ = alloc('wk', [128, 8, 256], BF16)
    wq_ = alloc('wq', [128, 8, 256], BF16)
    wv_ = alloc('wv', [128, 8, 512], BF16)
    wr_ = alloc('wr', [128, 8, 512], BF16)
    wg_ = alloc('wg', [128, 8, 16], BF16)
    o.dma('pool', wq_, w_in_d[:, :, 512:768], sem='wq')
    o.dma('pool', wk_, w_in_d[:, :, 768:1024], sem='wk')
    o.dma('pool', wv_, w_in_d[:, :, 1024:1536], sem='wv')
    o.dma('pool', wr_, w_in_d[:, :, 1536:2048], sem='wr')
    o.dma('pool', wg_, w_in_d[:, :, 2048:2064], sem='wg')
    wgu = alloc('wgu', [16, 256], BF16)
    o.dma('pool', wgu, gd['gla_wgu'], sem='wgu')
    bg = alloc('bg', [128, 2])
    nbg = alloc('nbg', [128, 2])
    gng = alloc('gng', [128, 512])
    tri = alloc('tri', [128, 128])
    rmask = alloc('rmask', [128, 512])
    onec = alloc('onec', [128, 1])
    o.dma('sp', bg, gd['gla_bg'], sem='c')
    o.dma('sp', gng, gd['gla_ng'], sem='c')
    o.dma('sp', tri, gd['tri'], sem='c')
    o.dma('sp', rmask, gd['rmask'], sem='c')
    o.ts('dve', nbg, bg, -1.0, None, mult)
    o.memset('dve', onec, 1.0)
    xin2 = [alloc(f'gxin{i}', [128, D]) for i in range(4)]
    xng2 = alloc('gxng', [128, 4, D])
    glr = alloc('glr', [16, 512], BF16)
    spl = alloc('spl', [128, 2, 512])
    cum = alloc('cum', [128, 2, 512])
    ekl = alloc('ekl', [128, 2, 512])
    eb = alloc('eb', [128, 2, 512])
    enb = alloc('enb', [128, 2, 512])
    klT = alloc('klT', [128, 2, 512])
    klt = alloc('klt', [128, 4, 256], BF16)
    vt = alloc('vt', [128, 4, 512], BF16)
    qeT = alloc('qeT', [128, 2, 512], BF16)
    keT = alloc('keT', [128, 2, 512], BF16)
    ncl = alloc('ncl', [128, 2, 4])
    dec = alloc('dec', [128, 2, 4])
    Sg_ = alloc('Sgla', [128, 2, 128])
    Sbf = alloc('Sbf', [128, 2, 128], BF16)
    scT = alloc('scT', [128, 4, 128], BF16)
    rsil = alloc('rsil', [128, 512])
    gnrs = alloc('gnrs', [128, 512])
    ysb = alloc('ysb', [128, 512])
    ssq = alloc('ssq', [128, 4])
    rs4 = alloc('rs4', [128, 4])
    junk = alloc('junk', [128, 128])
    dumpables['ygla'] = (T(ymix.ap[:, 4:8, :], [f'ymixg{c}' for c in range(NT)]), [128, 4, TOK], BF16)
    dumpables['Sgla'] = (Sg_, [128, 2, 128], F32)

    def gla_segment(xsrc, own):
        for g in range(4):
            if own:
                src_fn = lambda kt: hT[:, kt, g * 512:(g + 1) * 512].key(hTk(kt, g))
            else:
                for j in range(4):
                    i = 4 * g + j
                    o.dma('sp', xin2[j], xsrc[i * 128:(i + 1) * 128, :], sem=f'gxin{j}')
                    ln_stats_tile(i, xin2[j])
                ln_stats_group(g)
                tiles = []
                for j in range(4):
                    i = 4 * g + j
                    dst = xng2[:, j, :].key(f'gxng{j}')
                    o.act(dst, xin2[j], AF.Identity, bias=nmr[:, i:i + 1].key(f'nmr{g}'), scale=rstd[:, i:i + 1].key(f'rstd{g}'))
                    tiles.append(dst)
                transposes_to(lambda kt: hTg[:, kt, :].key(f'hTg{kt}'), tiles, lng0, lnb0)
                src_fn = lambda kt: hTg[:, kt, :].key(f'hTg{kt}')
            bank = pb[2]
            for kt in range(8):
                o.mm(bank[0:16, :], wg_[:, kt, :], src_fn(kt), start=(kt == 0), stop=(kt == 7))
            o.cp('act', glr, bank[0:16, :])
            for t in range(2):
                bk = pb[3]
                o.mm(bk, wgu[:, t * 128:(t + 1) * 128], glr, start=True, stop=True)
                o.act(spl[:, t, :], bk, AF.Exp, bias=nbg[:, t:t + 1], scale=-1.0)
                o.act(spl[:, t, :], spl[:, t, :], AF.Ln, bias=onec[:, 0:1], scale=1.0)
                o.scan(cum[:, t, :], rmask, spl[:, t, :], 0.0)
                cl = cum[:, t, :].v(lambda a: a.rearrange("p (c k) -> p c k", k=128))[:, :, 127]
                o.ts('dve', ncl[:, t, :], cl, -1.0 / 16.0, None, mult)
                o.act(dec[:, t, :], ncl[:, t, :], AF.Exp)
                for cc in range(4):
                    cs = slice(cc * 128, (cc + 1) * 128)
                    o.act(ekl[:, t, cs], cum[:, t, cs], AF.Exp, bias=ncl[:, t, cc:cc + 1], scale=1.0 / 16.0)
                if own:
                    o.act(eb[:, t, :], cum[:, t, :], AF.Exp, scale=-1.0 / 16.0)
                    o.act(enb[:, t, :], cum[:, t, :], AF.Exp, scale=1.0 / 16.0)
            for t in range(2):
                bk = pb[4 + t]
                for kt in range(8):
                    o.mm(bk, wk_[:, kt, t * 128:(t + 1) * 128], src_fn(kt), start=(kt == 0), stop=(kt == 7))
                o.tt('dve', klT[:, t, :], bk, ekl[:, t, :], mult)
                if own:
                    o.tt('dve', keT[:, t, :], bk, enb[:, t, :], mult)
                    bq = pb[6 + t]
                    for kt in range(8):
                        o.mm(bq, wq_[:, kt, t * 128:(t + 1) * 128], src_fn(kt), start=(kt == 0), stop=(kt == 7))
                    o.stt(qeT[:, t, :], bq, 0.125, eb[:, t, :], mult, mult)
            for j in range(4):
                c = 4 * g + j
                tsl = slice(j * 128, (j + 1) * 128)
                bv = pb[0]
                for kt in range(8):
                    o.mm(bv, src_fn(kt)[:, tsl], wv_[:, kt, :], start=(kt == 0), stop=(kt == 7))
                o.cp('act', vt[:, j, :], bv)
                bt = pb[1]
                for t in range(2):
                    o.tr(bt[:, t * 128:(t + 1) * 128], klT[:, t, tsl], ident)
                o.cp('dve', klt[:, j, :], bt[:, 0:256])
                if own:
                    bsb = (pb[2], pb[7])
                    for h in range(4):
                        t, r0 = divmod(h, 2)
                        rows = slice(64 * r0, 64 * r0 + 64)
                        o.mm(bsb[r0][:, t * 128:(t + 1) * 128], keT[rows, t, tsl], qeT[rows, t, tsl], start=True, stop=True)
                    for r0 in range(2):
                        o.tt('dve', T(scT.ap.rearrange("p (t r) i -> p r t i", r=2)[:, r0], scT.k),
                             bsb[r0][:, 0:256].v(lambda a: a.rearrange("p (t i) -> p t i", t=2)),
                             tri.v(lambda a: bc(a, [128, 2, 128], 1)), mult)
                    bob = (pb[3], pb[5])
                    for h in range(4):
                        t, r0 = divmod(h, 2)
                        rows = slice(64 * r0, 64 * r0 + 64)
                        hs = slice(h * 128, (h + 1) * 128)
                        ob = bob[r0][:, t * 128:(t + 1) * 128]
                        o.mm(ob, scT[:, h, :], vt[:, j, hs], start=True, stop=False)
                        o.mm(ob, qeT[rows, t, tsl], Sbf[rows, t, :], start=False, stop=True)
                    br = pb[4]
                    for kt in range(8):
                        o.mm(br, src_fn(kt)[:, tsl], wr_[:, kt, :], start=(kt == 0), stop=(kt == 7))
                    o.act(rsil, br, AF.Silu)
                    o.tt('dve', gnrs, rsil, gng, mult)
                    for h in range(4):
                        t, r0 = divmod(h, 2)
                        o.act(junk, bob[r0][:, t * 128:(t + 1) * 128], AF.Square, accum_out=ssq[:, h:h + 1])
                    o.ts('dve', rs4, ssq, 1.0 / 128.0, LN_EPS, mult, add)
                    o.act(rs4, rs4, AF.Sqrt)
                    o.recip(rs4, rs4)
                    for h in range(4):
                        t, r0 = divmod(h, 2)
                        hs = slice(h * 128, (h + 1) * 128)
                        o.stt(ysb[:, hs], bob[r0][:, t * 128:(t + 1) * 128], rs4[:, h:h + 1], gnrs[:, hs], mult, mult)
                    bt2 = pb[1]
                    for h in range(4):
                        o.tr(bt2[:, h * 128:(h + 1) * 128], ysb[:, h * 128:(h + 1) * 128], ident)
                    o.cp('act', T(ymix.ap[:, 4:8, c * 128:(c + 1) * 128], f'ymixg{c}'),
                         bt2.v(lambda a: a.rearrange("p (h k) -> p h k", h=4)))
                bu_ = pb[6]
                for h in range(4):
                    t, r0 = divmod(h, 2)
                    o.mm(bu_[64 * r0:64 * r0 + 64, t * 128:(t + 1) * 128], klt[:, j, h * 64:(h + 1) * 64],
                         vt[:, j, h * 128:(h + 1) * 128], start=True, stop=True)
                for t in range(2):
                    o.stt(Sg_[:, t, :], Sg_[:, t, :], dec[:, t, j:j + 1], bu_[:, t * 128:(t + 1) * 128], mult, add)
                o.cp('act', Sbf, Sg_)

    o.memset('dve', Sg_, 0.0)
    o.memset('dve', Sbf, 0.0)
    Sflat = Sg_.v(lambda a: a.rearrange("p t v -> p (t v)"))
    for k in (2, 1, 0):
        gla_segment(xprev_d[k], False)
        o.ts('dve', Sflat, Sflat, valid[:, k:k + 1], None, mult)
        o.cp('act', Sbf, Sg_)
    if 'stopG0' in debug:
        return finish()
    gla_segment(x_d, True)
    if 'stopMix' in debug:
        return finish()
    S.fence()
    A.release(mMix)
    bd = {n: din(n, sh) for n, sh in [
        ("g0rep", [128, D]), ("b0rep", [128, D]), ("g1rep", [128, D]), ("b1rep", [128, D]),
        ("g2rep", [128, D]), ("b2rep", [128, D]), ("g3rep", [128, D]), ("b3rep", [128, D]),
        ("g1col", [128, 8]), ("b1col", [128, 8]), ("g2col", [128, 8]), ("b2col", [128, 8]),
        ("gmcol", [128, 8]), ("bmcol", [128, 8]), ("ones", [128, 128]), ("mem", [256, D]),
        ("w_out", [128, 8, 1024]), ("w_mq", [128, 8, 1024]), ("w_mo", [128, 8, 1024]), ("w_mkv", [128, 8, 2048]),
        ("w_ff1", [128, 8, 4096]), ("w_ff2", [128, 32, 1024])]}
    xn = alloc('xn', [128, NT, D])
    st = alloc('st3', [128, NT, 12])
    mv = alloc('mv3', [128, NT, 2])
    sd = alloc('sd3', [128, NT])
    rstd = alloc('rstd3', [128, NT])
    nmr = alloc('nmr3', [128, NT])
    grep_, brep_ = alloc('grep', [128, D]), alloc('brep', [128, D])
    gcol, bcol = alloc('gcol', [128, 8]), alloc('bcol', [128, 8])
    tb = [alloc(f'tb{i}', [128, D]) for i in range(4)]
    wA = alloc('wA', [128, 8, 1024], BF16)
    mv0, sc0 = alloc('mv0', [128, 2]), alloc('sc0', [128, 4])
    st0 = alloc('st0', [128, 12])
    dumpables['xn'] = (xn.key([f'xn{i}' for i in range(NT)]), [128, NT, D], F32)

    def ln_tile_inplace(buf):
        for hh in range(2):
            o.bn_stats(st0[:, hh * 6:(hh + 1) * 6], buf[:, hh * 512:(hh + 1) * 512])
        o.bn_aggr(mv0, st0)
        o.act(sc0[:, 0:1], mv0[:, 1:2], AF.Sqrt, bias=epsc[:, 0:1], scale=1.0)
        o.recip(sc0[:, 1:2], sc0[:, 0:1])
        o.stt(sc0[:, 2:3], mv0[:, 0:1], -1.0, sc0[:, 1:2], mult, mult)
        o.act(buf, buf, AF.Identity, bias=sc0[:, 2:3], scale=sc0[:, 1:2])

    def res_from_x(i, dst):
        o.dma('sp', dst, x_d[i * 128:(i + 1) * 128, :], sem=f'oxin{i % 4}')
        ln_tile_inplace(dst)
        o.tt('dve', dst, dst, grep_, mult)
        o.tt('dve', dst, dst, brep_, add)

    def res_from_xn(i, dst):
        o.tt('dve', dst, xn[:, i, :].key(f'xn{i}'), grep_, mult)
        o.tt('dve', dst, dst, brep_, add)

    def post_ln_group(g, write_hT):
        ln_stats_group(g)
        tiles = []
        for j in range(4):
            i = 4 * g + j
            dst = xn[:, i, :].key(f'xn{i}')
            o.act(dst, tb[j], AF.Identity, bias=nmr[:, i:i + 1].key(f'nmr{g}'), scale=rstd[:, i:i + 1].key(f'rstd{g}'))
            tiles.append(dst)
        if write_hT:
            transposes_to(lambda kt: hT[:, kt, g * 512:(g + 1) * 512].key(hTk(kt, g)), tiles, gcol, bcol)

    def dense_res_ln(srcT_fn, nk, w_fn, res_fn, write_hT=True):
        for g in range(4):
            for j in range(4):
                i = 4 * g + j
                banks = (pb[2], pb[3])
                for half in range(2):
                    for kt in range(nk):
                        o.mm(banks[half], srcT_fn(kt, i), w_fn(kt, half), start=(kt == 0), stop=(kt == nk - 1))
                res_fn(i, tb[j])
                for half in range(2):
                    hs = slice(half * 512, (half + 1) * 512)
                    o.stt(tb[j][:, hs], tb[j][:, hs], DN_ALPHA, banks[half], mult, add)
                ln_stats_tile(i, tb[j])
            post_ln_group(g, write_hT)

    o.dma('pool', wA, bd['w_out'], sem='wA')
    o.dma('sp', grep_, bd['g0rep'], sem='c')
    o.dma('sp', brep_, bd['b0rep'], sem='c')
    o.dma('sp', gcol, bd['g1col'], sem='c')
    o.dma('sp', bcol, bd['b1col'], sem='c')

    def ymixT(kt, i):
        key = f'ymix{kt}_{i // 4}' if kt < 4 else f'ymixg{i}'
        return T(ymix.ap[:, kt, i * 128:(i + 1) * 128], key)

    dense_res_ln(ymixT, 8, lambda kt, half: wA[:, kt, half * 512:(half + 1) * 512], res_from_x)
    if 'stopO' in debug:
        return finish()
    S.fence()
    mX = A.mark()
    kT = alloc('kT', [128, 8, 256], BF16)
    vtk = alloc('vtk', [128, 2, 1024], BF16)
    wkvq = alloc('wkvq', [128, 8, 512], BF16)
    qTh = alloc('qTh', [128, 2, TOK], BF16)
    pT1 = alloc('pT1', [128, 2, 512], BF16)
    rrec = alloc('rrec', [128, 512])
    onesb = alloc('onesb', [128, 128], BF16)
    gmc, bmc = alloc('gmc', [128, 8]), alloc('bmc', [128, 8])
    o.dma('pool', onesb, bd['ones'], sem='onesb')
    o.dma('sp', gmc, bd['gmcol'], sem='c')
    o.dma('sp', bmc, bd['bmcol'], sem='c')
    oT = T(ymix.ap, 'oT')
    mS_ = A.mark()
    memT = alloc('memT', [128, 8, 256], BF16)
    for mt in range(2):
        o.dma('sp', tb[mt], bd['mem'][mt * 128:(mt + 1) * 128, :], sem=f'memin{mt}')
        ln_tile_inplace(tb[mt])
    for kt in range(8):
        bank = pb[kt % 2]
        for mt in range(2):
            o.tr(bank[:, mt * 128:(mt + 1) * 128], tb[mt][:, kt * 128:(kt + 1) * 128], ident)
        o.act(memT[:, kt, :], bank[:, 0:256], AF.Identity, bias=bmc[:, kt:kt + 1], scale=gmc[:, kt:kt + 1])
    for qd in range(4):
        o.dma('pool', wkvq, bd['w_mkv'][:, :, qd * 512:(qd + 1) * 512], sem='wkvq')
        if qd < 2:
            for c4 in range(4):
                bank = pb[4 + c4 % 2]
                for kt in range(8):
                    o.mm(bank[:, 0:256], wkvq[:, kt, c4 * 128:(c4 + 1) * 128], memT[:, kt, :], start=(kt == 0), stop=(kt == 7))
                o.cp('act', kT[:, qd * 4 + c4, :], bank[:, 0:256])
        else:
            for mt in range(2):
                bank = pb[6 + mt]
                for kt in range(8):
                    o.mm(bank, memT[:, kt, mt * 128:(mt + 1) * 128], wkvq[:, kt, :], start=(kt == 0), stop=(kt == 7))
                o.cp('act', vtk[:, mt, (qd - 2) * 512:(qd - 1) * 512], bank)
    S.fence()
    A.release(mS_)
    o.dma('pool', wA, bd['w_mq'], sem='wA')
    for h in range(4):
        for c2 in range(2):
            c = 2 * h + c2
            for g in range(4):
                bank = pb[4 + g % 2]
                for kt in range(8):
                    o.mm(bank, wA[:, kt, c * 128:(c + 1) * 128], hT[:, kt, g * 512:(g + 1) * 512].key(hTk(kt, g)),
                         start=(kt == 0), stop=(kt == 7))
                o.act(qTh[:, c2, g * 512:(g + 1) * 512], bank, AF.Identity, scale=1.0 / 16.0)
        for g in range(4):
            gs = slice(g * 512, (g + 1) * 512)
            for mt in range(2):
                bank = pb[6 + mt]
                for c2 in range(2):
                    o.mm(bank, kT[:, 2 * h + c2, mt * 128:(mt + 1) * 128], qTh[:, c2, gs], start=(c2 == 0), stop=(c2 == 1))
                o.act(pT1[:, mt, :], bank, AF.Exp)
            bank = pb[0]
            for mt in range(2):
                o.mm(bank, onesb, pT1[:, mt, :], start=(mt == 0), stop=(mt == 1))
            o.recip(rrec, bank)
            for c2 in range(2):
                c = 2 * h + c2
                bank = pb[1 + c2]
                for mt in range(2):
                    o.mm(bank, vtk[:, mt, c * 128:(c + 1) * 128], pT1[:, mt, :], start=(mt == 0), stop=(mt == 1))
                o.tt('dve', T(oT.ap[:, c, gs], f'oT{c}_{g}'), bank, rrec, mult)
    o.dma('pool', wA, bd['w_mo'], sem='wA')
    o.dma('sp', grep_, bd['g1rep'], sem='c')
    o.dma('sp', brep_, bd['b1rep'], sem='c')
    o.dma('sp', gcol, bd['g2col'], sem='c')
    o.dma('sp', bcol, bd['b2col'], sem='c')
    dense_res_ln(lambda kt, i: T(oT.ap[:, kt, i * 128:(i + 1) * 128], f'oT{kt}_{i // 4}'), 8,
                 lambda kt, half: wA[:, kt, half * 512:(half + 1) * 512], res_from_xn)
    if 'stopX' in debug:
        return finish()

    S.fence()
    A.release(mX)
    hid = T(ymix.ap, 'hid')
    w1q = alloc('w1q', [128, 8, 1024], BF16)
    w2q = alloc('w2q', [128, 8, 1024], BF16)
    rl = [T(tb[k_].ap[:, 0:512], tb[k_].k) for k_ in range(4)]
    o.dma('sp', grep_, bd['g2rep'], sem='c')
    o.dma('sp', brep_, bd['b2rep'], sem='c')
    for qt in range(4):
        o.dma('pool', w1q, bd['w_ff1'][:, :, qt * 1024:(qt + 1) * 1024], sem='w1q')
        o.dma('pool', w2q, bd['w_ff2'][:, qt * 8:(qt + 1) * 8, :], sem='w2q')
        for ft in range(8):
            for g in range(4):
                idx = ft * 4 + g
                bank = pb[4 + idx % 2]
                for kt in range(8):
                    o.mm(bank, w1q[:, kt, ft * 128:(ft + 1) * 128], hT[:, kt, g * 512:(g + 1) * 512].key(hTk(kt, g)),
                         start=(kt == 0), stop=(kt == 7))
                r_ = rl[idx % 4]
                o.act(r_, bank, AF.Relu)
                o.tt('dve', T(hid.ap[:, ft, g * 512:(g + 1) * 512], f'hid{ft}_{g}'), r_, r_, mult)
        if qt == 1:
            o.dma('sp', grep_, bd['g3rep'], sem='c')
            o.dma('sp', brep_, bd['b3rep'], sem='c')
        for g in range(4):
            for j in range(4):
                i = 4 * g + j
                banks = (pb[2], pb[3])
                for half in range(2):
                    for ft in range(8):
                        o.mm(banks[half], T(hid.ap[:, ft, i * 128:(i + 1) * 128], f'hid{ft}_{g}'),
                             w2q[:, ft, half * 512:(half + 1) * 512], start=(ft == 0), stop=(ft == 7))
                xi = xn[:, i, :].key(f'xn{i}')
                if qt == 0:
                    res_from_xn(i, xi)
                for half in range(2):
                    hs = slice(half * 512, (half + 1) * 512)
                    if qt == 0:
                        o.stt(xi[:, hs], xi[:, hs], DN_ALPHA, banks[half], mult, add)
                    elif qt < 3:
                        o.tt('dve', xi[:, hs], xi[:, hs], banks[half], add)
                    else:
                        o.tt('dve', tb[j][:, hs], xi[:, hs], banks[half], add)
                if qt == 3:
                    ln_stats_tile(i, tb[j])
            if qt == 3:
                ln_stats_group(g)
                for j in range(4):
                    i = 4 * g + j
                    o.act(tb[j], tb[j], AF.Identity, bias=nmr[:, i:i + 1].key(f'nmr{g}'), scale=rstd[:, i:i + 1].key(f'rstd{g}'))
                    o.tt('dve', tb[j], tb[j], grep_, mult)
                    o.tt('dve', tb[j], tb[j], brep_, add)
                    o.dma('sp', T(out_d.ap[i * 128:(i + 1) * 128, :], f'out{i}'), tb[j], sem=f'outs{j}')
    return finish()


def host_inputs(inp):
    f32 = np.float32
    x = np.ascontiguousarray(inp['x'], dtype=f32)

    def cols(v):
        v = np.asarray(v, f32).reshape(-1)
        return np.ascontiguousarray(v.reshape(-1, 128).T)

    def ktile(wm):
        K, N = wm.shape
        return np.ascontiguousarray(np.asarray(wm, f32).reshape(K // 128, 128, N).transpose(1, 0, 2))

    def rep(v):
        return np.ascontiguousarray(np.broadcast_to(np.asarray(v, f32).reshape(1, -1), (128, D)))

    def gh_layout(a_gp):
        a = np.asarray(a_gp, f32).reshape(4, 8, 64)
        a = np.broadcast_to(a.transpose(1, 0, 2)[:, None, :, :], (8, 16, 4, 64))
        return np.ascontiguousarray(a.reshape(128, 4, 64))

    def pair_layout(a_gp):
        a = np.asarray(a_gp, f32).reshape(16, 2, 64)
        return np.ascontiguousarray(a.transpose(1, 2, 0).reshape(128, 16))

    lst = np.asarray(inp['ssm_log_step'][0], f32)
    b_re = np.asarray(inp['ssm_b_re'][0], f32).reshape(4, 8, 64, 16)
    b_im = np.asarray(inp['ssm_b_im'][0], f32).reshape(4, 8, 64, 16)
    c_re = np.asarray(inp['ssm_c_re'][0], f32).reshape(4, 8, 16, 64)
    c_im = np.asarray(inp['ssm_c_im'][0], f32).reshape(4, 8, 16, 64)
    maske = np.zeros((8, 16, 2), f32)
    for gp in range(8):
        maske[gp, :, gp % 2] = 1.0
    common = {
        'ident': np.eye(128, dtype=f32),
        'lng0': cols(inp['emb_ln_g']),
        'lnb0': cols(inp['emb_ln_b']),
        'w_in': ktile(inp['w_in'][0]),
        's5_lam_re': gh_layout(inp['ssm_lam_re'][0]),
        's5_lam_im': gh_layout(inp['ssm_lam_im'][0]),
        's5_lst': np.ascontiguousarray(np.broadcast_to(lst.reshape(4, 8).T[:, None, :], (8, 16, 4)).reshape(128, 4)),
        's5_bre': np.ascontiguousarray(b_re.transpose(1, 3, 0, 2).reshape(128, 4, 64)),
        's5_bim': np.ascontiguousarray(b_im.transpose(1, 3, 0, 2).reshape(128, 4, 64)),
        's5_cre': np.ascontiguousarray(c_re.transpose(1, 2, 0, 3).reshape(128, 4, 64)),
        's5_cim': np.ascontiguousarray(c_im.transpose(1, 2, 0, 3).reshape(128, 4, 64)),
        's5_d': np.ascontiguousarray(np.asarray(inp['ssm_d'][0], f32).reshape(4, 8, 16).transpose(1, 2, 0).reshape(128, 4)),
        's5_maske': np.ascontiguousarray(maske.reshape(128, 2)),
        's5p_lam_re': pair_layout(inp['ssm_lam_re'][0]),
        's5p_lam_im': pair_layout(inp['ssm_lam_im'][0]),
        's5p_lst': np.ascontiguousarray(np.broadcast_to(lst.reshape(16, 2)[:, :, None], (16, 2, 64)).transpose(1, 2, 0).reshape(128, 16)),
        'kvec': np.ascontiguousarray(np.broadcast_to(np.arange(129, dtype=f32)[None, :], (128, 129))),
        'w_glu': ktile(inp['w_glu'][0]),
        'gla_wgu': np.ascontiguousarray(np.asarray(inp['w_gate_up'][0], f32)),
        'gla_bg': cols(inp['b_gate'][0]),
        'gla_ng': np.ascontiguousarray(np.broadcast_to(np.asarray(inp['gla_norm_g'][0], f32)[None, :], (128, 512))),
        'tri': np.triu(np.ones((128, 128), f32)),
        'ones': np.ones((128, 128), f32),
        'g0rep': rep(inp['emb_ln_g']), 'b0rep': rep(inp['emb_ln_b']),
        'g1rep': rep(inp['ln1_g'][0]), 'b1rep': rep(inp['ln1_b'][0]),
        'g2rep': rep(inp['ln2_g'][0]), 'b2rep': rep(inp['ln2_b'][0]),
        'g3rep': rep(inp['ln3_g'][0]), 'b3rep': rep(inp['ln3_b'][0]),
        'g1col': cols(inp['ln1_g'][0]), 'b1col': cols(inp['ln1_b'][0]),
        'g2col': cols(inp['ln2_g'][0]), 'b2col': cols(inp['ln2_b'][0]),
        'gmcol': cols(inp['mem_ln_g']), 'bmcol': cols(inp['mem_ln_b']),
        'w_out': ktile(inp['w_out'][0]), 'w_mq': ktile(inp['w_mq'][0]), 'w_mo': ktile(inp['w_mo'][0]),
        'w_mkv': ktile(inp['w_mkv'][0]), 'w_ff1': ktile(inp['w_ff1'][0]), 'w_ff2': ktile(inp['w_ff2'][0]),
        'rmask': np.ascontiguousarray(np.broadcast_to((np.arange(512) % 128 != 0).astype(f32)[None, :], (128, 512))),
        'b_glu': cols(inp['b_glu'][0]),
    }
    maps = []
    for c in range(NCORES):
        bb, j = divmod(c, 4)
        m = dict(common)
        m['x'] = np.ascontiguousarray(x[bb, j * TOK:(j + 1) * TOK, :])
        m['mem'] = np.ascontiguousarray(np.asarray(inp['mem'], f32)[bb])
        val = np.zeros((128, 3), f32)
        for k in range(3):
            src = j - 1 - k
            if src >= 0:
                m[f'xprev{k}'] = np.array(x[bb, src * TOK:(src + 1) * TOK, :], dtype=f32, order='C', copy=True)
                val[:, k] = 1.0
            else:
                m[f'xprev{k}'] = np.array(x[bb, j * TOK:(j + 1) * TOK, :], dtype=f32, order='C', copy=True)
        m['valid'] = val
        maps.append(m)
    return maps


_CACHE = {}


def kernel(**inputs):
    if 'nc' not in _CACHE:
        _CACHE['nc'] = build_program()[0]
    nc = _CACHE['nc']
    maps = host_inputs(inputs)
    res = run_bass_kernel_spmd(nc, maps, core_ids=list(range(NCORES)))
    out = np.empty((2, 8192, D), np.float32)
    for c in range(NCORES):
        bb, j = divmod(c, 4)
        out[bb, j * TOK:(j + 1) * TOK, :] = res.results[c]['out']
    return out
```
